# Optimizing a Trainium2 kernel written in Bass

```python
import math
import jax, jax.numpy as jnp
from jax import lax
import numpy as np

D_MODEL = 1024
BATCH = 8
SEQ = 2048
DEPTH = 2
DEC_BATCH = 128
DEC_SEQ = 8
PAST_LEN = 16384
PAGE_SIZE = 128

N_MIXERS = 4
G = D_MODEL // N_MIXERS
N_BLOCKS = 10
D_IN = N_BLOCKS * G
SC_WIDTH = 3
POOL_WINDOWS = (2, 4, 8, 16)
N_POOL = len(POOL_WINDOWS)
POOL_CH = G // N_POOL
POOL_BUF = max(POOL_WINDOWS) - 1
HGRN_HEADS = 4
HGRN_DK = G // HGRN_HEADS
HGRN_DV = G // HGRN_HEADS
HGRN_CHUNK = 64
CONF_WIDTH = 31
D_FF = -(-8 * D_MODEL // (3 * 256)) * 256
EPS = 1e-6
F_MIN = 1e-20

kernel_name = 'hybrid_shortconv_pool_hgrn2_conformer_step'


def rmsnorm(x, g):
    xf = x.astype(jnp.float32)
    y = xf * lax.rsqrt(jnp.mean(xf * xf, axis=-1, keepdims=True) + EPS)
    return (y * g.astype(jnp.float32)).astype(x.dtype)


def layernorm(x, g, b):
    xf = x.astype(jnp.float32)
    mu = jnp.mean(xf, axis=-1, keepdims=True)
    xc = xf - mu
    y = xc * lax.rsqrt(jnp.mean(xc * xc, axis=-1, keepdims=True) + EPS)
    return (y * g.astype(jnp.float32) + b.astype(jnp.float32)).astype(x.dtype)


def causal_dwconv(u, buf, w):
    width = w.shape[0]
    xx = jnp.concatenate([buf.astype(u.dtype), u], axis=1)
    y = lax.conv_general_dilated(xx, w.astype(u.dtype)[:, None, :], window_strides=(1,), padding='VALID',
                                 dimension_numbers=('NWC', 'WIO', 'NWC'), feature_group_count=u.shape[-1])
    return y, xx[:, xx.shape[1] - (width - 1):]


def multiscale_pool(p, buf, start_pos):
    n, t, _ = p.shape
    xx = jnp.concatenate([buf.astype(p.dtype), p], axis=1)
    cs = jnp.cumsum(xx.astype(jnp.float32), axis=1)
    cs = jnp.pad(cs, ((0, 0), (1, 0), (0, 0)))
    pos = start_pos + jnp.arange(t)
    outs = []
    for gi, w in enumerate(POOL_WINDOWS):
        sl = slice(gi * POOL_CH, (gi + 1) * POOL_CH)
        hi = cs[:, POOL_BUF + 1:POOL_BUF + 1 + t, sl]
        lo = cs[:, POOL_BUF + 1 - w:POOL_BUF + 1 - w + t, sl]
        cnt = jnp.minimum(pos + 1, w).astype(jnp.float32)
        outs.append((hi - lo) / cnt[None, :, None])
    mean = jnp.concatenate(outs, axis=-1)
    return (mean - p.astype(jnp.float32)).astype(p.dtype), xx[:, xx.shape[1] - POOL_BUF:]


def hgrn2_scan(q, k, v, logf, s0):
    n, t = q.shape[:2]
    c = min(HGRN_CHUNK, t)
    pad = (-t) % c
    if pad:
        pw = ((0, 0), (0, pad), (0, 0), (0, 0))
        q, k, v, logf = [jnp.pad(a, pw) for a in (q, k, v, logf)]
    nc = (t + pad) // c

    def to_chunks(a):
        return a.reshape(n, nc, c, *a.shape[2:]).swapaxes(0, 1)

    mask = jnp.tril(jnp.ones((c, c), dtype=bool))[None, :, :, None, None]

    def step(s, inp):
        qc, kc, vc, lc = inp
        b = jnp.cumsum(lc, axis=1)
        diff = b[:, :, None] - b[:, None]
        decay = jnp.where(mask, jnp.exp(jnp.minimum(diff, 0.0)), 0.0)
        attn = jnp.einsum('ntshk,nshk->nths', qc[:, :, None] * decay, kc)
        o = (jnp.einsum('nths,nshv->nthv', attn, vc)
             + jnp.einsum('nthk,nhkv->nthv', qc * jnp.exp(b), s))
        b_end = b[:, -1]
        s_new = (jnp.exp(b_end)[..., None] * s
                 + jnp.einsum('nshk,nshv->nhkv', kc * jnp.exp(b_end[:, None] - b), vc))
        return s_new, o

    s_fin, o = lax.scan(step, s0, (to_chunks(q), to_chunks(k), to_chunks(v), to_chunks(logf)))
    o = o.swapaxes(0, 1).reshape(n, nc * c, HGRN_HEADS, HGRN_DV)[:, :t]
    return o, s_fin


def hgrn2(zq, zf, zi, zg, lower, s0, norm_g):
    n, t, _ = zq.shape
    q = jax.nn.silu(zq.astype(jnp.float32).reshape(n, t, HGRN_HEADS, HGRN_DK))
    lb = lower.reshape(HGRN_HEADS, HGRN_DK)
    zf4 = zf.astype(jnp.float32).reshape(n, t, HGRN_HEADS, HGRN_DK)
    f = lb + (1.0 - lb) * jax.nn.sigmoid(zf4)
    logf = jnp.log(jnp.maximum(f, F_MIN))
    k = 1.0 - f
    v = zi.astype(jnp.float32).reshape(n, t, HGRN_HEADS, HGRN_DV)
    o, s = hgrn2_scan(q, k, v, logf, s0.astype(jnp.float32))
    o = o * lax.rsqrt(jnp.mean(o * o, axis=-1, keepdims=True) + EPS)
    o = o * norm_g.astype(jnp.float32).reshape(HGRN_HEADS, HGRN_DV)
    y = o.reshape(n, t, G) * jax.nn.silu(zg.astype(jnp.float32))
    return y.astype(zq.dtype), s.astype(s0.dtype)


def trunk(x, s_conv, s_pool, s_hgrn, s_conf, start_pos, wts):
    (norm_mix_pre, norm_mix_post, w_in, conv_w, pool_w, pool_scale, hgrn_lb, hgrn_norm,
     conf_dw, conf_b, conf_ln_g, conf_ln_b, w_out, norm_ffn_pre, norm_ffn_post,
     w_gate, w_up, w_down) = wts
    n, t, _ = x.shape
    sm = jax.nn.softmax(hgrn_lb.astype(jnp.float32), axis=0)
    lower = jnp.cumsum(sm, axis=0) - sm[0:1]
    new_conv, new_pool, new_hgrn, new_conf = [], [], [], []
    for l in range(DEPTH):
        h = rmsnorm(x, norm_mix_pre[l])
        (a_b, a_c, a_u, b_p, c_q, c_f, c_i, c_g, d_a, d_g) = jnp.split(h @ w_in[l], N_BLOCKS, axis=-1)
        ya, buf = causal_dwconv(a_c * a_u, s_conv[:, l], conv_w[l])
        ya = a_b * ya
        new_conv.append(buf)
        pooled, buf = multiscale_pool(b_p, s_pool[:, l], start_pos)
        yb = jnp.einsum('ntgc,gcd->ntgd', pooled.reshape(n, t, N_POOL, POOL_CH), pool_w[l])
        yb = yb.reshape(n, t, G) * pool_scale[l]
        new_pool.append(buf)
        yc, st = hgrn2(c_q, c_f, c_i, c_g, lower[l], s_hgrn[:, l], hgrn_norm[l])
        new_hgrn.append(st)
        u = d_a * jax.nn.sigmoid(d_g)
        yd, buf = causal_dwconv(u, s_conf[:, l], conf_dw[l])
        yd = jax.nn.silu(layernorm(yd + conf_b[l], conf_ln_g[l], conf_ln_b[l]))
        new_conf.append(buf)
        mix = jnp.concatenate([ya, yb, yc, yd], axis=-1) @ w_out[l]
        x = x + rmsnorm(mix, norm_mix_post[l])
        h = rmsnorm(x, norm_ffn_pre[l])
        ff = (jax.nn.silu(h @ w_gate[l]) * (h @ w_up[l])) @ w_down[l]
        x = x + rmsnorm(ff, norm_ffn_post[l])
    return (x, jnp.stack(new_conv, axis=1), jnp.stack(new_pool, axis=1),
            jnp.stack(new_hgrn, axis=1), jnp.stack(new_conf, axis=1))


def setup_inputs(seed: int = 0) -> dict:
    key = jax.random.key(seed)
    ks = jax.random.split(key, 32)

    def nrm(k, shape, s):
        return jax.random.normal(k, shape, jnp.float32) * s

    def gain(k, shape):
        return 1.0 + 0.05 * jax.random.normal(k, shape, jnp.float32)

    return {
        'x_prompt': nrm(ks[0], (BATCH, SEQ, D_MODEL), 1.0),
        'x_sample': nrm(ks[1], (DEC_BATCH, DEC_SEQ, D_MODEL), 1.0),
        'state_conv': nrm(ks[2], (DEC_BATCH, DEPTH, SC_WIDTH - 1, G), 1.0),
        'state_pool': nrm(ks[3], (DEC_BATCH, DEPTH, POOL_BUF, G), 1.0),
        'state_hgrn': nrm(ks[4], (DEC_BATCH, DEPTH, HGRN_HEADS, HGRN_DK, HGRN_DV), 0.5),
        'state_conf': nrm(ks[5], (DEC_BATCH, DEPTH, CONF_WIDTH - 1, G), 0.5),
        'norm_mix_pre': gain(ks[6], (DEPTH, D_MODEL)),
        'norm_mix_post': gain(ks[7], (DEPTH, D_MODEL)),
        'w_in': nrm(ks[8], (DEPTH, D_MODEL, D_IN), D_MODEL ** -0.5),
        'conv_w': nrm(ks[9], (DEPTH, SC_WIDTH, G), SC_WIDTH ** -0.5),
        'pool_w': nrm(ks[10], (DEPTH, N_POOL, POOL_CH, POOL_CH), POOL_CH ** -0.5),
        'pool_scale': gain(ks[11], (DEPTH, G)),
        'hgrn_lb': nrm(ks[12], (DEPTH, G), 0.5),
        'hgrn_norm': gain(ks[13], (DEPTH, G)),
        'conf_dw': nrm(ks[14], (DEPTH, CONF_WIDTH, G), CONF_WIDTH ** -0.5),
        'conf_b': nrm(ks[15], (DEPTH, G), 0.02),
        'conf_ln_g': gain(ks[16], (DEPTH, G)),
        'conf_ln_b': nrm(ks[17], (DEPTH, G), 0.02),
        'w_out': nrm(ks[18], (DEPTH, N_MIXERS * G, D_MODEL), (N_MIXERS * G) ** -0.5),
        'norm_ffn_pre': gain(ks[19], (DEPTH, D_MODEL)),
        'norm_ffn_post': gain(ks[20], (DEPTH, D_MODEL)),
        'w_gate': nrm(ks[21], (DEPTH, D_MODEL, D_FF), D_MODEL ** -0.5),
        'w_up': nrm(ks[22], (DEPTH, D_MODEL, D_FF), D_MODEL ** -0.5),
        'w_down': nrm(ks[23], (DEPTH, D_FF, D_MODEL), D_FF ** -0.5),
    }


def reference(x_prompt, x_sample, state_conv, state_pool, state_hgrn, state_conf,
              norm_mix_pre, norm_mix_post, w_in, conv_w, pool_w, pool_scale, hgrn_lb, hgrn_norm,
              conf_dw, conf_b, conf_ln_g, conf_ln_b, w_out, norm_ffn_pre, norm_ffn_post,
              w_gate, w_up, w_down):
    wts = (norm_mix_pre, norm_mix_post, w_in, conv_w, pool_w, pool_scale, hgrn_lb, hgrn_norm,
           conf_dw, conf_b, conf_ln_g, conf_ln_b, w_out, norm_ffn_pre, norm_ffn_post,
           w_gate, w_up, w_down)
    dt = x_prompt.dtype
    p_conv = jnp.zeros((BATCH, DEPTH, SC_WIDTH - 1, G), dt)
    p_pool = jnp.zeros((BATCH, DEPTH, POOL_BUF, G), dt)
    p_hgrn = jnp.zeros((BATCH, DEPTH, HGRN_HEADS, HGRN_DK, HGRN_DV), state_hgrn.dtype)
    p_conf = jnp.zeros((BATCH, DEPTH, CONF_WIDTH - 1, G), dt)
    y_prompt, conv_p, pool_p, hgrn_p, conf_p = trunk(x_prompt, p_conv, p_pool, p_hgrn, p_conf, 0, wts)
    y_sample, conv_s, pool_s, hgrn_s, conf_s = trunk(x_sample, state_conv, state_pool, state_hgrn,
                                                     state_conf, PAST_LEN, wts)
    return (y_prompt, y_sample, conv_p, pool_p, hgrn_p, conf_p, conv_s, pool_s, hgrn_s, conf_s)
```

```python
import numpy as np
from contextlib import ExitStack
import concourse.bass as bass
import concourse.mybir as mybir
from concourse.bass_utils import run_bass_kernel_spmd

F32 = mybir.dt.float32
BF16 = mybir.dt.bfloat16
AF = mybir.ActivationFunctionType
ALU = mybir.AluOpType

NT = 2176
EPS = 1e-6
F_MIN = 1e-20
ENGS = ("pe", "act", "dve", "pool", "sp")


class Buf:
    __slots__ = ("name", "w", "r")

    def __init__(self, name):
        self.name = name
        self.w = None
        self.r = []


class Op:
    __slots__ = ("eng", "fn", "waits", "signal", "idx", "is_dma", "chan", "val", "known", "is_nop")


class Sched:
    def __init__(self, nc, self_sync=("act", "dve", "pool")):
        self.nc = nc
        self.prog = {e: [] for e in ENGS}
        self.known = {e: {} for e in ENGS}
        self.self_sync = set(self_sync)
        self.chan_cnt = {}
        self.bufs = {}
        self.last = {}
        self.dma_pending = []

    def buf(self, name):
        b = self.bufs.get(name)
        if b is None:
            b = Buf(name)
            self.bufs[name] = b
        return b

    def _norm(self, lst):
        out = []
        for x in lst:
            if x is None:
                continue
            if isinstance(x, str):
                out.append(self.buf(x))
            else:
                out.extend(self._norm(x))
        return out

    def _add(self, eng, fn, reads, writes, is_dma=False, chan=None, extra=(), defer=0):
        reads = self._norm(reads)
        writes = self._norm(writes)
        op = Op()
        op.eng = eng
        op.fn = fn
        op.is_dma = is_dma
        op.chan = chan
        op.signal = False
        op.is_nop = False
        op.idx = len(self.prog[eng])
        op.waits = []
        kn = self.known[eng]
        toks = list(extra)
        for b in reads:
            if b.w is not None:
                toks.append(b.w)
        for b in writes:
            if b.w is not None:
                toks.append(b.w)
            toks.extend(b.r)
        best = {}
        for t in toks:
            if t[0] not in best or best[t[0]][1] < t[1]:
                best[t[0]] = t
        for t in best.values():
            src, v, top = t
            if src == eng and not top.is_dma and eng not in self.self_sync:
                continue
            if kn.get(src, -1) >= v:
                continue
            kn[src] = v
            op.waits.append(t)
            top.signal = True
            if top.known is not None:
                for s2, v2 in top.known.items():
                    if kn.get(s2, -1) < v2:
                        kn[s2] = v2
        if is_dma:
            n = self.chan_cnt.get(chan, 0) + 1
            self.chan_cnt[chan] = n
            op.val = 16 * n
            tok = ("dma:" + chan, op.val, op)
            op.known = None
            if defer >= 0:
                self.dma_pending.append([tok, defer])
        else:
            op.val = op.idx
            tok = (eng, op.idx, op)
            op.known = dict(kn)
            self.last[eng] = tok
        for b in reads:
            b.r.append(tok)
        for b in writes:
            b.w = tok
            b.r = []
        self.prog[eng].append(op)
        return op

    def op(self, eng, fn, reads=(), writes=()):
        return self._add(eng, fn, reads, writes)

    def dma(self, eng, fn, reads=(), writes=(), chan="d", defer=0):
        return self._add(eng, fn, reads, writes, is_dma=True, chan=chan, defer=defer)

    def barrier(self, engs=("act", "dve", "sp")):
        nc = self.nc
        toks = [self.last[e] for e in ENGS if e in self.last] + [t for t, d in self.dma_pending if d == 0]
        self.dma_pending = [[t, d - 1] for t, d in self.dma_pending if d > 0]
        hand = {"pe": nc.tensor, "act": nc.scalar, "dve": nc.vector, "sp": nc.sync}
        saved = dict(self.last)
        for e in engs:
            o = self._add(e, (lambda h=hand[e]: h.nop()), (), (), extra=toks)
            o.is_nop = True
        self.last = saved

    def emit(self, final_wait_chans=()):
        nc = self.nc
        with ExitStack() as es:
            esem = {e: es.enter_context(nc.semaphore("s_" + e)) for e in ENGS}
            csem = {c: es.enter_context(nc.semaphore("c_" + c)) for c in self.chan_cnt}
            sigcnt = {}
            for e in ENGS:
                c = 0
                for op in self.prog[e]:
                    if not op.is_dma and op.signal:
                        c += 1
                        sigcnt[(e, op.idx)] = c
            block = es.enter_context(nc.Block())
            hand = {"pe": nc.tensor, "act": nc.scalar, "dve": nc.vector, "pool": nc.gpsimd, "sp": nc.sync}

            def run(e):
                h = hand[e]
                for op in self.prog[e]:
                    for (src, v, top) in op.waits:
                        if top.is_dma:
                            h.wait_ge(csem[top.chan], v)
                        else:
                            h.wait_ge(esem[src], sigcnt[(src, v)])
                    ins = op.fn()
                    if op.is_dma:
                        ins.then_inc(csem[op.chan], 16)
                    elif op.signal:
                        ins.then_inc(esem[e], 1)
                if e == "sp":
                    for c in final_wait_chans:
                        h.wait_ge(csem[c], 16 * self.chan_cnt[c])

            @block.tensor
            def _(eng):
                run("pe")

            @block.scalar
            def _(eng):
                run("act")

            @block.vector
            def _(eng):
                run("dve")

            @block.gpsimd
            def _(eng):
                run("pool")

            @block.sync
            def _(eng):
                run("sp")


C_ID = 0
C_M0P = 128
C_M0S = 640
C_MC = 768
C_MS = 832
C_MJ = 896
C_OB = 904
C_RC = 1032
C_IW = 1064
NCST = 1066

P_G1, P_G2, P_G3, P_G4 = 0, 16, 32, 48
P_CW = 64
P_PS = 76
P_LB = 80
P_HN = 84
P_CB = 88
P_LG = 92
P_LBI = 96
P_DW = 100
NPAR = 224

NSLOT = 10
BIG32 = 12672


class _StopBuild(Exception):
    pass


MARKS = []
MARKS_DVE = []
S_holder = [None]


def build_program(stop=None):
    nc = bass.Bass("TRN2", target_bir_lowering=False)
    phase_ctr = [0]

    def phase():
        MARKS.append(sum(1 for o in S_holder[0].prog["pe"] if not getattr(o, "is_nop", False)))
        MARKS_DVE.append(sum(1 for o in S_holder[0].prog["dve"] if not getattr(o, "is_nop", False)))
        phase_ctr[0] += 1
        if stop is not None and phase_ctr[0] > stop:
            raise _StopBuild()

    def din(name, shape):
        return nc.dram_tensor(name, shape, F32, kind="ExternalInput").ap()

    def dout(name, shape):
        return nc.dram_tensor(name, shape, F32, kind="ExternalOutput").ap()

    xT = din("xT", [128, 8, NT])
    sconv = din("sconv", [128, 2, 2, 16, 2])
    spool = din("spool", [128, 2, 2, 16, 15])
    sconf = din("sconf", [128, 2, 2, 16, 30])
    shgrn = din("shgrn", [128, 2, 16, 2, 64])
    w_in_u = din("w_in_u", [2, 20, 128, 1024])
    w_out_u = din("w_out_u", [2, 8, 128, 1024])
    w_gate_u = din("w_gate_u", [2, 22, 128, 1024])
    w_up_u = din("w_up_u", [2, 22, 128, 1024])
    w_down_u = din("w_down_u", [2, 8, 128, 2816])
    par_d = din("par", [128, NPAR])
    cst_d = din("cst", [128, NCST])
    pwbd_d = din("pwbd", [128, 2, 2, 128])

    yT = dout("yT", [128, 8, NT])
    conv_pT = dout("conv_pT", [128, 2, 2, 2])
    pool_pT = dout("pool_pT", [128, 2, 2, 15])
    conf_pT = dout("conf_pT", [128, 2, 2, 30])
    hgrn_pT = dout("hgrn_pT", [128, 2, 2, 64])
    conv_sT = dout("conv_sT", [128, 2, 2, 16, 2])
    pool_sT = dout("pool_sT", [128, 2, 2, 16, 15])
    conf_sT = dout("conf_sT", [128, 2, 2, 16, 30])
    hgrn_sT = dout("hgrn_sT", [128, 2, 16, 2, 64])

    with ExitStack() as es:
        def sb(name, shape, dt):
            return es.enter_context(nc.sbuf_tensor(name, shape, dt))

        X = sb("X", [128, 8, NT], F32)
        HM = sb("HM", [128, 2, 8, 1152], BF16)
        BIG = sb("BIG", [128, BIG32], F32)
        SQ = sb("SQ", [128, 8, 512], BF16)
        RSTD = sb("RSTD", [128, 1152], F32)
        TMPA = sb("TMPA", [128, 512], F32)
        TMPB = sb("TMPB", [128, 512], F32)
        TMPC = sb("TMPC", [128, 512], F32)
        CST = sb("CST", [128, NCST], F32)
        PAR = sb("PAR", [128, NPAR], F32)
        PWB = sb("PWB", [128, 2, 2, 128], BF16)
        IDB = sb("IDB", [128, 128], BF16)
        ONEB = sb("ONEB", [128, 128], BF16)
        OBB = sb("OBB", [128, 128], BF16)
        EPSC = sb("EPSC", [128, 1], F32)
        LOW = sb("LOW", [128, 2, 2], F32)
        OML = sb("OML", [128, 2, 2], F32)
        LBT = sb("LBT", [128, 8], F32)
        CARA = sb("CARA", [128, 2, 2], F32)
        CARB = sb("CARB", [128, 2, 15], F32)
        CARD = sb("CARD", [128, 2, 30], BF16)
        SS = sb("SS", [128, 2, 64], F32)
        SBD = sb("SBD", [128, 2, 128], BF16)
        WR = sb("WR", [128, NSLOT, 1024], BF16)
        STGA = sb("STGA", [128, 2, 16, 2], F32)
        STGB = sb("STGB", [128, 2, 16, 15], F32)
        STGD = sb("STGD", [128, 2, 16, 30], F32)
        OPA = sb("OPA", [128, 2, 2], F32)
        OPB = sb("OPB", [128, 2, 15], F32)
        OPD = sb("OPD", [128, 2, 30], F32)
        PS_all = es.enter_context(nc.psum_tensor("psall", [128, 7, 512], F32))
        PS = [PS_all[:, i, :] for i in range(7)]
        PSB = es.enter_context(nc.psum_tensor("psb", [128, 1024], BF16))

        H = HM[:, 0]
        MIX = HM[:, 1]
        FF = HM[:].rearrange("p a k n -> p (a k n)").bitcast(F32).rearrange("p (k n) -> p k n", k=8)

        S = Sched(nc)
        S_holder[0] = S
        st = {"ps": 0, "w": 0, "held": set()}

        def mm(out, lhsT, rhs, start, stop, r, w):
            S.op("pe", lambda: nc.tensor.matmul(out, lhsT=lhsT, rhs=rhs, start=start, stop=stop), r, w)

        def tr(out, in_, ident, r, w):
            S.op("pe", lambda: nc.tensor.transpose(out, in_, ident), r, w)

        def act(out, in_, func, r, w, scale=None, bias=None):
            kw = {}
            if scale is not None:
                kw["scale"] = scale
            if bias is not None:
                kw["bias"] = bias
            S.op("act", lambda: nc.scalar.activation(out=out, in_=in_, func=func, **kw), r, w)

        def tt(out, in0, in1, op, r, w):
            S.op("dve", lambda: nc.vector.tensor_tensor(out=out, in0=in0, in1=in1, op=op), r, w)

        def ts(out, in0, s1, s2, op0, op1, r, w):
            if op1 is None:
                S.op("dve", lambda: nc.vector.tensor_scalar(out=out, in0=in0, scalar1=s1, scalar2=None, op0=op0), r, w)
            else:
                S.op("dve", lambda: nc.vector.tensor_scalar(out=out, in0=in0, scalar1=s1, scalar2=s2, op0=op0, op1=op1), r, w)

        def stt(out, in0, scalar, in1, op0, op1, r, w):
            S.op("dve", lambda: nc.vector.scalar_tensor_tensor(out=out, in0=in0, scalar=scalar, in1=in1, op0=op0, op1=op1), r, w)

        def vcp(out, in_, r, w):
            S.op("dve", lambda: nc.vector.tensor_copy(out=out, in_=in_), r, w)

        def vms(ap, val, w):
            S.op("dve", lambda: nc.vector.memset(ap, val), (), w)

        def vrec(out, in_, r, w):
            S.op("dve", lambda: nc.vector.reciprocal(out=out, in_=in_), r, w)

        def dma_in(out, in_, w, chan, defer=0):
            S.dma("sp", lambda: nc.sync.dma_start(out=out, in_=in_), (), w, chan=chan, defer=defer)

        def dma_out(out, in_, r, chan, defer=0):
            S.dma("sp", lambda: nc.sync.dma_start(out=out, in_=in_), r, (), chan="o_" + chan, defer=defer)

        def newps():
            while True:
                i = st["ps"] % 7
                st["ps"] += 1
                if i not in st["held"]:
                    return PS[i], "ps%d" % i

        def wget(src, ncols=1024):
            s = st["w"] % NSLOT
            st["w"] += 1
            name = "wr%d" % s
            dst = WR[:, s, 0:ncols]
            S.dma("pool", lambda: nc.gpsimd.dma_start(out=dst, in_=src), (), [name], chan=name)
            return WR[:, s, :], name

        def carve(off, shape, dt):
            n = 1
            for s_ in shape:
                n *= s_
            nb = n * (4 if dt == F32 else 2)
            assert off % 4 == 0
            n32 = (nb + 3) // 4
            assert off // 4 + n32 <= BIG32, (off, shape)
            ap = BIG[:, off // 4: off // 4 + n32]
            if dt == BF16:
                ap = ap.bitcast(BF16)[:, 0:n]
            if len(shape) == 2:
                ap = ap.rearrange("p (a b) -> p a b", a=shape[0])
            elif len(shape) == 3:
                ap = ap.rearrange("p (a b c) -> p a b c", a=shape[0], b=shape[1])
            elif len(shape) == 4:
                ap = ap.rearrange("p (a b c d) -> p a b c d", a=shape[0], b=shape[1], c=shape[2])
            return ap, off + 4 * n32

        def pc(col):
            return PAR[:, col:col + 1]

        dma_in(CST[:], cst_d, ["CST"], "c0")
        dma_in(PAR[:], par_d, ["PAR"], "c1")
        dma_in(TMPB[:], pwbd_d.rearrange("p a b c -> p (a b c)"), ["TMPB"], "c2")
        for k in range(8):
            dma_in(X[:, k, 0:1024], xT[:, k, 0:1024], ["X.%d.%d" % (k, t) for t in (0, 1)], "x%d" % k)
        for k in range(8):
            dma_in(X[:, k, 1024:NT], xT[:, k, 1024:NT], ["X.%d.%d" % (k, t) for t in (2, 3, 4)], "xb%d" % k)
        for c in range(2):
            dma_in(STGA[:, c], sconv[:, 0, c], ["STGA.%d" % c], "sa%d" % c, defer=-1)
            dma_in(STGB[:, c], spool[:, 0, c], ["STGB.%d" % c], "sb%d" % c, defer=-1)
            dma_in(STGD[:, c], sconf[:, 0, c], ["STGD.%d" % c], "sd%d" % c, defer=-1)
        vcp(PWB[:].rearrange("p a b c -> p (a b c)"), TMPB[:], ["TMPB"], ["PWB"])
        vcp(IDB[:], CST[:, C_ID:C_ID + 128], ["CST"], ["IDB"])
        vcp(OBB[:], CST[:, C_OB:C_OB + 128], ["CST"], ["OBB"])
        vms(ONEB[:], 1.0, ["ONEB"])
        vms(EPSC[:], EPS, ["EPSC"])
        act(LBT[:, 0:4], PAR[:, P_LB:P_LB + 4], AF.Exp, ["PAR"], ["LBT"])
        tt(LBT[:, 4:6], LBT[:, 0:2], LBT[:, 2:4], ALU.add, ["LBT"], ["LBT"])
        vrec(LBT[:, 6:8], LBT[:, 4:6], ["LBT"], ["LBT"])
        vms(LOW[:, 0, :], 0.0, ["LOW"])
        tt(LOW[:, 1, :], LBT[:, 2:4], LBT[:, 6:8], ALU.mult, ["LBT", "LOW"], ["LOW"])
        ts(OML[:].rearrange("p a b -> p (a b)"), LOW[:].rearrange("p a b -> p (a b)"), -1.0, 1.0, ALU.mult, ALU.add,
           ["LOW"], ["OML"])

        blocks = [
            [(0, 0, 0, 512), (1, 512, 512, 512)],
            [(2, 1024, 0, 512), (3, 1536, 512, 512), (4, 2048, 1024, 128)],
        ]

        def xb(k, gt):
            return "X.%d.%d" % (k, gt)

        def rms_stats(srcs, srcbufs, lc0, w, tag, presq=False):
            for k in range(8):
                if not presq:
                    act(SQ[:, k, 0:w], srcs[k], AF.Square, [srcbufs[k]], ["SQ.%d" % k])
            ps, psn = newps()
            for k in range(8):
                mm(ps[:, 0:w], ONEB[:], SQ[:, k, 0:w], k == 0, k == 7, ["ONEB", "SQ.%d" % k], [psn])
            act(TMPA[:, 0:w], ps[:, 0:w], AF.Ln, [psn, "EPSC"], ["TMPA"], scale=1.0 / 1024.0, bias=EPSC[:, 0:1])
            act(RSTD[:, lc0:lc0 + w], TMPA[:, 0:w], AF.Exp, ["TMPA"], ["RSTD.%d" % (lc0 // 512)], scale=-0.5)

        def make_h(tiles, gcol):
            for (gt, c0, lc0, w) in tiles:
                rms_stats([X[:, k, c0:c0 + w] for k in range(8)], [xb(k, gt) for k in range(8)], lc0, w, str(gt))
                for k in range(8):
                    stt(H[:, k, lc0:lc0 + w], X[:, k, c0:c0 + w], pc(gcol + k), RSTD[:, lc0:lc0 + w], ALU.mult, ALU.mult,
                        [xb(k, gt), "RSTD.%d" % (lc0 // 512), "PAR"], ["H.%d.%d" % (k, gt), "HMA.%d" % (k // 2)])

        def add_residual(src, srcname, tiles, gcol):
            for (gt, c0, lc0, w) in tiles:
                rms_stats([src[:, k, lc0:lc0 + w] for k in range(8)], ["%s.%d.%d" % (srcname, k, gt) for k in range(8)],
                          lc0, w, str(gt))
                for k in range(8):
                    stt(TMPB[:, 0:w], src[:, k, lc0:lc0 + w], pc(gcol + k), RSTD[:, lc0:lc0 + w], ALU.mult, ALU.mult,
                        ["%s.%d.%d" % (srcname, k, gt), "RSTD.%d" % (lc0 // 512), "PAR"] + (["HMA.%d" % k] if srcname == "FF" else []),
                        ["TMPB"])
                    tt(X[:, k, c0:c0 + w], X[:, k, c0:c0 + w], TMPB[:, 0:w], ALU.add, [xb(k, gt), "TMPB"], [xb(k, gt)])

        def proj_k(unit, uname, tiles_, evac):
            banks = [newps() for _ in tiles_]
            for k in range(8):
                for (gt, c0, lc0, w), (ps, psn) in zip(tiles_, banks):
                    mm(ps[:, 0:w], unit[:, k * 128:(k + 1) * 128], H[:, k, lc0:lc0 + w], k == 0, k == 7,
                       [uname, "H.%d.%d" % (k, gt)], [psn])
            for (gt, c0, lc0, w), (ps, psn) in zip(tiles_, banks):
                evac(gt, c0, lc0, w, ps, psn)

        def proj(unit, uname, tiles, evac):
            for (gt, c0, lc0, w) in tiles:
                ps, psn = newps()
                for k in range(8):
                    mm(ps[:, 0:w], unit[:, k * 128:(k + 1) * 128], H[:, k, lc0:lc0 + w], k == 0, k == 7,
                       [uname, "H.%d.%d" % (k, gt)], [psn])
                evac(gt, c0, lc0, w, ps, psn)

        def body():
          for l in range(2):
            for b in range(2):
                tiles = blocks[b]
                TB = 1152 if b == 1 else 1024
                ptiles = [t for t in tiles if t[0] != 4]
                has_s = (b == 1)
                phase()
                make_h(tiles, P_G1 + l * 8)

                def win(u):
                    return wget(w_in_u[l, u])

                phase()
                S.barrier()
                off = 0
                UAp, off = carve(off, [2, 1026], F32)
                UAs, off = carve(off, [2, 16, 10], F32)
                ACC, off = carve(off, [2, 1152], F32)
                ACt, off = carve(off, [1152], F32)
                PBp, off = carve(off, [2, 1040], F32)
                PBs, off = carve(off, [2, 16, 23], F32)
                T1, off = carve(off, [1040], F32)
                T2, off = carve(off, [1040], F32)
                T1s, off = carve(off, [16, 23], F32)
                T2s, off = carve(off, [16, 23], F32)
                T16, off = carve(off, [16], F32)
                PL, off = carve(off, [2, 1152], BF16)

                def A1(c):
                    if b == 0:
                        vms(UAp[:, c, 0:2], 0.0, ["UAp.%d" % c])
                    else:
                        vcp(UAp[:, c, 0:2], CARA[:, c, :], ["CARA"], ["UAp.%d" % c])
                        vcp(UAs[:, c, :, 0:2], STGA[:, c], ["STGA.%d" % c], ["UAs.%d" % c])
                    u_c, n_c = win(2 + c)
                    u_u, n_u = win(4 + c)

                    def ev_c(gt, c0, lc0, w, ps, psn):
                        act(ACt[:, lc0:lc0 + w], ps[:, 0:w], AF.Copy, [psn], ["ACt.%d" % gt])

                    def ev_u(gt, c0, lc0, w, ps, psn):
                        if gt != 4:
                            tt(UAp[:, c, 2 + lc0:2 + lc0 + w], ps[:, 0:w], ACt[:, lc0:lc0 + w], ALU.mult,
                               [psn, "ACt.%d" % gt], ["UAp.%d" % c])
                        else:
                            tt(UAs[:, c, :, 2:10], ps[:, 0:128].rearrange("p (j t) -> p j t", t=8),
                               ACt[:, lc0:lc0 + 128].rearrange("p (j t) -> p j t", t=8), ALU.mult,
                               [psn, "ACt.%d" % gt], ["UAs.%d" % c])

                    proj(u_c, n_c, tiles, ev_c)
                    proj(u_u, n_u, tiles, ev_u)

                def A2(c):
                    cw = P_CW + (l * 2 + c) * 3
                    ts(ACC[:, c, 0:1024], UAp[:, c, 0:1024], pc(cw), None, ALU.mult, None, ["UAp.%d" % c, "PAR"], ["ACC.%d" % c])
                    for kk in (1, 2):
                        stt(ACC[:, c, 0:1024], UAp[:, c, kk:kk + 1024], pc(cw + kk), ACC[:, c, 0:1024], ALU.mult, ALU.add,
                            ["UAp.%d" % c, "ACC.%d" % c, "PAR"], ["ACC.%d" % c])
                    if has_s:
                        accs = ACC[:, c, 1024:1152].rearrange("p (j t) -> p j t", t=8)
                        ts(accs, UAs[:, c, :, 0:8], pc(cw), None, ALU.mult, None, ["UAs.%d" % c, "PAR"], ["ACCs.%d" % c])
                        for kk in (1, 2):
                            stt(accs, UAs[:, c, :, kk:kk + 8], pc(cw + kk), accs, ALU.mult, ALU.add,
                                ["UAs.%d" % c, "ACCs.%d" % c, "PAR"], ["ACCs.%d" % c])

                def A3(c):
                    u_b, n_b = win(0 + c)

                    def ev_b(gt, c0, lc0, w, ps, psn):
                        tt(MIX[:, c, lc0:lc0 + w], ps[:, 0:w], ACC[:, c, lc0:lc0 + w], ALU.mult,
                           [psn, "ACC.%d" % c, "ACCs.%d" % c], ["MIX.%d" % c])

                    proj(u_b, n_b, tiles, ev_b)
                    if b == 0:
                        vcp(CARA[:, c, :], UAp[:, c, 1024:1026], ["UAp.%d" % c], ["CARA"])
                    else:
                        vcp(OPA[:, c, :], UAp[:, c, 1024:1026], ["UAp.%d" % c], ["OPA.%d" % c])
                        dma_out(conv_pT[:, l, c, :], OPA[:, c, :], ["OPA.%d" % c], "cpA%d" % c, defer=-1)
                        vcp(STGA[:, c], UAs[:, c, :, 8:10], ["UAs.%d" % c], ["STGA.%d" % c])
                        dma_out(conv_sT[:, l, c], STGA[:, c], ["STGA.%d" % c], "csA%d" % c, defer=-1)
                        if l == 0:
                            dma_in(STGA[:, c], sconv[:, 1, c], ["STGA.%d" % c], "sa%d" % c, defer=-1)

                def B1(c):
                    if b == 0:
                        vms(PBp[:, c, 0:15], 0.0, ["PBp.%d" % c])
                    else:
                        vcp(PBp[:, c, 0:15], CARB[:, c, :], ["CARB"], ["PBp.%d" % c])
                        vcp(PBs[:, c, :, 0:15], STGB[:, c], ["STGB.%d" % c], ["PBs.%d" % c])
                    u_p, n_p = win(6 + c)

                    def ev_p(gt, c0, lc0, w, ps, psn):
                        if gt != 4:
                            act(PBp[:, c, 15 + lc0:15 + lc0 + w], ps[:, 0:w], AF.Copy, [psn], ["PBp.%d" % c])
                        else:
                            act(PBs[:, c, :, 15:23], ps[:, 0:128].rearrange("p (j t) -> p j t", t=8), AF.Copy,
                                [psn], ["PBs.%d" % c])

                    proj(u_p, n_p, tiles, ev_p)

                def B2(c):
                    NP_ = 1039
                    pb = "PBp.%d" % c
                    tt(T1[:, 1:NP_], PBp[:, c, 1:NP_], PBp[:, c, 0:NP_ - 1], ALU.add, [pb], ["T1"])
                    tt(T2[:, 3:NP_], T1[:, 3:NP_], T1[:, 1:NP_ - 2], ALU.add, ["T1"], ["T2"])
                    if c == 1:
                        tt(T1[:, 7:NP_], T2[:, 7:NP_], T2[:, 3:NP_ - 4], ALU.add, ["T2", "T1"], ["T1"])
                        tt(T2[:, 15:NP_], T1[:, 15:NP_], T1[:, 7:NP_ - 8], ALU.add, ["T1", "T2"], ["T2"])
                    iw = CST[:, C_IW + c:C_IW + c + 1]
                    for (lo, hi, Tw) in ((0, 64, T1), (64, 128, T2)):
                        stt(PL[lo:hi, c, 0:1024], Tw[lo:hi, 15:NP_], iw[lo:hi], PBp[lo:hi, c, 15:NP_], ALU.mult, ALU.subtract,
                            ["T1", "T2", pb, "CST"], ["PL.%d" % c])
                        if b == 0:
                            tt(T16[lo:hi, :], Tw[lo:hi, 15:31], CST[lo:hi, C_RC + 16 * c:C_RC + 16 * c + 16], ALU.mult,
                               ["T1", "T2", "CST"], ["T16"])
                            tt(PL[lo:hi, c, 0:16], T16[lo:hi, :], PBp[lo:hi, c, 15:31], ALU.subtract,
                               ["T16", pb, "PL.%d" % c], ["PL.%d" % c])
                    if has_s:
                        sbn = "PBs.%d" % c
                        tt(T1s[:, :, 1:23], PBs[:, c, :, 1:23], PBs[:, c, :, 0:22], ALU.add, [sbn], ["T1s"])
                        tt(T2s[:, :, 3:23], T1s[:, :, 3:23], T1s[:, :, 1:21], ALU.add, ["T1s"], ["T2s"])
                        if c == 1:
                            tt(T1s[:, :, 7:23], T2s[:, :, 7:23], T2s[:, :, 3:19], ALU.add, ["T2s", "T1s"], ["T1s"])
                            tt(T2s[:, :, 15:23], T1s[:, :, 15:23], T1s[:, :, 7:15], ALU.add, ["T1s", "T2s"], ["T2s"])
                        pls = PL[:, c, 1024:1152].rearrange("p (j t) -> p j t", t=8)
                        for (lo, hi, Tw) in ((0, 64, T1s), (64, 128, T2s)):
                            stt(pls[lo:hi], Tw[lo:hi, :, 15:23], iw[lo:hi], PBs[lo:hi, c, :, 15:23], ALU.mult, ALU.subtract,
                                ["T1s", "T2s", sbn, "CST"], ["PLs.%d" % c])

                def B3(c):
                    pb = "PBp.%d" % c
                    for (gt, c0, lc0, w) in tiles:
                        ps, psn = newps()
                        mm(ps[:, 0:w], PWB[:, l, c, :], PL[:, c, lc0:lc0 + w], True, True,
                           ["PWB", "PL.%d" % c, "PLs.%d" % c], [psn])
                        act(MIX[:, 2 + c, lc0:lc0 + w], ps[:, 0:w], AF.Identity, [psn, "PAR"], ["MIX.%d" % (2 + c)],
                            scale=pc(P_PS + l * 2 + c))
                    if b == 0:
                        vcp(CARB[:, c, :], PBp[:, c, 1024:1039], [pb], ["CARB"])
                    else:
                        vcp(OPB[:, c, :], PBp[:, c, 1024:1039], [pb], ["OPB.%d" % c])
                        dma_out(pool_pT[:, l, c, :], OPB[:, c, :], ["OPB.%d" % c], "cpB%d" % c, defer=-1)
                        vcp(STGB[:, c], PBs[:, c, :, 8:23], ["PBs.%d" % c], ["STGB.%d" % c])
                        dma_out(pool_sT[:, l, c], STGB[:, c], ["STGB.%d" % c], "csB%d" % c, defer=-1)
                        if l == 0:
                            dma_in(STGB[:, c], spool[:, 1, c], ["STGB.%d" % c], "sb%d" % c, defer=-1)

                A1(0); B1(0); A2(0); B2(0); A1(1); A3(0); B1(1); B3(0); A2(1); B2(1); A3(1); B3(1)
                phase()

                phase()
                S.barrier()
                off = 0
                UDp, off = carve(off, [2, 1054], BF16)
                UDT, off = carve(off, [2, 30], F32)
                UDs32, off = carve(off, [2, 16, 38], F32)
                UDs, off = carve(off, [2, 16, 38], BF16)
                DIAG, off = carve(off, [2, 31, 128], BF16)
                Z2, off = carve(off, [2, 2, 512], F32)
                ZB, off = carve(off, [4, 512], BF16)
                SD3, off = carve(off, [2, 512], F32)
                SIG = SD3[:, 0]
                D1, off = carve(off, [512], F32)
                D2, off = carve(off, [512], F32)
                for kk in range(31):
                    ts(DIAG[:, 0, kk, :], CST[:, C_ID:C_ID + 128], pc(P_DW + (l * 2 + 0) * 31 + kk), None, ALU.mult, None,
                       ["CST", "PAR"], ["DIAG.%d.%d" % (0, kk)])
                    act(DIAG[:, 1, kk, :], CST[:, C_ID:C_ID + 128], AF.Copy, ["CST", "PAR"], ["DIAG.%d.%d" % (1, kk)],
                        scale=pc(P_DW + (l * 2 + 1) * 31 + kk))
                for c in range(2):
                    if b == 0:
                        vms(UDp[:, c, 0:30], 0.0, ["UDp.%d" % c])
                    else:
                        vcp(UDp[:, c, 0:30], CARD[:, c, :], ["CARD"], ["UDp.%d" % c])
                        vcp(UDs32[:, c, :, 0:30], STGD[:, c], ["STGD.%d" % c], ["UDs32.%d" % c])
                    u_g, n_g = win(18 + c)
                    u_a, n_a = win(16 + c)
                    for tl in tiles:
                        (gt, c0, lc0, w) = tl
                        proj(u_g, n_g, [tl], lambda gt, c0, lc0, w, ps, psn: act(SIG[:, 0:w], ps[:, 0:w], AF.Sigmoid, [psn], ["SIG"]))

                        def ev_a(gt, c0, lc0, w, ps, psn, c=c):
                            if gt != 4:
                                tt(UDp[:, c, 30 + lc0:30 + lc0 + w], ps[:, 0:w], SIG[:, 0:w], ALU.mult, [psn, "SIG"], ["UDp.%d" % c])
                                if gt == 3:
                                    tt(UDT[:, c, :], ps[:, 482:512], SIG[:, 482:512], ALU.mult, [psn, "SIG"], ["UDT.%d" % c])
                            else:
                                tt(UDs32[:, c, :, 30:38], ps[:, 0:128].rearrange("p (j t) -> p j t", t=8),
                                   SIG[:, 0:128].rearrange("p (j t) -> p j t", t=8), ALU.mult, [psn, "SIG"], ["UDs32.%d" % c])

                        proj(u_a, n_a, [tl], ev_a)
                    if has_s:
                        vcp(UDs[:, c], UDs32[:, c], ["UDs32.%d" % c], ["UDs.%d" % c])
                    if b == 0:
                        vcp(CARD[:, c, :], UDp[:, c, 1024:1054], ["UDp.%d" % c], ["CARD"])
                    else:
                        vcp(OPD[:, c, :], UDT[:, c, :], ["UDT.%d" % c], ["OPD.%d" % c])
                        dma_out(conf_pT[:, l, c, :], OPD[:, c, :], ["OPD.%d" % c], "cpD%d" % c, defer=-1)
                        vcp(STGD[:, c], UDs32[:, c, :, 8:38], ["UDs32.%d" % c], ["STGD.%d" % c])
                        dma_out(conf_sT[:, l, c], STGD[:, c], ["STGD.%d" % c], "csD%d" % c, defer=-1)
                        if l == 0:
                            dma_in(STGD[:, c], sconf[:, 1, c], ["STGD.%d" % c], "sd%d" % c, defer=-1)

                def d_conv(ti):
                    (gt, c0, lc0, w) = tiles[ti]
                    zi = ti % 2
                    for c in range(2):
                        ps, psn = newps()
                        for kk in range(31):
                            if gt != 4:
                                rhs = UDp[:, c, lc0 + kk:lc0 + kk + w]
                                rb = "UDp.%d" % c
                            else:
                                rhs = UDs[:, c, :, kk:kk + 8]
                                rb = "UDs.%d" % c
                            mm(ps[:, 0:w], DIAG[:, c, kk, :], rhs, kk == 0, kk == 30, ["DIAG.%d.%d" % (c, kk), rb], [psn])
                        act(Z2[:, zi, c, 0:w], ps[:, 0:w], AF.Identity, [psn, "PAR"], ["Z.%d.%d" % (c, zi)],
                            bias=pc(P_CB + l * 2 + c))

                def d_ln(ti):
                    (gt, c0, lc0, w) = tiles[ti]
                    zi = ti % 2
                    Zt = Z2[:, zi, :, 0:w]
                    zn = ["Z.0.%d" % zi, "Z.1.%d" % zi]
                    act(ZB[:, 0:2, 0:w], Zt, AF.Copy, zn, ["ZB.0", "ZB.1"])
                    act(ZB[:, 2:4, 0:w], Zt, AF.Square, zn, ["ZB.2", "ZB.3"])
                    ps1, pn1 = newps()
                    ps2, pn2 = newps()
                    for c in range(2):
                        mm(ps1[:, 0:w], ONEB[:], ZB[:, c, 0:w], c == 0, c == 1, ["ONEB", "ZB.%d" % c], [pn1])
                    for c in range(2):
                        mm(ps2[:, 0:w], ONEB[:], ZB[:, 2 + c, 0:w], c == 0, c == 1, ["ONEB", "ZB.%d" % (2 + c)], [pn2])
                    act(D1[:, 0:w], ps1[:, 0:w], AF.Identity, [pn1], ["D1"], scale=1.0 / 256.0)
                    tt(D2[:, 0:w], D1[:, 0:w], D1[:, 0:w], ALU.mult, ["D1"], ["D2"])
                    stt(D2[:, 0:w], ps2[:, 0:w], 1.0 / 256.0, D2[:, 0:w], ALU.mult, ALU.subtract, [pn2, "D2"], ["D2"])
                    act(D2[:, 0:w], D2[:, 0:w], AF.Ln, ["D2", "EPSC"], ["D2"], bias=EPSC[:, 0:1])
                    act(D2[:, 0:w], D2[:, 0:w], AF.Exp, ["D2"], ["D2"], scale=-0.5)
                    tt(SD3[:, :, 0:w], Zt, D1[:, 0:w].unsqueeze(1).broadcast_to([128, 2, w]), ALU.subtract,
                       zn + ["D1"], ["SIG", "D3"])
                    tt(SD3[:, :, 0:w], SD3[:, :, 0:w], D2[:, 0:w].unsqueeze(1).broadcast_to([128, 2, w]), ALU.mult,
                       ["SIG", "D3", "D2"], ["SIG", "D3"])
                    for c in range(2):
                        act(MIX[:, 6 + c, lc0:lc0 + w], SD3[:, c, 0:w], AF.Silu, ["SIG", "D3", "PAR"], ["MIX.%d" % (6 + c)],
                            scale=pc(P_LG + l * 2 + c), bias=pc(P_LBI + l * 2 + c))

                nt_ = len(tiles)
                d_conv(0)
                for ti in range(nt_):
                    if ti + 1 < nt_:
                        d_conv(ti + 1)
                    d_ln(ti)

                phase()
                S.barrier()
                off = 0
                QT, off = carve(off, [2, 1152], BF16)
                KTt, off = carve(off, [2, 1152], BF16)
                O, off = carve(off, [2, 1152], F32)
                EBEp, off = carve(off, [2, 16], F32)
                EBEs, off = carve(off, [2, 16], F32)
                STt2, off = carve(off, [2, 8, 2, 64], F32)
                off_dead = off
                VP, off = carve(off, [2, 2, 2, 128], BF16)
                KP, off = carve(off, [2, 2, 2, 128], BF16)
                AT, off = carve(off, [2, 4, 64], BF16)
                S0BD, off = carve(off, [8, 2, 128], BF16)
                VBLK, off = carve(off, [2, 8, 64], BF16)
                off3 = off
                TF2, off = carve(off, [2, 1152], F32)
                TE_a, off = carve(off, [512], F32)
                uq = [win(8 + p) for p in range(2)]
                uf = [win(10 + p) for p in range(2)]
                for tl_ in tiles:
                    for p in range(2):
                        proj(uq[p][0], uq[p][1], [tl_], lambda gt, c0, lc0, w, ps, psn: act(O[:, p, lc0:lc0 + w], ps[:, 0:w], AF.Silu, [psn],
                                                                                            ["OQ.%d.%d" % (p, gt)]))
                    for p in range(2):
                        proj(uf[p][0], uf[p][1], [tl_], lambda gt, c0, lc0, w, ps, psn: act(TF2[:, p, lc0:lc0 + w], ps[:, 0:w], AF.Sigmoid, [psn],
                                                                                            ["TF.%d.%d" % (p, hx) for hx in range(lc0 // 256, (lc0 + w + 255) // 256)]))
                items = []
                for (gt_, c0_, lc0_, w_) in tiles:
                    halves = [(lc0_, 256), (lc0_ + 256, 256)] if w_ == 512 else [(lc0_, w_)]
                    for (hl, hw) in halves:
                        for p_ in range(2):
                            items.append(((gt_, c0_ + (hl - lc0_), hl, hw), p_))

                def p1_stage1(it):
                    (gt, c0, lc0, w), p = items[it]
                    TLB = TMPB if it % 2 == 0 else TMPC
                    tln, tfn = ("TMPB" if it % 2 == 0 else "TMPC"), "TF.%d.%d" % (p, lc0 // 256)
                    TFt = TF2[:, p, lc0:lc0 + w]
                    ts(TFt, TFt, OML[:, l, p:p + 1], LOW[:, l, p:p + 1], ALU.mult, ALU.add, [tfn, "OML", "LOW"], [tfn])
                    ts(TLB[:, 0:w], TFt, F_MIN, None, ALU.max, None, [tfn], [tln])
                    act(TLB[:, 0:w], TLB[:, 0:w], AF.Ln, [tln], [tln])

                def p1_stage2(it):
                    (gt, c0, lc0, w), p = items[it]
                    TLB = TMPB if it % 2 == 0 else TMPC
                    TE = TE_a if it % 2 == 0 else TMPA
                    tln, ten, tfn = ("TMPB" if it % 2 == 0 else "TMPC"), ("TE_a" if it % 2 == 0 else "TMPA"), "TF.%d.%d" % (p, lc0 // 256)
                    TFt = TF2[:, p, lc0:lc0 + w]
                    m0 = CST[:, C_M0P:C_M0P + w] if gt != 4 else CST[:, C_M0S:C_M0S + 128]
                    S.op("dve", lambda: nc.vector.tensor_tensor_scan(
                        out=TLB[:, 0:w], data0=m0, data1=TLB[:, 0:w], initial=0.0, op0=ALU.mult, op1=ALU.add),
                         [tln, "CST"], [tln])
                    act(TE[:, 0:w], TLB[:, 0:w], AF.Exp, [tln], [ten])
                    tt(QT[:, p, lc0:lc0 + w], O[:, p, lc0:lc0 + w], TE[:, 0:w], ALU.mult, ["OQ.%d.%d" % (p, gt), ten], ["QT.%d.%d" % (p, lc0 // 256)])
                    if gt != 4:
                        ch0 = lc0 // 64
                        vcp(EBEp[:, p, ch0:ch0 + w // 64], TE[:, 0:w].rearrange("p (a b) -> p a b", b=64)[:, :, 63], [ten], ["EBEp"])
                    else:
                        vcp(EBEs[:, p, :], TE[:, 0:128].rearrange("p (a b) -> p a b", b=8)[:, :, 7], [ten], ["EBEs"])
                    act(TE[:, 0:w], TLB[:, 0:w], AF.Exp, [tln, ten], [ten], scale=-1.0)
                    ts(TFt, TFt, -1.0, 1.0, ALU.mult, ALU.add, [tfn], [tfn])
                    tt(KTt[:, p, lc0:lc0 + w], TFt, TE[:, 0:w], ALU.mult, [tfn, ten], ["KTt.%d.%d" % (p, lc0 // 256)])

                p1_calls = [(p1_stage1, 0, None)]
                for it in range(1, len(items)):
                    p1_calls.append((p1_stage1, it, None))
                    p1_calls.append((p1_stage2, it - 1, it - 1))
                p1_calls.append((p1_stage2, len(items) - 1, len(items) - 1))
                p1_state = {"pos": 0, "done": -1}

                def p1_emit_one():
                    if p1_state["pos"] >= len(p1_calls):
                        return False
                    fn, arg, done = p1_calls[p1_state["pos"]]
                    p1_state["pos"] += 1
                    fn(arg)
                    if done is not None:
                        p1_state["done"] = done
                    return True

                def p1_flush_tile(hidx):
                    while p1_state["done"] < 2 * hidx + 1:
                        assert p1_emit_one()
                if st["w"] % NSLOT == NSLOT - 1:
                    st["w"] += 1
                vslot = st["w"] % NSLOT
                u_i0, n_i0 = win(12)
                u_i1, n_i1 = win(13)
                vms(VP[:].rearrange("p a b c d -> p (a b c d)"), 0.0, ["VP0", "VP1"])
                vms(KP[:].rearrange("p a b c d -> p (a b c d)"), 0.0, ["KP0", "KP1"])
                vms(S0BD[:].rearrange("p a b c -> p (a b c)"), 0.0, ["S0BD"])
                if has_s:
                    for sc_ in range(2):
                        dma_in(STt2[:, sc_], shgrn[:, l, 8 * sc_:8 * sc_ + 8, :, :], ["STt%d" % sc_], "sh%d" % sc_)
                if b == 0:
                    vms(SS[:].rearrange("p a b -> p (a b)"), 0.0, ["SS"])
                    vms(SBD[:].rearrange("p a b -> p (a b)"), 0.0, ["SBD"])
                nch = TB // 64
                st["held"] = {5, 6}

                def stageA(ch):
                        lcol = ch * 64
                        is_s = ch >= 16
                        gt = tiles[min(lcol // 512, len(tiles) - 1)][0]
                        e = (ch // 2) % 2
                        hf = ch % 2
                        r0, r1 = hf * 64, hf * 64 + 64
                        vpn, kpn, atn = "VP%d" % e, "KP%d" % e, "AT%d.%d" % (e, hf)
                        if hf == 0:
                            psv, pvn = newps()
                            for k in range(8):
                                mm(psv[:, 0:256], H[:, k, lcol:lcol + 128], WR[:, vslot:vslot + 2, k * 128:(k + 1) * 128],
                                   k == 0, k == 7, [n_i0, n_i1, "H.%d.%d" % (k, gt)], [pvn])
                            for h in range(2):
                                act(VP[:, e, :, h, h * 64:(h + 1) * 64],
                                    psv[:, 0:256].rearrange("s (p h v) -> s p h v", p=2, h=2)[:, :, h, :], AF.Copy, [pvn], [vpn])
                            for p in range(2):
                                tr(PSB[:, p * 128:(p + 1) * 128], KTt[:, p, lcol:lcol + 128], IDB[:], ["KTt.%d.%d" % (p, lcol // 256), "IDB"], ["psb"])
                            for h in range(2):
                                act(KP[:, e, :, h, h * 64:(h + 1) * 64],
                                    PSB[:, 0:256].rearrange("s (p h v) -> s p h v", p=2, h=2)[:, :, h, :], AF.Copy, ["psb"], [kpn])
                        for h in range(2):
                            for p in range(2):
                                mm(PS_all[r0:r1, 5 + h, p * 64:(p + 1) * 64], KTt[h * 64:(h + 1) * 64, p, lcol:lcol + 64],
                                   QT[h * 64:(h + 1) * 64, p, lcol:lcol + 64], True, True, ["KTt.%d.%d" % (p, lcol // 256), "QT.%d.%d" % (p, lcol // 256)], ["ps%d" % (5 + h)])
                        mk = CST[r0:r1, C_MS:C_MS + 64] if is_s else CST[r0:r1, C_MC:C_MC + 64]
                        tt(AT[r0:r1, e].rearrange("s (p h) t -> s h p t", h=2),
                           PS_all[r0:r1, 5:7, 0:128].rearrange("s h (p t) -> s h p t", p=2),
                           mk.unsqueeze(1).unsqueeze(1).broadcast_to([64, 2, 2, 64]), ALU.mult, ["ps5", "ps6", "CST"], [atn])

                def stageB(ch):
                        lcol = ch * 64
                        is_s = ch >= 16
                        gt = tiles[min(lcol // 512, len(tiles) - 1)][0]
                        e = (ch // 2) % 2
                        hf = ch % 2
                        r0, r1 = hf * 64, hf * 64 + 64
                        vpn, kpn, atn = "VP%d" % e, "KP%d" % e, "AT%d.%d" % (e, hf)
                        if not is_s:
                            pso, pon = newps()
                            for p in range(2):
                                for h in range(2):
                                    mm(pso[:, p * 64:(p + 1) * 64], VP[r0:r1, e, p, h, :], AT[r0:r1, e, p * 2 + h, :], h == 0, False,
                                       [vpn, atn], [pon])
                                mm(pso[:, p * 64:(p + 1) * 64], SBD[:, p, :], QT[:, p, lcol:lcol + 64], False, True,
                                   ["SBD", "QT.%d.%d" % (p, lcol // 256)], [pon])
                            act(O[:, :, lcol:lcol + 64], pso[:, 0:128].rearrange("p (a b) -> p a b", a=2), AF.Copy, [pon],
                                ["O.%d" % gt, "OQ.0.%d" % gt, "OQ.1.%d" % gt])
                            psu, pun = newps()
                            for p in range(2):
                                for h in range(2):
                                    mm(psu[:, p * 64:(p + 1) * 64], KP[r0:r1, e, p, h, :], VP[r0:r1, e, p, h, h * 64:(h + 1) * 64],
                                       h == 0, h == 1, [kpn, vpn], [pun])
                            tt(SS[:], psu[:, 0:128].rearrange("p (a b) -> p a b", a=2), SS[:], ALU.add, [pun, "SS"], ["SS"])
                            tt(SS[:], SS[:], EBEp[:, :, ch:ch + 1].broadcast_to([128, 2, 64]), ALU.mult, ["SS", "EBEp"], ["SS"])
                            tt(SBD[:].rearrange("q p (h v) -> q p h v", h=2), SS[:].unsqueeze(2).broadcast_to([128, 2, 2, 64]),
                               CST[:, C_OB:C_OB + 128].rearrange("q (h v) -> q h v", h=2).unsqueeze(1).broadcast_to([128, 2, 2, 64]),
                               ALU.mult, ["SS", "CST"], ["SBD"])
                            if b == 1 and ch == 15:
                                dma_out(hgrn_pT[:, l, :, :], SS[:], ["SS"], "hp", defer=-1)
                        else:
                            sc = ch - 16
                            STt = STt2[:, sc]
                            stn = "STt%d" % sc
                            for h in range(2):
                                vcp(S0BD[h * 64:(h + 1) * 64, :, :, h * 64:(h + 1) * 64], STt[h * 64:(h + 1) * 64, :, :, :],
                                    [stn], ["S0BD"])
                            for p in range(2):
                                pso, pon = newps()
                                for h in range(2):
                                    mm(pso[:, 0:64], VP[r0:r1, e, p, h, :], AT[r0:r1, e, p * 2 + h, :], h == 0, False, [vpn, atn], [pon])
                                for j in range(8):
                                    mm(pso[:, 8 * j:8 * j + 8], S0BD[:, j, p, :], QT[:, p, lcol + 8 * j:lcol + 8 * j + 8], False, j == 7,
                                       ["S0BD", "QT.%d.%d" % (p, lcol // 256)], [pon])
                                act(O[:, p, lcol:lcol + 64], pso[:, 0:64], AF.Copy, [pon], ["O.%d" % gt, "OQ.0.%d" % gt, "OQ.1.%d" % gt])
                                for h in range(2):
                                    tt(VBLK[r0:r1, h, :, :], VP[r0:r1, e, p, h, h * 64:(h + 1) * 64].unsqueeze(1).broadcast_to([64, 8, 64]),
                                       CST[r0:r1, C_MJ:C_MJ + 8].unsqueeze(2).broadcast_to([64, 8, 64]), ALU.mult, [vpn, "CST"], ["VBLK"])
                                psu, pun = newps()
                                for h in range(2):
                                    mm(psu[:, 0:512], KP[r0:r1, e, p, h, :], VBLK[r0:r1, h].rearrange("p a b -> p (a b)"), h == 0, h == 1,
                                       [kpn, "VBLK"], [pun])
                                tt(STt[:, :, p, :], psu[:, 0:512].rearrange("p (a b) -> p a b", a=8), STt[:, :, p, :], ALU.add,
                                   [pun, stn, "S0BD"], [stn])
                                tt(STt[:, :, p, :], STt[:, :, p, :], EBEs[:, p, 8 * sc:8 * sc + 8].unsqueeze(2).broadcast_to([128, 8, 64]),
                                   ALU.mult, [stn, "EBEs"], [stn])
                            dma_out(hgrn_sT[:, l, 8 * sc:8 * sc + 8, :, :], STt, [stn], "hs%d" % sc, defer=1)

                def tix(ch):
                    return ch // 4

                p1_flush_tile(0)
                stageA(0)
                for ch in range(nch):
                    if ch + 1 < nch:
                        p1_flush_tile(tix(ch + 1))
                        stageA(ch + 1)
                    stageB(ch)
                    p1_emit_one()
                while p1_emit_one():
                    pass
                st["held"] = set()
                phase()
                S.barrier()
                SG, _ = carve(0, [2, 512], F32)
                off = off_dead
                OSQ2, off = carve(off, [2, 512], BF16)
                N12, off = carve(off, [2, 512], F32)
                N2a, _ = carve(4608, [512], F32)
                assert off <= 34304
                MO, _ = carve(34304, [8, 512], F32)
                ug = [win(14), win(15)]
                phase()
                wo_units = [wget(w_out_u[l, m]) for m in range(8)]

                def c3_tile(tl):
                    (gt, c0, lc0, w) = tl
                    for p in range(2):
                        proj(ug[p][0], ug[p][1], [tl], lambda gt, c0, lc0, w, ps, psn: act(SG[:, p, 0:w], ps[:, 0:w], AF.Silu, [psn],
                                                                                         ["SG.%d" % p]))
                    for p in range(2):
                        OSQ, N1 = OSQ2[:, p], N12[:, p]
                        on, n1n = "OSQ%d" % p, "N1%d" % p
                        act(OSQ[:, 0:w], O[:, p, lc0:lc0 + w], AF.Square, ["O.%d" % gt], [on])
                        ps, psn = newps()
                        mm(ps[:, 0:w], OBB[:], OSQ[:, 0:w], True, True, ["OBB", on], [psn])
                        act(N1[:, 0:w], ps[:, 0:w], AF.Ln, [psn, "EPSC"], [n1n], scale=1.0 / 64.0, bias=EPSC[:, 0:1])
                        act(N1[:, 0:w], N1[:, 0:w], AF.Exp, [n1n], [n1n], scale=-0.5)
                        stt(N2a[:, 0:w], O[:, p, lc0:lc0 + w], pc(P_HN + l * 2 + p), N1[:, 0:w], ALU.mult, ALU.mult,
                            ["O.%d" % gt, n1n, "PAR"], ["N2a"])
                        tt(MIX[:, 4 + p, lc0:lc0 + w], N2a[:, 0:w], SG[:, p, 0:w], ALU.mult, ["N2a", "SG.%d" % p],
                           ["MIX.%d.%d" % (4 + p, gt)])

                def wout_tile(tl):
                    (gt, c0, lc0, w) = tl
                    korder = (0, 1, 2, 3, 6, 7, 4, 5)
                    for m in range(8):
                        u_o, n_o = wo_units[m]
                        ps, psn = newps()
                        for i_, k in enumerate(korder):
                            kn = "MIX.%d.%d" % (k, gt) if k in (4, 5) else "MIX.%d" % k
                            mm(ps[:, 0:w], u_o[:, k * 128:(k + 1) * 128], MIX[:, k, lc0:lc0 + w], i_ == 0, i_ == 7, [n_o, kn], [psn])
                        act(MO[:, m, 0:w], ps[:, 0:w], AF.Copy, [psn], ["MO.%d" % m])
                        act(SQ[:, m, 0:w], ps[:, 0:w], AF.Square, [psn], ["SQ.%d" % m])

                def resid_tile(tl):
                    (gt, c0, lc0, w) = tl
                    rms_stats([MO[:, k, 0:w] for k in range(8)], ["MO.%d" % k for k in range(8)], lc0, w, str(gt), presq=True)
                    for k in range(8):
                        stt(TMPB[:, 0:w], MO[:, k, 0:w], pc(P_G2 + l * 8 + k), RSTD[:, lc0:lc0 + w], ALU.mult, ALU.mult,
                            ["MO.%d" % k, "RSTD.%d" % (lc0 // 512), "PAR"], ["TMPB"])
                        tt(X[:, k, c0:c0 + w], X[:, k, c0:c0 + w], TMPB[:, 0:w], ALU.add, [xb(k, gt), "TMPB"], [xb(k, gt)])

                nt_ = len(tiles)
                c3_tile(tiles[0])
                for i in range(nt_):
                    wout_tile(tiles[i])
                    if i + 1 < nt_:
                        c3_tile(tiles[i + 1])
                    resid_tile(tiles[i])
                    make_h([tiles[i]], P_G3 + l * 8)

                S.barrier()
                ACTB, _ = carve(0, [22, 1152], BF16)
                ffc = [0]

                def gate_up(j, tl, u_g, n_g, u_u, n_u):
                    (gt, c0, lc0, w) = tl
                    tmp = TMPB if (ffc[0] % 2 == 0) else TMPC
                    tn = "TMPB" if (ffc[0] % 2 == 0) else "TMPC"
                    ffc[0] += 1
                    proj(u_g, n_g, [tl], lambda gt, c0, lc0, w, ps, psn: act(tmp[:, 0:w], ps[:, 0:w], AF.Silu, [psn], [tn]))
                    proj(u_u, n_u, [tl], lambda gt, c0, lc0, w, ps, psn: tt(ACTB[:, j, lc0:lc0 + w], ps[:, 0:w], tmp[:, 0:w], ALU.mult,
                                                                             [psn, tn], ["ACTB.%d.%d" % (j, gt)]))

                J0 = 4
                first = [(wget(w_gate_u[l, j]), wget(w_up_u[l, j])) for j in range(J0)]
                for tl in tiles:
                    for j in range(J0):
                        (u_g, n_g), (u_u, n_u) = first[j]
                        gate_up(j, tl, u_g, n_g, u_u, n_u)
                for j in range(J0, 22):
                    u_g, n_g = wget(w_gate_u[l, j])
                    u_u, n_u = wget(w_up_u[l, j])
                    tmps = {}

                    def ev_g(gt, c0, lc0, w, ps, psn):
                        tmp, tn = (TMPB, "TMPB") if (ffc[0] % 2 == 0) else (TMPC, "TMPC")
                        ffc[0] += 1
                        tmps[gt] = (tmp, tn)
                        act(tmp[:, 0:w], ps[:, 0:w], AF.Silu, [psn], [tn])

                    def ev_u(gt, c0, lc0, w, ps, psn, j=j):
                        tmp, tn = tmps[gt]
                        tt(ACTB[:, j, lc0:lc0 + w], ps[:, 0:w], tmp[:, 0:w], ALU.mult, [psn, tn], ["ACTB.%d.%d" % (j, gt)])

                    for g0 in range(0, len(tiles), 2):
                        grp = tiles[g0:g0 + 2]
                        proj_k(u_g, n_g, grp, ev_g)
                        proj_k(u_u, n_u, grp, ev_u)
                phase()
                def down_pieces(m):
                    return [wget(w_down_u[l, m, :, k0 * 128:(k0 + nk) * 128], ncols=nk * 128) for (k0, nk) in ((0, 8), (8, 8), (16, 6))]

                def down_group(m, tl, pieces):
                    (gt, c0, lc0, w) = tl
                    ps, psn = newps()
                    for j in range(22):
                        pa, pn_ = pieces[j // 8]
                        jj = j % 8
                        mm(ps[:, 0:w], pa[:, jj * 128:(jj + 1) * 128], ACTB[:, j, lc0:lc0 + w], j == 0, j == 21,
                           [pn_, "ACTB.%d.%d" % (j, gt)], [psn])
                    act(FF[:, m, lc0:lc0 + w], ps[:, 0:w], AF.Copy, [psn], ["FF.%d.%d" % (m, gt)])

                for m in range(5):
                    pcs = down_pieces(m)
                    for tl in tiles:
                        down_group(m, tl, pcs)
                last = {m: down_pieces(m) for m in (5, 6, 7)}
                for m in (5, 6, 7):
                    down_group(m, tiles[0], last[m])
                for i in range(1, len(tiles)):
                    down_group(5, tiles[i], last[5])
                    add_residual(FF, "FF", [tiles[i - 1]], P_G4 + l * 8)
                    down_group(6, tiles[i], last[6])
                    down_group(7, tiles[i], last[7])
                add_residual(FF, "FF", [tiles[-1]], P_G4 + l * 8)
                if l == 1:
                    for (gt, c0, lc0, w) in tiles:
                        for k in range(8):
                            dma_out(yT[:, k, c0:c0 + w], X[:, k, c0:c0 + w], [xb(k, gt)], "y")
        try:
            body()
        except _StopBuild:
            S.barrier()
            for k in range(8):
                dma_out(yT[:, k, :], X[:, k, :], [xb(k, t) for t in range(5)], "y")
        S.emit(final_wait_chans=[c for c in S.chan_cnt if c.startswith("o_")])
    return nc


def _consts():
    c = np.zeros((128, NCST), np.float32)
    c[:, C_ID:C_ID + 128] = np.eye(128, dtype=np.float32)
    m = np.ones(512, np.float32)
    m[0::64] = 0.0
    c[:, C_M0P:C_M0P + 512] = m[None]
    m = np.ones(128, np.float32)
    m[0::8] = 0.0
    c[:, C_M0S:C_M0S + 128] = m[None]
    s = np.arange(64)[:, None]
    t = np.arange(64)[None, :]
    for r_ in (0, 64):
        c[r_:r_ + 64, C_MC:C_MC + 64] = (s <= t).astype(np.float32)
        c[r_:r_ + 64, C_MS:C_MS + 64] = ((s <= t) & (s // 8 == t // 8)).astype(np.float32)
        c[r_:r_ + 64, C_MJ:C_MJ + 8] = (np.arange(64)[:, None] // 8 == np.arange(8)[None, :]).astype(np.float32)
    pp = np.arange(128)
    c[:, C_OB:C_OB + 128] = (pp[:, None] // 64 == pp[None, :] // 64).astype(np.float32)
    wins = (2, 4, 8, 16)
    for ch in range(2):
        for half in range(2):
            w = wins[ch * 2 + half]
            cnt = np.minimum(np.arange(16) + 1, w).astype(np.float32)
            c[half * 64:(half + 1) * 64, C_RC + 16 * ch:C_RC + 16 * ch + 16] = (1.0 / cnt)[None]
            c[half * 64:(half + 1) * 64, C_IW + ch] = 1.0 / w
    return c


def _fm(v):
    v = np.asarray(v, np.float32)
    lead = v.shape[:-1]
    n = v.shape[-1] // 128
    return np.moveaxis(v.reshape(lead + (n, 128)), -1, 0)


_PROG = {}


def kernel(x_prompt, x_sample, state_conv, state_pool, state_hgrn, state_conf,
           norm_mix_pre, norm_mix_post, w_in, conv_w, pool_w, pool_scale, hgrn_lb, hgrn_norm,
           conf_dw, conf_b, conf_ln_g, conf_ln_b, w_out, norm_ffn_pre, norm_ffn_post,
           w_gate, w_up, w_down):
    f = lambda a: np.ascontiguousarray(np.asarray(a, dtype=np.float32))
    x_prompt, x_sample = f(x_prompt), f(x_sample)
    par = np.zeros((128, NPAR), np.float32)
    for col, arr in ((P_G1, norm_mix_pre), (P_G2, norm_mix_post), (P_G3, norm_ffn_pre), (P_G4, norm_ffn_post)):
        par[:, col:col + 16] = _fm(arr).reshape(128, 16)
    par[:, P_CW:P_CW + 12] = np.transpose(_fm(conv_w), (0, 1, 3, 2)).reshape(128, 12)
    for col, arr in ((P_PS, pool_scale), (P_LB, hgrn_lb), (P_HN, hgrn_norm), (P_CB, conf_b), (P_LG, conf_ln_g), (P_LBI, conf_ln_b)):
        par[:, col:col + 4] = _fm(arr).reshape(128, 4)
    par[:, P_DW:P_DW + 124] = np.transpose(_fm(conf_dw), (0, 1, 3, 2)).reshape(128, 124)
    cst = _consts()
    pw = f(pool_w)
    pwbd = np.zeros((128, 2, 2, 128), np.float32)
    for l in range(2):
        for g in range(4):
            c, hh = g // 2, g % 2
            pwbd[hh * 64:(hh + 1) * 64, l, c, hh * 64:(hh + 1) * 64] = pw[l, g]

    def units(w, nu):
        w = f(w)
        L, KK, NN = w.shape
        K = KK // 128
        return np.ascontiguousarray(w.reshape(L, K, 128, nu, 128).transpose(0, 3, 2, 1, 4).reshape(L, nu, 128, K * 128))

    w_in_u = units(w_in, 20)
    w_out_u = units(w_out, 8)
    w_gate_u = units(w_gate, 22)
    w_up_u = units(w_up, 22)
    w_down_u = units(w_down, 8)
    sc_, sp_, sh_, sf_ = f(state_conv), f(state_pool), f(state_hgrn), f(state_conf)

    in_maps = []
    for b in range(8):
        tok = np.concatenate([x_prompt[b], x_sample[16 * b:16 * b + 16].reshape(128, 1024)], axis=0)
        xT = np.ascontiguousarray(tok.reshape(NT, 8, 128).transpose(2, 1, 0))

        def st(a, R):
            return np.ascontiguousarray(a[16 * b:16 * b + 16].reshape(16, 2, R, 2, 128).transpose(4, 1, 3, 0, 2))

        hg = sh_[16 * b:16 * b + 16].reshape(16, 2, 2, 2, 64, 64).transpose(3, 4, 1, 0, 2, 5).reshape(128, 2, 16, 2, 64)
        in_maps.append({
            "xT": xT, "sconv": st(sc_, 2), "spool": st(sp_, 15), "sconf": st(sf_, 30), "shgrn": np.ascontiguousarray(hg),
            "w_in_u": w_in_u, "w_out_u": w_out_u, "w_gate_u": w_gate_u, "w_up_u": w_up_u, "w_down_u": w_down_u,
            "par": par, "cst": cst, "pwbd": pwbd,
        })
    if "nc" not in _PROG:
        import os
        stop = os.environ.get("MK_STOP")
        _PROG["nc"] = build_program(None if stop is None else int(stop))
    res = run_bass_kernel_spmd(_PROG["nc"], in_maps, core_ids=list(range(8)))
    R = res.results

    y_prompt = np.empty((8, 2048, 1024), np.float32)
    y_sample = np.empty((128, 8, 1024), np.float32)
    conv_p = np.empty((8, 2, 2, 256), np.float32)
    pool_p = np.empty((8, 2, 15, 256), np.float32)
    hgrn_p = np.empty((8, 2, 4, 64, 64), np.float32)
    conf_p = np.empty((8, 2, 30, 256), np.float32)
    conv_s = np.empty((128, 2, 2, 256), np.float32)
    pool_s = np.empty((128, 2, 15, 256), np.float32)
    hgrn_s = np.empty((128, 2, 4, 64, 64), np.float32)
    conf_s = np.empty((128, 2, 30, 256), np.float32)
    for b in range(8):
        r = R[b]
        tok = np.asarray(r["yT"]).transpose(2, 1, 0).reshape(NT, 1024)
        y_prompt[b] = tok[:2048]
        y_sample[16 * b:16 * b + 16] = tok[2048:].reshape(16, 8, 1024)
        for dst, key in ((conv_p, "conv_pT"), (pool_p, "pool_pT"), (conf_p, "conf_pT")):
            a = np.asarray(r[key])
            dst[b] = a.transpose(1, 3, 2, 0).reshape(2, a.shape[3], 256)
        hp = np.asarray(r["hgrn_pT"]).reshape(2, 64, 2, 2, 64)
        hgrn_p[b] = hp.transpose(2, 3, 0, 1, 4).reshape(2, 4, 64, 64)
        for dst, key in ((conv_s, "conv_sT"), (pool_s, "pool_sT"), (conf_s, "conf_sT")):
            a = np.asarray(r[key])
            dst[16 * b:16 * b + 16] = a.transpose(3, 1, 4, 2, 0).reshape(16, 2, a.shape[4], 256)
        hs = np.asarray(r["hgrn_sT"]).reshape(2, 64, 2, 16, 2, 64)
        hgrn_s[16 * b:16 * b + 16] = hs.transpose(3, 2, 4, 0, 1, 5).reshape(16, 2, 4, 64, 64)
    return (y_prompt, y_sample, conv_p, pool_p, hgrn_p, conf_p, conv_s, pool_s, hgrn_s, conf_s)
```

```python
import numpy as np
from contextlib import ExitStack
import concourse.bass as bass
import concourse.mybir as mybir
from concourse.bass_utils import run_bass_kernel_spmd

F32 = mybir.dt.float32
BF16 = mybir.dt.bfloat16
AF = mybir.ActivationFunctionType
ALU = mybir.AluOpType

NT = 2176
EPS = 1e-6
F_MIN = 1e-20
ENGS = ("pe", "act", "dve", "pool", "sp")


class Buf:
    __slots__ = ("name", "w", "r")

    def __init__(self, name):
        self.name = name
        self.w = None
        self.r = []


class Op:
    __slots__ = ("eng", "fn", "waits", "signal", "idx", "is_dma", "chan", "val", "known", "is_nop")


class Sched:
    def __init__(self, nc, self_sync=("act", "dve", "pool")):
        self.nc = nc
        self.prog = {e: [] for e in ENGS}
        self.known = {e: {} for e in ENGS}
        self.self_sync = set(self_sync)
        self.chan_cnt = {}
        self.bufs = {}
        self.last = {}
        self.dma_pending = []

    def buf(self, name):
        b = self.bufs.get(name)
        if b is None:
            b = Buf(name)
            self.bufs[name] = b
        return b

    def _norm(self, lst):
        out = []
        for x in lst:
            if x is None:
                continue
            if isinstance(x, str):
                out.append(self.buf(x))
            else:
                out.extend(self._norm(x))
        return out

    def _add(self, eng, fn, reads, writes, is_dma=False, chan=None, extra=(), defer=0):
        reads = self._norm(reads)
        writes = self._norm(writes)
        op = Op()
        op.eng = eng
        op.fn = fn
        op.is_dma = is_dma
        op.chan = chan
        op.signal = False
        op.is_nop = False
        op.idx = len(self.prog[eng])
        op.waits = []
        kn = self.known[eng]
        toks = list(extra)
        for b in reads:
            if b.w is not None:
                toks.append(b.w)
        for b in writes:
            if b.w is not None:
                toks.append(b.w)
            toks.extend(b.r)
        best = {}
        for t in toks:
            if t[0] not in best or best[t[0]][1] < t[1]:
                best[t[0]] = t
        for t in best.values():
            src, v, top = t
            if src == eng and not top.is_dma and eng not in self.self_sync:
                continue
            if kn.get(src, -1) >= v:
                continue
            kn[src] = v
            op.waits.append(t)
            top.signal = True
            if top.known is not None:
                for s2, v2 in top.known.items():
                    if kn.get(s2, -1) < v2:
                        kn[s2] = v2
        if is_dma:
            n = self.chan_cnt.get(chan, 0) + 1
            self.chan_cnt[chan] = n
            op.val = 16 * n
            tok = ("dma:" + chan, op.val, op)
            op.known = None
            if defer >= 0:
                self.dma_pending.append([tok, defer])
        else:
            op.val = op.idx
            tok = (eng, op.idx, op)
            op.known = dict(kn)
            self.last[eng] = tok
        for b in reads:
            b.r.append(tok)
        for b in writes:
            b.w = tok
            b.r = []
        self.prog[eng].append(op)
        return op

    def op(self, eng, fn, reads=(), writes=()):
        return self._add(eng, fn, reads, writes)

    def dma(self, eng, fn, reads=(), writes=(), chan="d", defer=0):
        return self._add(eng, fn, reads, writes, is_dma=True, chan=chan, defer=defer)

    def barrier(self, engs=("act", "dve", "sp")):
        nc = self.nc
        toks = [self.last[e] for e in ENGS if e in self.last] + [t for t, d in self.dma_pending if d == 0]
        self.dma_pending = [[t, d - 1] for t, d in self.dma_pending if d > 0]
        hand = {"pe": nc.tensor, "act": nc.scalar, "dve": nc.vector, "sp": nc.sync}
        saved = dict(self.last)
        for e in engs:
            o = self._add(e, (lambda h=hand[e]: h.nop()), (), (), extra=toks)
            o.is_nop = True
        self.last = saved

    def emit(self, final_wait_chans=()):
        nc = self.nc
        with ExitStack() as es:
            esem = {e: es.enter_context(nc.semaphore("s_" + e)) for e in ENGS}
            csem = {c: es.enter_context(nc.semaphore("c_" + c)) for c in self.chan_cnt}
            sigcnt = {}
            for e in ENGS:
                c = 0
                for op in self.prog[e]:
                    if not op.is_dma and op.signal:
                        c += 1
                        sigcnt[(e, op.idx)] = c
            block = es.enter_context(nc.Block())
            hand = {"pe": nc.tensor, "act": nc.scalar, "dve": nc.vector, "pool": nc.gpsimd, "sp": nc.sync}

            def run(e):
                h = hand[e]
                for op in self.prog[e]:
                    for (src, v, top) in op.waits:
                        if top.is_dma:
                            h.wait_ge(csem[top.chan], v)
                        else:
                            h.wait_ge(esem[src], sigcnt[(src, v)])
                    ins = op.fn()
                    if op.is_dma:
                        ins.then_inc(csem[op.chan], 16)
                    elif op.signal:
                        ins.then_inc(esem[e], 1)
                if e == "sp":
                    for c in final_wait_chans:
                        h.wait_ge(csem[c], 16 * self.chan_cnt[c])

            @block.tensor
            def _(eng):
                run("pe")

            @block.scalar
            def _(eng):
                run("act")

            @block.vector
            def _(eng):
                run("dve")

            @block.gpsimd
            def _(eng):
                run("pool")

            @block.sync
            def _(eng):
                run("sp")


C_ID = 0
C_M0P = 128
C_M0S = 640
C_MC = 768
C_MS = 832
C_MJ = 896
C_OB = 904
C_RC = 1032
C_IW = 1064
NCST = 1066

P_G1, P_G2, P_G3, P_G4 = 0, 16, 32, 48
P_CW = 64
P_PS = 76
P_LB = 80
P_HN = 84
P_CB = 88
P_LG = 92
P_LBI = 96
P_DW = 100
NPAR = 224

NSLOT = 10
BIG32 = 12672


class _StopBuild(Exception):
    pass


MARKS = []
MARKS_DVE = []
S_holder = [None]


def build_program(stop=None):
    nc = bass.Bass("TRN2", target_bir_lowering=False)
    phase_ctr = [0]

    def phase():
        MARKS.append(sum(1 for o in S_holder[0].prog["pe"] if not getattr(o, "is_nop", False)))
        MARKS_DVE.append(sum(1 for o in S_holder[0].prog["dve"] if not getattr(o, "is_nop", False)))
        phase_ctr[0] += 1
        if stop is not None and phase_ctr[0] > stop:
            raise _StopBuild()

    def din(name, shape):
        return nc.dram_tensor(name, shape, F32, kind="ExternalInput").ap()

    def dout(name, shape):
        return nc.dram_tensor(name, shape, F32, kind="ExternalOutput").ap()

    xT = din("xT", [128, 8, NT])
    sconv = din("sconv", [128, 2, 2, 16, 2])
    spool = din("spool", [128, 2, 2, 16, 15])
    sconf = din("sconf", [128, 2, 2, 16, 30])
    shgrn = din("shgrn", [128, 2, 16, 2, 64])
    w_in_u = din("w_in_u", [2, 20, 128, 1024])
    w_out_u = din("w_out_u", [2, 8, 128, 1024])
    w_gate_u = din("w_gate_u", [2, 22, 128, 1024])
    w_up_u = din("w_up_u", [2, 22, 128, 1024])
    w_down_u = din("w_down_u", [2, 8, 128, 2816])
    par_d = din("par", [128, NPAR])
    cst_d = din("cst", [128, NCST])
    pwbd_d = din("pwbd", [128, 2, 2, 128])

    yT = dout("yT", [128, 8, NT])
    conv_pT = dout("conv_pT", [128, 2, 2, 2])
    pool_pT = dout("pool_pT", [128, 2, 2, 15])
    conf_pT = dout("conf_pT", [128, 2, 2, 30])
    hgrn_pT = dout("hgrn_pT", [128, 2, 2, 64])
    conv_sT = dout("conv_sT", [128, 2, 2, 16, 2])
    pool_sT = dout("pool_sT", [128, 2, 2, 16, 15])
    conf_sT = dout("conf_sT", [128, 2, 2, 16, 30])
    hgrn_sT = dout("hgrn_sT", [128, 2, 16, 2, 64])

    with ExitStack() as es:
        def sb(name, shape, dt):
            return es.enter_context(nc.sbuf_tensor(name, shape, dt))

        X = sb("X", [128, 8, NT], F32)
        HM = sb("HM", [128, 2, 8, 1152], BF16)
        BIG = sb("BIG", [128, BIG32], F32)
        SQ = sb("SQ", [128, 8, 512], BF16)
        RSTD = sb("RSTD", [128, 1152], F32)
        TMPA = sb("TMPA", [128, 512], F32)
        TMPB = sb("TMPB", [128, 512], F32)
        TMPC = sb("TMPC", [128, 512], F32)
        CST = sb("CST", [128, NCST], F32)
        PAR = sb("PAR", [128, NPAR], F32)
        PWB = sb("PWB", [128, 2, 2, 128], BF16)
        IDB = sb("IDB", [128, 128], BF16)
        ONEB = sb("ONEB", [128, 128], BF16)
        OBB = sb("OBB", [128, 128], BF16)
        EPSC = sb("EPSC", [128, 1], F32)
        LOW = sb("LOW", [128, 2, 2], F32)
        OML = sb("OML", [128, 2, 2], F32)
        LBT = sb("LBT", [128, 8], F32)
        CARA = sb("CARA", [128, 2, 2], F32)
        CARB = sb("CARB", [128, 2, 15], F32)
        CARD = sb("CARD", [128, 2, 30], BF16)
        SS = sb("SS", [128, 2, 64], F32)
        SBD = sb("SBD", [128, 2, 128], BF16)
        WR = sb("WR", [128, NSLOT, 1024], BF16)
        STGA = sb("STGA", [128, 2, 16, 2], F32)
        STGB = sb("STGB", [128, 2, 16, 15], F32)
        STGD = sb("STGD", [128, 2, 16, 30], F32)
        OPA = sb("OPA", [128, 2, 2], F32)
        OPB = sb("OPB", [128, 2, 15], F32)
        OPD = sb("OPD", [128, 2, 30], F32)
        PS_all = es.enter_context(nc.psum_tensor("psall", [128, 7, 512], F32))
        PS = [PS_all[:, i, :] for i in range(7)]
        PSB = es.enter_context(nc.psum_tensor("psb", [128, 1024], BF16))

        H = HM[:, 0]
        MIX = HM[:, 1]
        FF = HM[:].rearrange("p a k n -> p (a k n)").bitcast(F32).rearrange("p (k n) -> p k n", k=8)

        S = Sched(nc)
        S_holder[0] = S
        st = {"ps": 0, "w": 0, "held": set()}

        def mm(out, lhsT, rhs, start, stop, r, w):
            S.op("pe", lambda: nc.tensor.matmul(out, lhsT=lhsT, rhs=rhs, start=start, stop=stop), r, w)

        def tr(out, in_, ident, r, w):
            S.op("pe", lambda: nc.tensor.transpose(out, in_, ident), r, w)

        def act(out, in_, func, r, w, scale=None, bias=None):
            kw = {}
            if scale is not None:
                kw["scale"] = scale
            if bias is not None:
                kw["bias"] = bias
            S.op("act", lambda: nc.scalar.activation(out=out, in_=in_, func=func, **kw), r, w)

        def tt(out, in0, in1, op, r, w):
            S.op("dve", lambda: nc.vector.tensor_tensor(out=out, in0=in0, in1=in1, op=op), r, w)

        def ts(out, in0, s1, s2, op0, op1, r, w):
            if op1 is None:
                S.op("dve", lambda: nc.vector.tensor_scalar(out=out, in0=in0, scalar1=s1, scalar2=None, op0=op0), r, w)
            else:
                S.op("dve", lambda: nc.vector.tensor_scalar(out=out, in0=in0, scalar1=s1, scalar2=s2, op0=op0, op1=op1), r, w)

        def stt(out, in0, scalar, in1, op0, op1, r, w):
            S.op("dve", lambda: nc.vector.scalar_tensor_tensor(out=out, in0=in0, scalar=scalar, in1=in1, op0=op0, op1=op1), r, w)

        def vcp(out, in_, r, w):
            S.op("dve", lambda: nc.vector.tensor_copy(out=out, in_=in_), r, w)

        def vms(ap, val, w):
            S.op("dve", lambda: nc.vector.memset(ap, val), (), w)

        def vrec(out, in_, r, w):
            S.op("dve", lambda: nc.vector.reciprocal(out=out, in_=in_), r, w)

        def dma_in(out, in_, w, chan, defer=0):
            S.dma("sp", lambda: nc.sync.dma_start(out=out, in_=in_), (), w, chan=chan, defer=defer)

        def dma_out(out, in_, r, chan, defer=0):
            S.dma("sp", lambda: nc.sync.dma_start(out=out, in_=in_), r, (), chan="o_" + chan, defer=defer)

        def newps():
            while True:
                i = st["ps"] % 7
                st["ps"] += 1
                if i not in st["held"]:
                    return PS[i], "ps%d" % i

        def wget(src, ncols=1024):
            s = st["w"] % NSLOT
            st["w"] += 1
            name = "wr%d" % s
            dst = WR[:, s, 0:ncols]
            S.dma("pool", lambda: nc.gpsimd.dma_start(out=dst, in_=src), (), [name], chan=name)
            return WR[:, s, :], name

        def carve(off, shape, dt):
            n = 1
            for s_ in shape:
                n *= s_
            nb = n * (4 if dt == F32 else 2)
            assert off % 4 == 0
            n32 = (nb + 3) // 4
            assert off // 4 + n32 <= BIG32, (off, shape)
            ap = BIG[:, off // 4: off // 4 + n32]
            if dt == BF16:
                ap = ap.bitcast(BF16)[:, 0:n]
            if len(shape) == 2:
                ap = ap.rearrange("p (a b) -> p a b", a=shape[0])
            elif len(shape) == 3:
                ap = ap.rearrange("p (a b c) -> p a b c", a=shape[0], b=shape[1])
            elif len(shape) == 4:
                ap = ap.rearrange("p (a b c d) -> p a b c d", a=shape[0], b=shape[1], c=shape[2])
            return ap, off + 4 * n32

        def pc(col):
            return PAR[:, col:col + 1]

        dma_in(CST[:], cst_d, ["CST"], "c0")
        dma_in(PAR[:], par_d, ["PAR"], "c1")
        dma_in(TMPB[:], pwbd_d.rearrange("p a b c -> p (a b c)"), ["TMPB"], "c2")
        for k in range(8):
            dma_in(X[:, k, 0:1024], xT[:, k, 0:1024], ["X.%d.%d" % (k, t) for t in (0, 1)], "x%d" % k)
        for k in range(8):
            dma_in(X[:, k, 1024:NT], xT[:, k, 1024:NT], ["X.%d.%d" % (k, t) for t in (2, 3, 4)], "xb%d" % k)
        for c in range(2):
            dma_in(STGA[:, c], sconv[:, 0, c], ["STGA.%d" % c], "sa%d" % c, defer=-1)
            dma_in(STGB[:, c], spool[:, 0, c], ["STGB.%d" % c], "sb%d" % c, defer=-1)
            dma_in(STGD[:, c], sconf[:, 0, c], ["STGD.%d" % c], "sd%d" % c, defer=-1)
        vcp(PWB[:].rearrange("p a b c -> p (a b c)"), TMPB[:], ["TMPB"], ["PWB"])
        vcp(IDB[:], CST[:, C_ID:C_ID + 128], ["CST"], ["IDB"])
        vcp(OBB[:], CST[:, C_OB:C_OB + 128], ["CST"], ["OBB"])
        vms(ONEB[:], 1.0, ["ONEB"])
        vms(EPSC[:], EPS, ["EPSC"])
        act(LBT[:, 0:4], PAR[:, P_LB:P_LB + 4], AF.Exp, ["PAR"], ["LBT"])
        tt(LBT[:, 4:6], LBT[:, 0:2], LBT[:, 2:4], ALU.add, ["LBT"], ["LBT"])
        vrec(LBT[:, 6:8], LBT[:, 4:6], ["LBT"], ["LBT"])
        vms(LOW[:, 0, :], 0.0, ["LOW"])
        tt(LOW[:, 1, :], LBT[:, 2:4], LBT[:, 6:8], ALU.mult, ["LBT", "LOW"], ["LOW"])
        ts(OML[:].rearrange("p a b -> p (a b)"), LOW[:].rearrange("p a b -> p (a b)"), -1.0, 1.0, ALU.mult, ALU.add,
           ["LOW"], ["OML"])

        blocks = [
            [(0, 0, 0, 512), (1, 512, 512, 512)],
            [(2, 1024, 0, 512), (3, 1536, 512, 512), (4, 2048, 1024, 128)],
        ]

        def xb(k, gt):
            return "X.%d.%d" % (k, gt)

        def rms_stats(srcs, srcbufs, lc0, w, tag, presq=False):
            for k in range(8):
                if not presq:
                    act(SQ[:, k, 0:w], srcs[k], AF.Square, [srcbufs[k]], ["SQ.%d" % k])
            ps, psn = newps()
            for k in range(8):
                mm(ps[:, 0:w], ONEB[:], SQ[:, k, 0:w], k == 0, k == 7, ["ONEB", "SQ.%d" % k], [psn])
            act(TMPA[:, 0:w], ps[:, 0:w], AF.Ln, [psn, "EPSC"], ["TMPA"], scale=1.0 / 1024.0, bias=EPSC[:, 0:1])
            act(RSTD[:, lc0:lc0 + w], TMPA[:, 0:w], AF.Exp, ["TMPA"], ["RSTD.%d" % (lc0 // 512)], scale=-0.5)

        def make_h(tiles, gcol):
            for (gt, c0, lc0, w) in tiles:
                rms_stats([X[:, k, c0:c0 + w] for k in range(8)], [xb(k, gt) for k in range(8)], lc0, w, str(gt))
                for k in range(8):
                    stt(H[:, k, lc0:lc0 + w], X[:, k, c0:c0 + w], pc(gcol + k), RSTD[:, lc0:lc0 + w], ALU.mult, ALU.mult,
                        [xb(k, gt), "RSTD.%d" % (lc0 // 512), "PAR"], ["H.%d.%d" % (k, gt), "HMA.%d" % (k // 2)])

        def add_residual(src, srcname, tiles, gcol):
            for (gt, c0, lc0, w) in tiles:
                rms_stats([src[:, k, lc0:lc0 + w] for k in range(8)], ["%s.%d.%d" % (srcname, k, gt) for k in range(8)],
                          lc0, w, str(gt))
                for k in range(8):
                    stt(TMPB[:, 0:w], src[:, k, lc0:lc0 + w], pc(gcol + k), RSTD[:, lc0:lc0 + w], ALU.mult, ALU.mult,
                        ["%s.%d.%d" % (srcname, k, gt), "RSTD.%d" % (lc0 // 512), "PAR"] + (["HMA.%d" % k] if srcname == "FF" else []),
                        ["TMPB"])
                    tt(X[:, k, c0:c0 + w], X[:, k, c0:c0 + w], TMPB[:, 0:w], ALU.add, [xb(k, gt), "TMPB"], [xb(k, gt)])

        def proj_k(unit, uname, tiles_, evac):
            banks = [newps() for _ in tiles_]
            for k in range(8):
                for (gt, c0, lc0, w), (ps, psn) in zip(tiles_, banks):
                    mm(ps[:, 0:w], unit[:, k * 128:(k + 1) * 128], H[:, k, lc0:lc0 + w], k == 0, k == 7,
                       [uname, "H.%d.%d" % (k, gt)], [psn])
            for (gt, c0, lc0, w), (ps, psn) in zip(tiles_, banks):
                evac(gt, c0, lc0, w, ps, psn)

        def proj(unit, uname, tiles, evac):
            for (gt, c0, lc0, w) in tiles:
                ps, psn = newps()
                for k in range(8):
                    mm(ps[:, 0:w], unit[:, k * 128:(k + 1) * 128], H[:, k, lc0:lc0 + w], k == 0, k == 7,
                       [uname, "H.%d.%d" % (k, gt)], [psn])
                evac(gt, c0, lc0, w, ps, psn)

        def body():
          for l in range(2):
            for b in range(2):
                tiles = blocks[b]
                TB = 1152 if b == 1 else 1024
                ptiles = [t for t in tiles if t[0] != 4]
                has_s = (b == 1)
                phase()
                make_h(tiles, P_G1 + l * 8)

                def win(u):
                    return wget(w_in_u[l, u])

                phase()
                S.barrier()
                off = 0
                UAp, off = carve(off, [2, 1026], F32)
                UAs, off = carve(off, [2, 16, 10], F32)
                ACC, off = carve(off, [2, 1152], F32)
                ACt, off = carve(off, [1152], F32)
                PBp, off = carve(off, [2, 1040], F32)
                PBs, off = carve(off, [2, 16, 23], F32)
                T1, off = carve(off, [1040], F32)
                T2, off = carve(off, [1040], F32)
                T1s, off = carve(off, [16, 23], F32)
                T2s, off = carve(off, [16, 23], F32)
                T16, off = carve(off, [16], F32)
                PL, off = carve(off, [2, 1152], BF16)

                def A1(c):
                    if b == 0:
                        vms(UAp[:, c, 0:2], 0.0, ["UAp.%d" % c])
                    else:
                        vcp(UAp[:, c, 0:2], CARA[:, c, :], ["CARA"], ["UAp.%d" % c])
                        vcp(UAs[:, c, :, 0:2], STGA[:, c], ["STGA.%d" % c], ["UAs.%d" % c])
                    u_c, n_c = win(2 + c)
                    u_u, n_u = win(4 + c)

                    def ev_c(gt, c0, lc0, w, ps, psn):
                        act(ACt[:, lc0:lc0 + w], ps[:, 0:w], AF.Copy, [psn], ["ACt.%d" % gt])

                    def ev_u(gt, c0, lc0, w, ps, psn):
                        if gt != 4:
                            tt(UAp[:, c, 2 + lc0:2 + lc0 + w], ps[:, 0:w], ACt[:, lc0:lc0 + w], ALU.mult,
                               [psn, "ACt.%d" % gt], ["UAp.%d" % c])
                        else:
                            tt(UAs[:, c, :, 2:10], ps[:, 0:128].rearrange("p (j t) -> p j t", t=8),
                               ACt[:, lc0:lc0 + 128].rearrange("p (j t) -> p j t", t=8), ALU.mult,
                               [psn, "ACt.%d" % gt], ["UAs.%d" % c])

                    proj(u_c, n_c, tiles, ev_c)
                    proj(u_u, n_u, tiles, ev_u)

                def A2(c):
                    cw = P_CW + (l * 2 + c) * 3
                    ts(ACC[:, c, 0:1024], UAp[:, c, 0:1024], pc(cw), None, ALU.mult, None, ["UAp.%d" % c, "PAR"], ["ACC.%d" % c])
                    for kk in (1, 2):
                        stt(ACC[:, c, 0:1024], UAp[:, c, kk:kk + 1024], pc(cw + kk), ACC[:, c, 0:1024], ALU.mult, ALU.add,
                            ["UAp.%d" % c, "ACC.%d" % c, "PAR"], ["ACC.%d" % c])
                    if has_s:
                        accs = ACC[:, c, 1024:1152].rearrange("p (j t) -> p j t", t=8)
                        ts(accs, UAs[:, c, :, 0:8], pc(cw), None, ALU.mult, None, ["UAs.%d" % c, "PAR"], ["ACCs.%d" % c])
                        for kk in (1, 2):
                            stt(accs, UAs[:, c, :, kk:kk + 8], pc(cw + kk), accs, ALU.mult, ALU.add,
                                ["UAs.%d" % c, "ACCs.%d" % c, "PAR"], ["ACCs.%d" % c])

                def A3(c):
                    u_b, n_b = win(0 + c)

                    def ev_b(gt, c0, lc0, w, ps, psn):
                        tt(MIX[:, c, lc0:lc0 + w], ps[:, 0:w], ACC[:, c, lc0:lc0 + w], ALU.mult,
                           [psn, "ACC.%d" % c, "ACCs.%d" % c], ["MIX.%d" % c])

                    proj(u_b, n_b, tiles, ev_b)
                    if b == 0:
                        vcp(CARA[:, c, :], UAp[:, c, 1024:1026], ["UAp.%d" % c], ["CARA"])
                    else:
                        vcp(OPA[:, c, :], UAp[:, c, 1024:1026], ["UAp.%d" % c], ["OPA.%d" % c])
                        dma_out(conv_pT[:, l, c, :], OPA[:, c, :], ["OPA.%d" % c], "cpA%d" % c, defer=-1)
                        vcp(STGA[:, c], UAs[:, c, :, 8:10], ["UAs.%d" % c], ["STGA.%d" % c])
                        dma_out(conv_sT[:, l, c], STGA[:, c], ["STGA.%d" % c], "csA%d" % c, defer=-1)
                        if l == 0:
                            dma_in(STGA[:, c], sconv[:, 1, c], ["STGA.%d" % c], "sa%d" % c, defer=-1)

                def B1(c):
                    if b == 0:
                        vms(PBp[:, c, 0:15], 0.0, ["PBp.%d" % c])
                    else:
                        vcp(PBp[:, c, 0:15], CARB[:, c, :], ["CARB"], ["PBp.%d" % c])
                        vcp(PBs[:, c, :, 0:15], STGB[:, c], ["STGB.%d" % c], ["PBs.%d" % c])
                    u_p, n_p = win(6 + c)

                    def ev_p(gt, c0, lc0, w, ps, psn):
                        if gt != 4:
                            act(PBp[:, c, 15 + lc0:15 + lc0 + w], ps[:, 0:w], AF.Copy, [psn], ["PBp.%d" % c])
                        else:
                            act(PBs[:, c, :, 15:23], ps[:, 0:128].rearrange("p (j t) -> p j t", t=8), AF.Copy,
                                [psn], ["PBs.%d" % c])

                    proj(u_p, n_p, tiles, ev_p)

                def B2(c):
                    NP_ = 1039
                    pb = "PBp.%d" % c
                    tt(T1[:, 1:NP_], PBp[:, c, 1:NP_], PBp[:, c, 0:NP_ - 1], ALU.add, [pb], ["T1"])
                    tt(T2[:, 3:NP_], T1[:, 3:NP_], T1[:, 1:NP_ - 2], ALU.add, ["T1"], ["T2"])
                    if c == 1:
                        tt(T1[:, 7:NP_], T2[:, 7:NP_], T2[:, 3:NP_ - 4], ALU.add, ["T2", "T1"], ["T1"])
                        tt(T2[:, 15:NP_], T1[:, 15:NP_], T1[:, 7:NP_ - 8], ALU.add, ["T1", "T2"], ["T2"])
                    iw = CST[:, C_IW + c:C_IW + c + 1]
                    for (lo, hi, Tw) in ((0, 64, T1), (64, 128, T2)):
                        stt(PL[lo:hi, c, 0:1024], Tw[lo:hi, 15:NP_], iw[lo:hi], PBp[lo:hi, c, 15:NP_], ALU.mult, ALU.subtract,
                            ["T1", "T2", pb, "CST"], ["PL.%d" % c])
                        if b == 0:
                            tt(T16[lo:hi, :], Tw[lo:hi, 15:31], CST[lo:hi, C_RC + 16 * c:C_RC + 16 * c + 16], ALU.mult,
                               ["T1", "T2", "CST"], ["T16"])
                            tt(PL[lo:hi, c, 0:16], T16[lo:hi, :], PBp[lo:hi, c, 15:31], ALU.subtract,
                               ["T16", pb, "PL.%d" % c], ["PL.%d" % c])
                    if has_s:
                        sbn = "PBs.%d" % c
                        tt(T1s[:, :, 1:23], PBs[:, c, :, 1:23], PBs[:, c, :, 0:22], ALU.add, [sbn], ["T1s"])
                        tt(T2s[:, :, 3:23], T1s[:, :, 3:23], T1s[:, :, 1:21], ALU.add, ["T1s"], ["T2s"])
                        if c == 1:
                            tt(T1s[:, :, 7:23], T2s[:, :, 7:23], T2s[:, :, 3:19], ALU.add, ["T2s", "T1s"], ["T1s"])
                            tt(T2s[:, :, 15:23], T1s[:, :, 15:23], T1s[:, :, 7:15], ALU.add, ["T1s", "T2s"], ["T2s"])
                        pls = PL[:, c, 1024:1152].rearrange("p (j t) -> p j t", t=8)
                        for (lo, hi, Tw) in ((0, 64, T1s), (64, 128, T2s)):
                            stt(pls[lo:hi], Tw[lo:hi, :, 15:23], iw[lo:hi], PBs[lo:hi, c, :, 15:23], ALU.mult, ALU.subtract,
                                ["T1s", "T2s", sbn, "CST"], ["PLs.%d" % c])

                def B3(c):
                    pb = "PBp.%d" % c
                    for (gt, c0, lc0, w) in tiles:
                        ps, psn = newps()
                        mm(ps[:, 0:w], PWB[:, l, c, :], PL[:, c, lc0:lc0 + w], True, True,
                           ["PWB", "PL.%d" % c, "PLs.%d" % c], [psn])
                        act(MIX[:, 2 + c, lc0:lc0 + w], ps[:, 0:w], AF.Identity, [psn, "PAR"], ["MIX.%d" % (2 + c)],
                            scale=pc(P_PS + l * 2 + c))
                    if b == 0:
                        vcp(CARB[:, c, :], PBp[:, c, 1024:1039], [pb], ["CARB"])
                    else:
                        vcp(OPB[:, c, :], PBp[:, c, 1024:1039], [pb], ["OPB.%d" % c])
                        dma_out(pool_pT[:, l, c, :], OPB[:, c, :], ["OPB.%d" % c], "cpB%d" % c, defer=-1)
                        vcp(STGB[:, c], PBs[:, c, :, 8:23], ["PBs.%d" % c], ["STGB.%d" % c])
                        dma_out(pool_sT[:, l, c], STGB[:, c], ["STGB.%d" % c], "csB%d" % c, defer=-1)
                        if l == 0:
                            dma_in(STGB[:, c], spool[:, 1, c], ["STGB.%d" % c], "sb%d" % c, defer=-1)

                A1(0); B1(0); A2(0); B2(0); A1(1); A3(0); B1(1); B3(0); A2(1); B2(1); A3(1); B3(1)
                phase()

                phase()
                S.barrier()
                off = 0
                UDp, off = carve(off, [2, 1054], BF16)
                UDT, off = carve(off, [2, 30], F32)
                UDs32, off = carve(off, [2, 16, 38], F32)
                UDs, off = carve(off, [2, 16, 38], BF16)
                DIAG, off = carve(off, [2, 31, 128], BF16)
                Z2, off = carve(off, [2, 2, 512], F32)
                ZB, off = carve(off, [4, 512], BF16)
                SD3, off = carve(off, [2, 512], F32)
                SIG = SD3[:, 0]
                D1, off = carve(off, [512], F32)
                D2, off = carve(off, [512], F32)
                for kk in range(31):
                    ts(DIAG[:, 0, kk, :], CST[:, C_ID:C_ID + 128], pc(P_DW + (l * 2 + 0) * 31 + kk), None, ALU.mult, None,
                       ["CST", "PAR"], ["DIAG.%d.%d" % (0, kk)])
                    act(DIAG[:, 1, kk, :], CST[:, C_ID:C_ID + 128], AF.Copy, ["CST", "PAR"], ["DIAG.%d.%d" % (1, kk)],
                        scale=pc(P_DW + (l * 2 + 1) * 31 + kk))
                for c in range(2):
                    if b == 0:
                        vms(UDp[:, c, 0:30], 0.0, ["UDp.%d" % c])
                    else:
                        vcp(UDp[:, c, 0:30], CARD[:, c, :], ["CARD"], ["UDp.%d" % c])
                        vcp(UDs32[:, c, :, 0:30], STGD[:, c], ["STGD.%d" % c], ["UDs32.%d" % c])
                    u_g, n_g = win(18 + c)
                    u_a, n_a = win(16 + c)
                    for tl in tiles:
                        (gt, c0, lc0, w) = tl
                        proj(u_g, n_g, [tl], lambda gt, c0, lc0, w, ps, psn: act(SIG[:, 0:w], ps[:, 0:w], AF.Sigmoid, [psn], ["SIG"]))

                        def ev_a(gt, c0, lc0, w, ps, psn, c=c):
                            if gt != 4:
                                tt(UDp[:, c, 30 + lc0:30 + lc0 + w], ps[:, 0:w], SIG[:, 0:w], ALU.mult, [psn, "SIG"], ["UDp.%d" % c])
                                if gt == 3:
                                    tt(UDT[:, c, :], ps[:, 482:512], SIG[:, 482:512], ALU.mult, [psn, "SIG"], ["UDT.%d" % c])
                            else:
                                tt(UDs32[:, c, :, 30:38], ps[:, 0:128].rearrange("p (j t) -> p j t", t=8),
                                   SIG[:, 0:128].rearrange("p (j t) -> p j t", t=8), ALU.mult, [psn, "SIG"], ["UDs32.%d" % c])

                        proj(u_a, n_a, [tl], ev_a)
                    if has_s:
                        vcp(UDs[:, c], UDs32[:, c], ["UDs32.%d" % c], ["UDs.%d" % c])
                    if b == 0:
                        vcp(CARD[:, c, :], UDp[:, c, 1024:1054], ["UDp.%d" % c], ["CARD"])
                    else:
                        vcp(OPD[:, c, :], UDT[:, c, :], ["UDT.%d" % c], ["OPD.%d" % c])
                        dma_out(conf_pT[:, l, c, :], OPD[:, c, :], ["OPD.%d" % c], "cpD%d" % c, defer=-1)
                        vcp(STGD[:, c], UDs32[:, c, :, 8:38], ["UDs32.%d" % c], ["STGD.%d" % c])
                        dma_out(conf_sT[:, l, c], STGD[:, c], ["STGD.%d" % c], "csD%d" % c, defer=-1)
                        if l == 0:
                            dma_in(STGD[:, c], sconf[:, 1, c], ["STGD.%d" % c], "sd%d" % c, defer=-1)

                def d_conv(ti):
                    (gt, c0, lc0, w) = tiles[ti]
                    zi = ti % 2
                    for c in range(2):
                        ps, psn = newps()
                        for kk in range(31):
                            if gt != 4:
                                rhs = UDp[:, c, lc0 + kk:lc0 + kk + w]
                                rb = "UDp.%d" % c
                            else:
                                rhs = UDs[:, c, :, kk:kk + 8]
                                rb = "UDs.%d" % c
                            mm(ps[:, 0:w], DIAG[:, c, kk, :], rhs, kk == 0, kk == 30, ["DIAG.%d.%d" % (c, kk), rb], [psn])
                        act(Z2[:, zi, c, 0:w], ps[:, 0:w], AF.Identity, [psn, "PAR"], ["Z.%d.%d" % (c, zi)],
                            bias=pc(P_CB + l * 2 + c))

                def d_ln(ti):
                    (gt, c0, lc0, w) = tiles[ti]
                    zi = ti % 2
                    Zt = Z2[:, zi, :, 0:w]
                    zn = ["Z.0.%d" % zi, "Z.1.%d" % zi]
                    act(ZB[:, 0:2, 0:w], Zt, AF.Copy, zn, ["ZB.0", "ZB.1"])
                    act(ZB[:, 2:4, 0:w], Zt, AF.Square, zn, ["ZB.2", "ZB.3"])
                    ps1, pn1 = newps()
                    ps2, pn2 = newps()
                    for c in range(2):
                        mm(ps1[:, 0:w], ONEB[:], ZB[:, c, 0:w], c == 0, c == 1, ["ONEB", "ZB.%d" % c], [pn1])
                    for c in range(2):
                        mm(ps2[:, 0:w], ONEB[:], ZB[:, 2 + c, 0:w], c == 0, c == 1, ["ONEB", "ZB.%d" % (2 + c)], [pn2])
                    act(D1[:, 0:w], ps1[:, 0:w], AF.Identity, [pn1], ["D1"], scale=1.0 / 256.0)
                    tt(D2[:, 0:w], D1[:, 0:w], D1[:, 0:w], ALU.mult, ["D1"], ["D2"])
                    stt(D2[:, 0:w], ps2[:, 0:w], 1.0 / 256.0, D2[:, 0:w], ALU.mult, ALU.subtract, [pn2, "D2"], ["D2"])
                    act(D2[:, 0:w], D2[:, 0:w], AF.Ln, ["D2", "EPSC"], ["D2"], bias=EPSC[:, 0:1])
                    act(D2[:, 0:w], D2[:, 0:w], AF.Exp, ["D2"], ["D2"], scale=-0.5)
                    tt(SD3[:, :, 0:w], Zt, D1[:, 0:w].unsqueeze(1).broadcast_to([128, 2, w]), ALU.subtract,
                       zn + ["D1"], ["SIG", "D3"])
                    tt(SD3[:, :, 0:w], SD3[:, :, 0:w], D2[:, 0:w].unsqueeze(1).broadcast_to([128, 2, w]), ALU.mult,
                       ["SIG", "D3", "D2"], ["SIG", "D3"])
                    for c in range(2):
                        act(MIX[:, 6 + c, lc0:lc0 + w], SD3[:, c, 0:w], AF.Silu, ["SIG", "D3", "PAR"], ["MIX.%d" % (6 + c)],
                            scale=pc(P_LG + l * 2 + c), bias=pc(P_LBI + l * 2 + c))

                nt_ = len(tiles)
                d_conv(0)
                for ti in range(nt_):
                    if ti + 1 < nt_:
                        d_conv(ti + 1)
                    d_ln(ti)

                phase()
                S.barrier()
                off = 0
                QT, off = carve(off, [2, 1152], BF16)
                KTt, off = carve(off, [2, 1152], BF16)
                O, off = carve(off, [2, 1152], F32)
                EBEp, off = carve(off, [2, 16], F32)
                EBEs, off = carve(off, [2, 16], F32)
                STt2, off = carve(off, [2, 8, 2, 64], F32)
                off_dead = off
                VP, off = carve(off, [2, 2, 2, 128], BF16)
                KP, off = carve(off, [2, 2, 2, 128], BF16)
                AT, off = carve(off, [2, 4, 64], BF16)
                S0BD, off = carve(off, [8, 2, 128], BF16)
                VBLK, off = carve(off, [2, 8, 64], BF16)
                off3 = off
                TF2, off = carve(off, [2, 1152], F32)
                TE_a, off = carve(off, [512], F32)
                uq = [win(8 + p) for p in range(2)]
                uf = [win(10 + p) for p in range(2)]
                for p in range(2):
                    proj(uq[p][0], uq[p][1], tiles, lambda gt, c0, lc0, w, ps, psn: act(O[:, p, lc0:lc0 + w], ps[:, 0:w], AF.Silu, [psn],
                                                                                      ["OQ.%d.%d" % (p, gt)]))
                for p in range(2):
                    proj(uf[p][0], uf[p][1], tiles, lambda gt, c0, lc0, w, ps, psn: act(TF2[:, p, lc0:lc0 + w], ps[:, 0:w], AF.Sigmoid, [psn],
                                                                                      ["TF.%d.%d" % (p, hx) for hx in range(lc0 // 256, (lc0 + w + 255) // 256)]))
                items = []
                for (gt_, c0_, lc0_, w_) in tiles:
                    halves = [(lc0_, 256), (lc0_ + 256, 256)] if w_ == 512 else [(lc0_, w_)]
                    for (hl, hw) in halves:
                        for p_ in range(2):
                            items.append(((gt_, c0_ + (hl - lc0_), hl, hw), p_))

                def p1_stage1(it):
                    (gt, c0, lc0, w), p = items[it]
                    TLB = TMPB if it % 2 == 0 else TMPC
                    tln, tfn = ("TMPB" if it % 2 == 0 else "TMPC"), "TF.%d.%d" % (p, lc0 // 256)
                    TFt = TF2[:, p, lc0:lc0 + w]
                    ts(TFt, TFt, OML[:, l, p:p + 1], LOW[:, l, p:p + 1], ALU.mult, ALU.add, [tfn, "OML", "LOW"], [tfn])
                    ts(TLB[:, 0:w], TFt, F_MIN, None, ALU.max, None, [tfn], [tln])
                    act(TLB[:, 0:w], TLB[:, 0:w], AF.Ln, [tln], [tln])

                def p1_stage2(it):
                    (gt, c0, lc0, w), p = items[it]
                    TLB = TMPB if it % 2 == 0 else TMPC
                    TE = TE_a if it % 2 == 0 else TMPA
                    tln, ten, tfn = ("TMPB" if it % 2 == 0 else "TMPC"), ("TE_a" if it % 2 == 0 else "TMPA"), "TF.%d.%d" % (p, lc0 // 256)
                    TFt = TF2[:, p, lc0:lc0 + w]
                    m0 = CST[:, C_M0P:C_M0P + w] if gt != 4 else CST[:, C_M0S:C_M0S + 128]
                    S.op("dve", lambda: nc.vector.tensor_tensor_scan(
                        out=TLB[:, 0:w], data0=m0, data1=TLB[:, 0:w], initial=0.0, op0=ALU.mult, op1=ALU.add),
                         [tln, "CST"], [tln])
                    act(TE[:, 0:w], TLB[:, 0:w], AF.Exp, [tln], [ten])
                    tt(QT[:, p, lc0:lc0 + w], O[:, p, lc0:lc0 + w], TE[:, 0:w], ALU.mult, ["OQ.%d.%d" % (p, gt), ten], ["QT.%d.%d" % (p, lc0 // 256)])
                    if gt != 4:
                        ch0 = lc0 // 64
                        vcp(EBEp[:, p, ch0:ch0 + w // 64], TE[:, 0:w].rearrange("p (a b) -> p a b", b=64)[:, :, 63], [ten], ["EBEp"])
                    else:
                        vcp(EBEs[:, p, :], TE[:, 0:128].rearrange("p (a b) -> p a b", b=8)[:, :, 7], [ten], ["EBEs"])
                    act(TE[:, 0:w], TLB[:, 0:w], AF.Exp, [tln, ten], [ten], scale=-1.0)
                    ts(TFt, TFt, -1.0, 1.0, ALU.mult, ALU.add, [tfn], [tfn])
                    tt(KTt[:, p, lc0:lc0 + w], TFt, TE[:, 0:w], ALU.mult, [tfn, ten], ["KTt.%d.%d" % (p, lc0 // 256)])

                p1_calls = [(p1_stage1, 0, None)]
                for it in range(1, len(items)):
                    p1_calls.append((p1_stage1, it, None))
                    p1_calls.append((p1_stage2, it - 1, it - 1))
                p1_calls.append((p1_stage2, len(items) - 1, len(items) - 1))
                p1_state = {"pos": 0, "done": -1}

                def p1_emit_one():
                    if p1_state["pos"] >= len(p1_calls):
                        return False
                    fn, arg, done = p1_calls[p1_state["pos"]]
                    p1_state["pos"] += 1
                    fn(arg)
                    if done is not None:
                        p1_state["done"] = done
                    return True

                def p1_flush_tile(hidx):
                    while p1_state["done"] < 2 * hidx + 1:
                        assert p1_emit_one()
                if st["w"] % NSLOT == NSLOT - 1:
                    st["w"] += 1
                vslot = st["w"] % NSLOT
                u_i0, n_i0 = win(12)
                u_i1, n_i1 = win(13)
                vms(VP[:].rearrange("p a b c d -> p (a b c d)"), 0.0, ["VP0", "VP1"])
                vms(KP[:].rearrange("p a b c d -> p (a b c d)"), 0.0, ["KP0", "KP1"])
                vms(S0BD[:].rearrange("p a b c -> p (a b c)"), 0.0, ["S0BD"])
                if has_s:
                    for sc_ in range(2):
                        dma_in(STt2[:, sc_], shgrn[:, l, 8 * sc_:8 * sc_ + 8, :, :], ["STt%d" % sc_], "sh%d" % sc_)
                if b == 0:
                    vms(SS[:].rearrange("p a b -> p (a b)"), 0.0, ["SS"])
                    vms(SBD[:].rearrange("p a b -> p (a b)"), 0.0, ["SBD"])
                nch = TB // 64
                st["held"] = {5, 6}

                def stageA(ch):
                        lcol = ch * 64
                        is_s = ch >= 16
                        gt = tiles[min(lcol // 512, len(tiles) - 1)][0]
                        e = (ch // 2) % 2
                        hf = ch % 2
                        r0, r1 = hf * 64, hf * 64 + 64
                        vpn, kpn, atn = "VP%d" % e, "KP%d" % e, "AT%d.%d" % (e, hf)
                        if hf == 0:
                            psv, pvn = newps()
                            for k in range(8):
                                mm(psv[:, 0:256], H[:, k, lcol:lcol + 128], WR[:, vslot:vslot + 2, k * 128:(k + 1) * 128],
                                   k == 0, k == 7, [n_i0, n_i1, "H.%d.%d" % (k, gt)], [pvn])
                            for h in range(2):
                                act(VP[:, e, :, h, h * 64:(h + 1) * 64],
                                    psv[:, 0:256].rearrange("s (p h v) -> s p h v", p=2, h=2)[:, :, h, :], AF.Copy, [pvn], [vpn])
                            for p in range(2):
                                tr(PSB[:, p * 128:(p + 1) * 128], KTt[:, p, lcol:lcol + 128], IDB[:], ["KTt.%d.%d" % (p, lcol // 256), "IDB"], ["psb"])
                            for h in range(2):
                                act(KP[:, e, :, h, h * 64:(h + 1) * 64],
                                    PSB[:, 0:256].rearrange("s (p h v) -> s p h v", p=2, h=2)[:, :, h, :], AF.Copy, ["psb"], [kpn])
                        for h in range(2):
                            for p in range(2):
                                mm(PS_all[r0:r1, 5 + h, p * 64:(p + 1) * 64], KTt[h * 64:(h + 1) * 64, p, lcol:lcol + 64],
                                   QT[h * 64:(h + 1) * 64, p, lcol:lcol + 64], True, True, ["KTt.%d.%d" % (p, lcol // 256), "QT.%d.%d" % (p, lcol // 256)], ["ps%d" % (5 + h)])
                        mk = CST[r0:r1, C_MS:C_MS + 64] if is_s else CST[r0:r1, C_MC:C_MC + 64]
                        tt(AT[r0:r1, e].rearrange("s (p h) t -> s h p t", h=2),
                           PS_all[r0:r1, 5:7, 0:128].rearrange("s h (p t) -> s h p t", p=2),
                           mk.unsqueeze(1).unsqueeze(1).broadcast_to([64, 2, 2, 64]), ALU.mult, ["ps5", "ps6", "CST"], [atn])

                def stageB(ch):
                        lcol = ch * 64
                        is_s = ch >= 16
                        gt = tiles[min(lcol // 512, len(tiles) - 1)][0]
                        e = (ch // 2) % 2
                        hf = ch % 2
                        r0, r1 = hf * 64, hf * 64 + 64
                        vpn, kpn, atn = "VP%d" % e, "KP%d" % e, "AT%d.%d" % (e, hf)
                        if not is_s:
                            pso, pon = newps()
                            for p in range(2):
                                for h in range(2):
                                    mm(pso[:, p * 64:(p + 1) * 64], VP[r0:r1, e, p, h, :], AT[r0:r1, e, p * 2 + h, :], h == 0, False,
                                       [vpn, atn], [pon])
                                mm(pso[:, p * 64:(p + 1) * 64], SBD[:, p, :], QT[:, p, lcol:lcol + 64], False, True,
                                   ["SBD", "QT.%d.%d" % (p, lcol // 256)], [pon])
                            act(O[:, :, lcol:lcol + 64], pso[:, 0:128].rearrange("p (a b) -> p a b", a=2), AF.Copy, [pon],
                                ["O.%d" % gt, "OQ.0.%d" % gt, "OQ.1.%d" % gt])
                            psu, pun = newps()
                            for p in range(2):
                                for h in range(2):
                                    mm(psu[:, p * 64:(p + 1) * 64], KP[r0:r1, e, p, h, :], VP[r0:r1, e, p, h, h * 64:(h + 1) * 64],
                                       h == 0, h == 1, [kpn, vpn], [pun])
                            tt(SS[:], psu[:, 0:128].rearrange("p (a b) -> p a b", a=2), SS[:], ALU.add, [pun, "SS"], ["SS"])
                            tt(SS[:], SS[:], EBEp[:, :, ch:ch + 1].broadcast_to([128, 2, 64]), ALU.mult, ["SS", "EBEp"], ["SS"])
                            tt(SBD[:].rearrange("q p (h v) -> q p h v", h=2), SS[:].unsqueeze(2).broadcast_to([128, 2, 2, 64]),
                               CST[:, C_OB:C_OB + 128].rearrange("q (h v) -> q h v", h=2).unsqueeze(1).broadcast_to([128, 2, 2, 64]),
                               ALU.mult, ["SS", "CST"], ["SBD"])
                            if b == 1 and ch == 15:
                                dma_out(hgrn_pT[:, l, :, :], SS[:], ["SS"], "hp", defer=-1)
                        else:
                            sc = ch - 16
                            STt = STt2[:, sc]
                            stn = "STt%d" % sc
                            for h in range(2):
                                vcp(S0BD[h * 64:(h + 1) * 64, :, :, h * 64:(h + 1) * 64], STt[h * 64:(h + 1) * 64, :, :, :],
                                    [stn], ["S0BD"])
                            for p in range(2):
                                pso, pon = newps()
                                for h in range(2):
                                    mm(pso[:, 0:64], VP[r0:r1, e, p, h, :], AT[r0:r1, e, p * 2 + h, :], h == 0, False, [vpn, atn], [pon])
                                for j in range(8):
                                    mm(pso[:, 8 * j:8 * j + 8], S0BD[:, j, p, :], QT[:, p, lcol + 8 * j:lcol + 8 * j + 8], False, j == 7,
                                       ["S0BD", "QT.%d.%d" % (p, lcol // 256)], [pon])
                                act(O[:, p, lcol:lcol + 64], pso[:, 0:64], AF.Copy, [pon], ["O.%d" % gt, "OQ.0.%d" % gt, "OQ.1.%d" % gt])
                                for h in range(2):
                                    tt(VBLK[r0:r1, h, :, :], VP[r0:r1, e, p, h, h * 64:(h + 1) * 64].unsqueeze(1).broadcast_to([64, 8, 64]),
                                       CST[r0:r1, C_MJ:C_MJ + 8].unsqueeze(2).broadcast_to([64, 8, 64]), ALU.mult, [vpn, "CST"], ["VBLK"])
                                psu, pun = newps()
                                for h in range(2):
                                    mm(psu[:, 0:512], KP[r0:r1, e, p, h, :], VBLK[r0:r1, h].rearrange("p a b -> p (a b)"), h == 0, h == 1,
                                       [kpn, "VBLK"], [pun])
                                tt(STt[:, :, p, :], psu[:, 0:512].rearrange("p (a b) -> p a b", a=8), STt[:, :, p, :], ALU.add,
                                   [pun, stn, "S0BD"], [stn])
                                tt(STt[:, :, p, :], STt[:, :, p, :], EBEs[:, p, 8 * sc:8 * sc + 8].unsqueeze(2).broadcast_to([128, 8, 64]),
                                   ALU.mult, [stn, "EBEs"], [stn])
                            dma_out(hgrn_sT[:, l, 8 * sc:8 * sc + 8, :, :], STt, [stn], "hs%d" % sc, defer=1)

                def tix(ch):
                    return ch // 4

                p1_flush_tile(0)
                stageA(0)
                for ch in range(nch):
                    if ch + 1 < nch:
                        p1_flush_tile(tix(ch + 1))
                        stageA(ch + 1)
                    stageB(ch)
                    p1_emit_one()
                while p1_emit_one():
                    pass
                st["held"] = set()
                phase()
                S.barrier()
                SG, _ = carve(0, [2, 512], F32)
                off = off_dead
                OSQ2, off = carve(off, [2, 512], BF16)
                N12, off = carve(off, [2, 512], F32)
                N2a, _ = carve(4608, [512], F32)
                assert off <= 34304
                MO, _ = carve(34304, [8, 512], F32)
                ug = [win(14), win(15)]
                phase()
                wo_units = [wget(w_out_u[l, m]) for m in range(8)]

                def c3_tile(tl):
                    (gt, c0, lc0, w) = tl
                    for p in range(2):
                        proj(ug[p][0], ug[p][1], [tl], lambda gt, c0, lc0, w, ps, psn: act(SG[:, p, 0:w], ps[:, 0:w], AF.Silu, [psn],
                                                                                         ["SG.%d" % p]))
                    pss = []
                    for p in range(2):
                        OSQ = OSQ2[:, p]
                        on = "OSQ%d" % p
                        act(OSQ[:, 0:w], O[:, p, lc0:lc0 + w], AF.Square, ["O.%d" % gt], [on])
                        ps, psn = newps()
                        mm(ps[:, 0:w], OBB[:], OSQ[:, 0:w], True, True, ["OBB", on], [psn])
                        pss.append((ps, psn))
                    for p in range(2):
                        N1 = N12[:, p]
                        n1n = "N1%d" % p
                        ps, psn = pss[p]
                        act(N1[:, 0:w], ps[:, 0:w], AF.Ln, [psn, "EPSC"], [n1n], scale=1.0 / 64.0, bias=EPSC[:, 0:1])
                        act(N1[:, 0:w], N1[:, 0:w], AF.Exp, [n1n], [n1n], scale=-0.5)
                        stt(N2a[:, 0:w], O[:, p, lc0:lc0 + w], pc(P_HN + l * 2 + p), N1[:, 0:w], ALU.mult, ALU.mult,
                            ["O.%d" % gt, n1n, "PAR"], ["N2a"])
                        tt(MIX[:, 4 + p, lc0:lc0 + w], N2a[:, 0:w], SG[:, p, 0:w], ALU.mult, ["N2a", "SG.%d" % p],
                           ["MIX.%d.%d" % (4 + p, gt)])

                def wout_tile(tl):
                    (gt, c0, lc0, w) = tl
                    korder = (0, 1, 2, 3, 6, 7, 4, 5)
                    for m in range(8):
                        u_o, n_o = wo_units[m]
                        ps, psn = newps()
                        for i_, k in enumerate(korder):
                            kn = "MIX.%d.%d" % (k, gt) if k in (4, 5) else "MIX.%d" % k
                            mm(ps[:, 0:w], u_o[:, k * 128:(k + 1) * 128], MIX[:, k, lc0:lc0 + w], i_ == 0, i_ == 7, [n_o, kn], [psn])
                        act(MO[:, m, 0:w], ps[:, 0:w], AF.Copy, [psn], ["MO.%d" % m])
                        act(SQ[:, m, 0:w], ps[:, 0:w], AF.Square, [psn], ["SQ.%d" % m])

                def resid_stats(tl):
                    (gt, c0, lc0, w) = tl
                    rms_stats([MO[:, k, 0:w] for k in range(8)], ["MO.%d" % k for k in range(8)], lc0, w, str(gt), presq=True)

                def resid_apply(tl):
                    (gt, c0, lc0, w) = tl
                    for k in range(8):
                        stt(TMPB[:, 0:w], MO[:, k, 0:w], pc(P_G2 + l * 8 + k), RSTD[:, lc0:lc0 + w], ALU.mult, ALU.mult,
                            ["MO.%d" % k, "RSTD.%d" % (lc0 // 512), "PAR"], ["TMPB"])
                        tt(X[:, k, c0:c0 + w], X[:, k, c0:c0 + w], TMPB[:, 0:w], ALU.add, [xb(k, gt), "TMPB"], [xb(k, gt)])

                nt_ = len(tiles)
                c3_tile(tiles[0])
                for i in range(nt_):
                    wout_tile(tiles[i])
                    resid_stats(tiles[i])
                    if i + 1 < nt_:
                        c3_tile(tiles[i + 1])
                    resid_apply(tiles[i])
                    if i >= 1:
                        make_h([tiles[i - 1]], P_G3 + l * 8)
                make_h([tiles[nt_ - 1]], P_G3 + l * 8)

                S.barrier()
                ACTB, _ = carve(0, [22, 1152], BF16)
                ffc = [0]

                def gate_up(j, tl, u_g, n_g, u_u, n_u):
                    (gt, c0, lc0, w) = tl
                    tmp = TMPB if (ffc[0] % 2 == 0) else TMPC
                    tn = "TMPB" if (ffc[0] % 2 == 0) else "TMPC"
                    ffc[0] += 1
                    proj(u_g, n_g, [tl], lambda gt, c0, lc0, w, ps, psn: act(tmp[:, 0:w], ps[:, 0:w], AF.Silu, [psn], [tn]))
                    proj(u_u, n_u, [tl], lambda gt, c0, lc0, w, ps, psn: tt(ACTB[:, j, lc0:lc0 + w], ps[:, 0:w], tmp[:, 0:w], ALU.mult,
                                                                             [psn, tn], ["ACTB.%d.%d" % (j, gt)]))

                J0 = 4
                first = [(wget(w_gate_u[l, j]), wget(w_up_u[l, j])) for j in range(J0)]
                for tl in tiles:
                    for j in range(J0):
                        (u_g, n_g), (u_u, n_u) = first[j]
                        gate_up(j, tl, u_g, n_g, u_u, n_u)
                for j in range(J0, 22):
                    u_g, n_g = wget(w_gate_u[l, j])
                    u_u, n_u = wget(w_up_u[l, j])
                    tmps = {}

                    def ev_g(gt, c0, lc0, w, ps, psn):
                        tmp, tn = (TMPB, "TMPB") if (ffc[0] % 2 == 0) else (TMPC, "TMPC")
                        ffc[0] += 1
                        tmps[gt] = (tmp, tn)
                        act(tmp[:, 0:w], ps[:, 0:w], AF.Silu, [psn], [tn])

                    def ev_u(gt, c0, lc0, w, ps, psn, j=j):
                        tmp, tn = tmps[gt]
                        tt(ACTB[:, j, lc0:lc0 + w], ps[:, 0:w], tmp[:, 0:w], ALU.mult, [psn, tn], ["ACTB.%d.%d" % (j, gt)])

                    for g0 in range(0, len(tiles), 2):
                        grp = tiles[g0:g0 + 2]
                        proj_k(u_g, n_g, grp, ev_g)
                        proj_k(u_u, n_u, grp, ev_u)
                phase()
                def down_pieces(m):
                    return [wget(w_down_u[l, m, :, k0 * 128:(k0 + nk) * 128], ncols=nk * 128) for (k0, nk) in ((0, 8), (8, 8), (16, 6))]

                def down_group(m, tl, pieces):
                    (gt, c0, lc0, w) = tl
                    ps, psn = newps()
                    for j in range(22):
                        pa, pn_ = pieces[j // 8]
                        jj = j % 8
                        mm(ps[:, 0:w], pa[:, jj * 128:(jj + 1) * 128], ACTB[:, j, lc0:lc0 + w], j == 0, j == 21,
                           [pn_, "ACTB.%d.%d" % (j, gt)], [psn])
                    act(FF[:, m, lc0:lc0 + w], ps[:, 0:w], AF.Copy, [psn], ["FF.%d.%d" % (m, gt)])

                for m in range(5):
                    pcs = down_pieces(m)
                    for tl in tiles:
                        down_group(m, tl, pcs)
                last = {m: down_pieces(m) for m in (5, 6, 7)}
                for m in (5, 6, 7):
                    down_group(m, tiles[0], last[m])
                for i in range(1, len(tiles)):
                    down_group(5, tiles[i], last[5])
                    add_residual(FF, "FF", [tiles[i - 1]], P_G4 + l * 8)
                    down_group(6, tiles[i], last[6])
                    down_group(7, tiles[i], last[7])
                add_residual(FF, "FF", [tiles[-1]], P_G4 + l * 8)
                if l == 1:
                    for (gt, c0, lc0, w) in tiles:
                        for k in range(8):
                            dma_out(yT[:, k, c0:c0 + w], X[:, k, c0:c0 + w], [xb(k, gt)], "y")
        try:
            body()
        except _StopBuild:
            S.barrier()
            for k in range(8):
                dma_out(yT[:, k, :], X[:, k, :], [xb(k, t) for t in range(5)], "y")
        S.emit(final_wait_chans=[c for c in S.chan_cnt if c.startswith("o_")])
    return nc


def _consts():
    c = np.zeros((128, NCST), np.float32)
    c[:, C_ID:C_ID + 128] = np.eye(128, dtype=np.float32)
    m = np.ones(512, np.float32)
    m[0::64] = 0.0
    c[:, C_M0P:C_M0P + 512] = m[None]
    m = np.ones(128, np.float32)
    m[0::8] = 0.0
    c[:, C_M0S:C_M0S + 128] = m[None]
    s = np.arange(64)[:, None]
    t = np.arange(64)[None, :]
    for r_ in (0, 64):
        c[r_:r_ + 64, C_MC:C_MC + 64] = (s <= t).astype(np.float32)
        c[r_:r_ + 64, C_MS:C_MS + 64] = ((s <= t) & (s // 8 == t // 8)).astype(np.float32)
        c[r_:r_ + 64, C_MJ:C_MJ + 8] = (np.arange(64)[:, None] // 8 == np.arange(8)[None, :]).astype(np.float32)
    pp = np.arange(128)
    c[:, C_OB:C_OB + 128] = (pp[:, None] // 64 == pp[None, :] // 64).astype(np.float32)
    wins = (2, 4, 8, 16)
    for ch in range(2):
        for half in range(2):
            w = wins[ch * 2 + half]
            cnt = np.minimum(np.arange(16) + 1, w).astype(np.float32)
            c[half * 64:(half + 1) * 64, C_RC + 16 * ch:C_RC + 16 * ch + 16] = (1.0 / cnt)[None]
            c[half * 64:(half + 1) * 64, C_IW + ch] = 1.0 / w
    return c


def _fm(v):
    v = np.asarray(v, np.float32)
    lead = v.shape[:-1]
    n = v.shape[-1] // 128
    return np.moveaxis(v.reshape(lead + (n, 128)), -1, 0)


_PROG = {}


def kernel(x_prompt, x_sample, state_conv, state_pool, state_hgrn, state_conf,
           norm_mix_pre, norm_mix_post, w_in, conv_w, pool_w, pool_scale, hgrn_lb, hgrn_norm,
           conf_dw, conf_b, conf_ln_g, conf_ln_b, w_out, norm_ffn_pre, norm_ffn_post,
           w_gate, w_up, w_down):
    f = lambda a: np.ascontiguousarray(np.asarray(a, dtype=np.float32))
    x_prompt, x_sample = f(x_prompt), f(x_sample)
    par = np.zeros((128, NPAR), np.float32)
    for col, arr in ((P_G1, norm_mix_pre), (P_G2, norm_mix_post), (P_G3, norm_ffn_pre), (P_G4, norm_ffn_post)):
        par[:, col:col + 16] = _fm(arr).reshape(128, 16)
    par[:, P_CW:P_CW + 12] = np.transpose(_fm(conv_w), (0, 1, 3, 2)).reshape(128, 12)
    for col, arr in ((P_PS, pool_scale), (P_LB, hgrn_lb), (P_HN, hgrn_norm), (P_CB, conf_b), (P_LG, conf_ln_g), (P_LBI, conf_ln_b)):
        par[:, col:col + 4] = _fm(arr).reshape(128, 4)
    par[:, P_DW:P_DW + 124] = np.transpose(_fm(conf_dw), (0, 1, 3, 2)).reshape(128, 124)
    cst = _consts()
    pw = f(pool_w)
    pwbd = np.zeros((128, 2, 2, 128), np.float32)
    for l in range(2):
        for g in range(4):
            c, hh = g // 2, g % 2
            pwbd[hh * 64:(hh + 1) * 64, l, c, hh * 64:(hh + 1) * 64] = pw[l, g]

    def units(w, nu):
        w = f(w)
        L, KK, NN = w.shape
        K = KK // 128
        return np.ascontiguousarray(w.reshape(L, K, 128, nu, 128).transpose(0, 3, 2, 1, 4).reshape(L, nu, 128, K * 128))

    w_in_u = units(w_in, 20)
    w_out_u = units(w_out, 8)
    w_gate_u = units(w_gate, 22)
    w_up_u = units(w_up, 22)
    w_down_u = units(w_down, 8)
    sc_, sp_, sh_, sf_ = f(state_conv), f(state_pool), f(state_hgrn), f(state_conf)

    in_maps = []
    for b in range(8):
        tok = np.concatenate([x_prompt[b], x_sample[16 * b:16 * b + 16].reshape(128, 1024)], axis=0)
        xT = np.ascontiguousarray(tok.reshape(NT, 8, 128).transpose(2, 1, 0))

        def st(a, R):
            return np.ascontiguousarray(a[16 * b:16 * b + 16].reshape(16, 2, R, 2, 128).transpose(4, 1, 3, 0, 2))

        hg = sh_[16 * b:16 * b + 16].reshape(16, 2, 2, 2, 64, 64).transpose(3, 4, 1, 0, 2, 5).reshape(128, 2, 16, 2, 64)
        in_maps.append({
            "xT": xT, "sconv": st(sc_, 2), "spool": st(sp_, 15), "sconf": st(sf_, 30), "shgrn": np.ascontiguousarray(hg),
            "w_in_u": w_in_u, "w_out_u": w_out_u, "w_gate_u": w_gate_u, "w_up_u": w_up_u, "w_down_u": w_down_u,
            "par": par, "cst": cst, "pwbd": pwbd,
        })
    if "nc" not in _PROG:
        import os
        stop = os.environ.get("MK_STOP")
        _PROG["nc"] = build_program(None if stop is None else int(stop))
    res = run_bass_kernel_spmd(_PROG["nc"], in_maps, core_ids=list(range(8)))
    R = res.results

    y_prompt = np.empty((8, 2048, 1024), np.float32)
    y_sample = np.empty((128, 8, 1024), np.float32)
    conv_p = np.empty((8, 2, 2, 256), np.float32)
    pool_p = np.empty((8, 2, 15, 256), np.float32)
    hgrn_p = np.empty((8, 2, 4, 64, 64), np.float32)
    conf_p = np.empty((8, 2, 30, 256), np.float32)
    conv_s = np.empty((128, 2, 2, 256), np.float32)
    pool_s = np.empty((128, 2, 15, 256), np.float32)
    hgrn_s = np.empty((128, 2, 4, 64, 64), np.float32)
    conf_s = np.empty((128, 2, 30, 256), np.float32)
    for b in range(8):
        r = R[b]
        tok = np.asarray(r["yT"]).transpose(2, 1, 0).reshape(NT, 1024)
        y_prompt[b] = tok[:2048]
        y_sample[16 * b:16 * b + 16] = tok[2048:].reshape(16, 8, 1024)
        for dst, key in ((conv_p, "conv_pT"), (pool_p, "pool_pT"), (conf_p, "conf_pT")):
            a = np.asarray(r[key])
            dst[b] = a.transpose(1, 3, 2, 0).reshape(2, a.shape[3], 256)
        hp = np.asarray(r["hgrn_pT"]).reshape(2, 64, 2, 2, 64)
        hgrn_p[b] = hp.transpose(2, 3, 0, 1, 4).reshape(2, 4, 64, 64)
        for dst, key in ((conv_s, "conv_sT"), (pool_s, "pool_sT"), (conf_s, "conf_sT")):
            a = np.asarray(r[key])
            dst[16 * b:16 * b + 16] = a.transpose(3, 1, 4, 2, 0).reshape(16, 2, a.shape[4], 256)
        hs = np.asarray(r["hgrn_sT"]).reshape(2, 64, 2, 16, 2, 64)
        hgrn_s[16 * b:16 * b + 16] = hs.transpose(3, 2, 4, 0, 1, 5).reshape(16, 2, 4, 64, 64)
    return (y_prompt, y_sample, conv_p, pool_p, hgrn_p, conf_p, conv_s, pool_s, hgrn_s, conf_s)
```

```python
import numpy as np
from contextlib import ExitStack
import concourse.bass as bass
import concourse.mybir as mybir
from concourse.bass_utils import run_bass_kernel_spmd

F32 = mybir.dt.float32
BF16 = mybir.dt.bfloat16
AF = mybir.ActivationFunctionType
ALU = mybir.AluOpType

NT = 2176
EPS = 1e-6
F_MIN = 1e-20
ENGS = ("pe", "act", "dve", "pool", "sp")


class Buf:
    __slots__ = ("name", "w", "r")

    def __init__(self, name):
        self.name = name
        self.w = None
        self.r = []


class Op:
    __slots__ = ("eng", "fn", "waits", "signal", "idx", "is_dma", "chan", "val", "known", "is_nop")


class Sched:
    def __init__(self, nc, self_sync=("act", "dve", "pool")):
        self.nc = nc
        self.prog = {e: [] for e in ENGS}
        self.known = {e: {} for e in ENGS}
        self.self_sync = set(self_sync)
        self.chan_cnt = {}
        self.bufs = {}
        self.last = {}
        self.dma_pending = []

    def buf(self, name):
        b = self.bufs.get(name)
        if b is None:
            b = Buf(name)
            self.bufs[name] = b
        return b

    def _norm(self, lst):
        out = []
        for x in lst:
            if x is None:
                continue
            if isinstance(x, str):
                out.append(self.buf(x))
            else:
                out.extend(self._norm(x))
        return out

    def _add(self, eng, fn, reads, writes, is_dma=False, chan=None, extra=(), defer=0):
        reads = self._norm(reads)
        writes = self._norm(writes)
        op = Op()
        op.eng = eng
        op.fn = fn
        op.is_dma = is_dma
        op.chan = chan
        op.signal = False
        op.is_nop = False
        op.idx = len(self.prog[eng])
        op.waits = []
        kn = self.known[eng]
        toks = list(extra)
        for b in reads:
            if b.w is not None:
                toks.append(b.w)
        for b in writes:
            if b.w is not None:
                toks.append(b.w)
            toks.extend(b.r)
        best = {}
        for t in toks:
            if t[0] not in best or best[t[0]][1] < t[1]:
                best[t[0]] = t
        for t in best.values():
            src, v, top = t
            if src == eng and not top.is_dma and eng not in self.self_sync:
                continue
            if kn.get(src, -1) >= v:
                continue
            kn[src] = v
            op.waits.append(t)
            top.signal = True
            if top.known is not None:
                for s2, v2 in top.known.items():
                    if kn.get(s2, -1) < v2:
                        kn[s2] = v2
        if is_dma:
            n = self.chan_cnt.get(chan, 0) + 1
            self.chan_cnt[chan] = n
            op.val = 16 * n
            tok = ("dma:" + chan, op.val, op)
            op.known = None
            if defer >= 0:
                self.dma_pending.append([tok, defer])
        else:
            op.val = op.idx
            tok = (eng, op.idx, op)
            op.known = dict(kn)
            self.last[eng] = tok
        for b in reads:
            b.r.append(tok)
        for b in writes:
            b.w = tok
            b.r = []
        self.prog[eng].append(op)
        return op

    def op(self, eng, fn, reads=(), writes=()):
        return self._add(eng, fn, reads, writes)

    def dma(self, eng, fn, reads=(), writes=(), chan="d", defer=0):
        return self._add(eng, fn, reads, writes, is_dma=True, chan=chan, defer=defer)

    def barrier(self, engs=("act", "dve", "sp")):
        nc = self.nc
        toks = [self.last[e] for e in ENGS if e in self.last] + [t for t, d in self.dma_pending if d == 0]
        self.dma_pending = [[t, d - 1] for t, d in self.dma_pending if d > 0]
        hand = {"pe": nc.tensor, "act": nc.scalar, "dve": nc.vector, "sp": nc.sync}
        saved = dict(self.last)
        for e in engs:
            o = self._add(e, (lambda h=hand[e]: h.nop()), (), (), extra=toks)
            o.is_nop = True
        self.last = saved

    def emit(self, final_wait_chans=()):
        nc = self.nc
        with ExitStack() as es:
            esem = {e: es.enter_context(nc.semaphore("s_" + e)) for e in ENGS}
            csem = {c: es.enter_context(nc.semaphore("c_" + c)) for c in self.chan_cnt}
            sigcnt = {}
            for e in ENGS:
                c = 0
                for op in self.prog[e]:
                    if not op.is_dma and op.signal:
                        c += 1
                        sigcnt[(e, op.idx)] = c
            block = es.enter_context(nc.Block())
            hand = {"pe": nc.tensor, "act": nc.scalar, "dve": nc.vector, "pool": nc.gpsimd, "sp": nc.sync}

            def run(e):
                h = hand[e]
                for op in self.prog[e]:
                    for (src, v, top) in op.waits:
                        if top.is_dma:
                            h.wait_ge(csem[top.chan], v)
                        else:
                            h.wait_ge(esem[src], sigcnt[(src, v)])
                    ins = op.fn()
                    if op.is_dma:
                        ins.then_inc(csem[op.chan], 16)
                    elif op.signal:
                        ins.then_inc(esem[e], 1)
                if e == "sp":
                    for c in final_wait_chans:
                        h.wait_ge(csem[c], 16 * self.chan_cnt[c])

            @block.tensor
            def _(eng):
                run("pe")

            @block.scalar
            def _(eng):
                run("act")

            @block.vector
            def _(eng):
                run("dve")

            @block.gpsimd
            def _(eng):
                run("pool")

            @block.sync
            def _(eng):
                run("sp")


C_ID = 0
C_M0P = 128
C_M0S = 640
C_MC = 768
C_MS = 832
C_MJ = 896
C_OB = 904
C_RC = 1032
C_IW = 1064
NCST = 1066

P_G1, P_G2, P_G3, P_G4 = 0, 16, 32, 48
P_CW = 64
P_PS = 76
P_LB = 80
P_HN = 84
P_CB = 88
P_LG = 92
P_LBI = 96
P_DW = 100
NPAR = 224

NSLOT = 10
BIG32 = 12672


class _StopBuild(Exception):
    pass


MARKS = []
MARKS_DVE = []
S_holder = [None]


def build_program(stop=None):
    nc = bass.Bass("TRN2", target_bir_lowering=False)
    phase_ctr = [0]

    def phase():
        MARKS.append(sum(1 for o in S_holder[0].prog["pe"] if not getattr(o, "is_nop", False)))
        MARKS_DVE.append(sum(1 for o in S_holder[0].prog["dve"] if not getattr(o, "is_nop", False)))
        phase_ctr[0] += 1
        if stop is not None and phase_ctr[0] > stop:
            raise _StopBuild()

    def din(name, shape):
        return nc.dram_tensor(name, shape, F32, kind="ExternalInput").ap()

    def dout(name, shape):
        return nc.dram_tensor(name, shape, F32, kind="ExternalOutput").ap()

    xT = din("xT", [128, 8, NT])
    sconv = din("sconv", [128, 2, 2, 16, 2])
    spool = din("spool", [128, 2, 2, 16, 15])
    sconf = din("sconf", [128, 2, 2, 16, 30])
    shgrn = din("shgrn", [128, 2, 16, 2, 64])
    w_in_u = din("w_in_u", [2, 20, 128, 1024])
    w_out_u = din("w_out_u", [2, 8, 128, 1024])
    w_gate_u = din("w_gate_u", [2, 22, 128, 1024])
    w_up_u = din("w_up_u", [2, 22, 128, 1024])
    w_down_u = din("w_down_u", [2, 8, 128, 2816])
    par_d = din("par", [128, NPAR])
    cst_d = din("cst", [128, NCST])
    pwbd_d = din("pwbd", [128, 2, 2, 128])

    yT = dout("yT", [128, 8, NT])
    conv_pT = dout("conv_pT", [128, 2, 2, 2])
    pool_pT = dout("pool_pT", [128, 2, 2, 15])
    conf_pT = dout("conf_pT", [128, 2, 2, 30])
    hgrn_pT = dout("hgrn_pT", [128, 2, 2, 64])
    conv_sT = dout("conv_sT", [128, 2, 2, 16, 2])
    pool_sT = dout("pool_sT", [128, 2, 2, 16, 15])
    conf_sT = dout("conf_sT", [128, 2, 2, 16, 30])
    hgrn_sT = dout("hgrn_sT", [128, 2, 16, 2, 64])

    with ExitStack() as es:
        def sb(name, shape, dt):
            return es.enter_context(nc.sbuf_tensor(name, shape, dt))

        X = sb("X", [128, 8, NT], F32)
        HM = sb("HM", [128, 2, 8, 1152], BF16)
        BIG = sb("BIG", [128, BIG32], F32)
        SQ = sb("SQ", [128, 8, 512], BF16)
        RSTD = sb("RSTD", [128, 1152], F32)
        TMPA = sb("TMPA", [128, 512], F32)
        TMPB = sb("TMPB", [128, 512], F32)
        TMPC = sb("TMPC", [128, 512], F32)
        CST = sb("CST", [128, NCST], F32)
        PAR = sb("PAR", [128, NPAR], F32)
        PWB = sb("PWB", [128, 2, 2, 128], BF16)
        IDB = sb("IDB", [128, 128], BF16)
        ONEB = sb("ONEB", [128, 128], BF16)
        OBB = sb("OBB", [128, 128], BF16)
        EPSC = sb("EPSC", [128, 1], F32)
        LOW = sb("LOW", [128, 2, 2], F32)
        OML = sb("OML", [128, 2, 2], F32)
        LBT = sb("LBT", [128, 8], F32)
        CARA = sb("CARA", [128, 2, 2], F32)
        CARB = sb("CARB", [128, 2, 15], F32)
        CARD = sb("CARD", [128, 2, 30], BF16)
        SS = sb("SS", [128, 2, 64], F32)
        SBD = sb("SBD", [128, 2, 128], BF16)
        WR = sb("WR", [128, NSLOT, 1024], BF16)
        STGA = sb("STGA", [128, 2, 16, 2], F32)
        STGB = sb("STGB", [128, 2, 16, 15], F32)
        STGD = sb("STGD", [128, 2, 16, 30], F32)
        OPA = sb("OPA", [128, 2, 2], F32)
        OPB = sb("OPB", [128, 2, 15], F32)
        OPD = sb("OPD", [128, 2, 30], F32)
        PS_all = es.enter_context(nc.psum_tensor("psall", [128, 7, 512], F32))
        PS = [PS_all[:, i, :] for i in range(7)]
        PSB = es.enter_context(nc.psum_tensor("psb", [128, 1024], BF16))

        H = HM[:, 0]
        MIX = HM[:, 1]
        FF = HM[:].rearrange("p a k n -> p (a k n)").bitcast(F32).rearrange("p (k n) -> p k n", k=8)

        S = Sched(nc)
        S_holder[0] = S
        st = {"ps": 0, "w": 0, "held": set()}

        def mm(out, lhsT, rhs, start, stop, r, w):
            S.op("pe", lambda: nc.tensor.matmul(out, lhsT=lhsT, rhs=rhs, start=start, stop=stop), r, w)

        def tr(out, in_, ident, r, w):
            S.op("pe", lambda: nc.tensor.transpose(out, in_, ident), r, w)

        def act(out, in_, func, r, w, scale=None, bias=None):
            kw = {}
            if scale is not None:
                kw["scale"] = scale
            if bias is not None:
                kw["bias"] = bias
            S.op("act", lambda: nc.scalar.activation(out=out, in_=in_, func=func, **kw), r, w)

        def tt(out, in0, in1, op, r, w):
            S.op("dve", lambda: nc.vector.tensor_tensor(out=out, in0=in0, in1=in1, op=op), r, w)

        def ts(out, in0, s1, s2, op0, op1, r, w):
            if op1 is None:
                S.op("dve", lambda: nc.vector.tensor_scalar(out=out, in0=in0, scalar1=s1, scalar2=None, op0=op0), r, w)
            else:
                S.op("dve", lambda: nc.vector.tensor_scalar(out=out, in0=in0, scalar1=s1, scalar2=s2, op0=op0, op1=op1), r, w)

        def stt(out, in0, scalar, in1, op0, op1, r, w):
            S.op("dve", lambda: nc.vector.scalar_tensor_tensor(out=out, in0=in0, scalar=scalar, in1=in1, op0=op0, op1=op1), r, w)

        def vcp(out, in_, r, w):
            S.op("dve", lambda: nc.vector.tensor_copy(out=out, in_=in_), r, w)

        def vms(ap, val, w):
            S.op("dve", lambda: nc.vector.memset(ap, val), (), w)

        def vrec(out, in_, r, w):
            S.op("dve", lambda: nc.vector.reciprocal(out=out, in_=in_), r, w)

        def dma_in(out, in_, w, chan, defer=0):
            S.dma("sp", lambda: nc.sync.dma_start(out=out, in_=in_), (), w, chan=chan, defer=defer)

        def dma_out(out, in_, r, chan, defer=0):
            S.dma("sp", lambda: nc.sync.dma_start(out=out, in_=in_), r, (), chan="o_" + chan, defer=defer)

        def newps():
            while True:
                i = st["ps"] % 7
                st["ps"] += 1
                if i not in st["held"]:
                    return PS[i], "ps%d" % i

        def wget(src, ncols=1024):
            s = st["w"] % NSLOT
            st["w"] += 1
            name = "wr%d" % s
            dst = WR[:, s, 0:ncols]
            S.dma("pool", lambda: nc.gpsimd.dma_start(out=dst, in_=src), (), [name], chan=name)
            return WR[:, s, :], name

        def carve(off, shape, dt):
            n = 1
            for s_ in shape:
                n *= s_
            nb = n * (4 if dt == F32 else 2)
            assert off % 4 == 0
            n32 = (nb + 3) // 4
            assert off // 4 + n32 <= BIG32, (off, shape)
            ap = BIG[:, off // 4: off // 4 + n32]
            if dt == BF16:
                ap = ap.bitcast(BF16)[:, 0:n]
            if len(shape) == 2:
                ap = ap.rearrange("p (a b) -> p a b", a=shape[0])
            elif len(shape) == 3:
                ap = ap.rearrange("p (a b c) -> p a b c", a=shape[0], b=shape[1])
            elif len(shape) == 4:
                ap = ap.rearrange("p (a b c d) -> p a b c d", a=shape[0], b=shape[1], c=shape[2])
            return ap, off + 4 * n32

        def pc(col):
            return PAR[:, col:col + 1]

        dma_in(CST[:], cst_d, ["CST"], "c0")
        dma_in(PAR[:], par_d, ["PAR"], "c1")
        dma_in(TMPB[:], pwbd_d.rearrange("p a b c -> p (a b c)"), ["TMPB"], "c2")
        for k in range(8):
            dma_in(X[:, k, 0:512], xT[:, k, 0:512], ["X.%d.0" % k], "x%d" % k)
        for k in range(8):
            dma_in(X[:, k, 512:1024], xT[:, k, 512:1024], ["X.%d.1" % k], "xc%d" % k)
        for k in range(8):
            dma_in(X[:, k, 1024:NT], xT[:, k, 1024:NT], ["X.%d.%d" % (k, t) for t in (2, 3, 4)], "xb%d" % k)
        for c in range(2):
            dma_in(STGA[:, c], sconv[:, 0, c], ["STGA.%d" % c], "sa%d" % c, defer=-1)
            dma_in(STGB[:, c], spool[:, 0, c], ["STGB.%d" % c], "sb%d" % c, defer=-1)
            dma_in(STGD[:, c], sconf[:, 0, c], ["STGD.%d" % c], "sd%d" % c, defer=-1)
        vcp(PWB[:].rearrange("p a b c -> p (a b c)"), TMPB[:], ["TMPB"], ["PWB"])
        vcp(IDB[:], CST[:, C_ID:C_ID + 128], ["CST"], ["IDB"])
        vcp(OBB[:], CST[:, C_OB:C_OB + 128], ["CST"], ["OBB"])
        vms(ONEB[:], 1.0, ["ONEB"])
        vms(EPSC[:], EPS, ["EPSC"])
        act(LBT[:, 0:4], PAR[:, P_LB:P_LB + 4], AF.Exp, ["PAR"], ["LBT"])
        tt(LBT[:, 4:6], LBT[:, 0:2], LBT[:, 2:4], ALU.add, ["LBT"], ["LBT"])
        vrec(LBT[:, 6:8], LBT[:, 4:6], ["LBT"], ["LBT"])
        vms(LOW[:, 0, :], 0.0, ["LOW"])
        tt(LOW[:, 1, :], LBT[:, 2:4], LBT[:, 6:8], ALU.mult, ["LBT", "LOW"], ["LOW"])
        ts(OML[:].rearrange("p a b -> p (a b)"), LOW[:].rearrange("p a b -> p (a b)"), -1.0, 1.0, ALU.mult, ALU.add,
           ["LOW"], ["OML"])

        blocks = [
            [(0, 0, 0, 512), (1, 512, 512, 512)],
            [(2, 1024, 0, 512), (3, 1536, 512, 512), (4, 2048, 1024, 128)],
        ]

        def xb(k, gt):
            return "X.%d.%d" % (k, gt)

        def rms_stats(srcs, srcbufs, lc0, w, tag, presq=False):
            for k in range(8):
                if not presq:
                    act(SQ[:, k, 0:w], srcs[k], AF.Square, [srcbufs[k]], ["SQ.%d" % k])
            ps, psn = newps()
            for k in range(8):
                mm(ps[:, 0:w], ONEB[:], SQ[:, k, 0:w], k == 0, k == 7, ["ONEB", "SQ.%d" % k], [psn])
            act(TMPA[:, 0:w], ps[:, 0:w], AF.Ln, [psn, "EPSC"], ["TMPA"], scale=1.0 / 1024.0, bias=EPSC[:, 0:1])
            act(RSTD[:, lc0:lc0 + w], TMPA[:, 0:w], AF.Exp, ["TMPA"], ["RSTD.%d" % (lc0 // 512)], scale=-0.5)

        def make_h(tiles, gcol):
            for (gt, c0, lc0, w) in tiles:
                rms_stats([X[:, k, c0:c0 + w] for k in range(8)], [xb(k, gt) for k in range(8)], lc0, w, str(gt))
                for k in range(8):
                    stt(H[:, k, lc0:lc0 + w], X[:, k, c0:c0 + w], pc(gcol + k), RSTD[:, lc0:lc0 + w], ALU.mult, ALU.mult,
                        [xb(k, gt), "RSTD.%d" % (lc0 // 512), "PAR"], ["H.%d.%d" % (k, gt), "HMA.%d" % (k // 2)])

        def add_residual(src, srcname, tiles, gcol):
            for (gt, c0, lc0, w) in tiles:
                rms_stats([src[:, k, lc0:lc0 + w] for k in range(8)], ["%s.%d.%d" % (srcname, k, gt) for k in range(8)],
                          lc0, w, str(gt))
                for k in range(8):
                    stt(TMPB[:, 0:w], src[:, k, lc0:lc0 + w], pc(gcol + k), RSTD[:, lc0:lc0 + w], ALU.mult, ALU.mult,
                        ["%s.%d.%d" % (srcname, k, gt), "RSTD.%d" % (lc0 // 512), "PAR"] + (["HMA.%d" % k] if srcname == "FF" else []),
                        ["TMPB"])
                    tt(X[:, k, c0:c0 + w], X[:, k, c0:c0 + w], TMPB[:, 0:w], ALU.add, [xb(k, gt), "TMPB"], [xb(k, gt)])

        def proj_k(unit, uname, tiles_, evac):
            banks = [newps() for _ in tiles_]
            for k in range(8):
                for (gt, c0, lc0, w), (ps, psn) in zip(tiles_, banks):
                    mm(ps[:, 0:w], unit[:, k * 128:(k + 1) * 128], H[:, k, lc0:lc0 + w], k == 0, k == 7,
                       [uname, "H.%d.%d" % (k, gt)], [psn])
            for (gt, c0, lc0, w), (ps, psn) in zip(tiles_, banks):
                evac(gt, c0, lc0, w, ps, psn)

        def proj(unit, uname, tiles, evac):
            for (gt, c0, lc0, w) in tiles:
                ps, psn = newps()
                for k in range(8):
                    mm(ps[:, 0:w], unit[:, k * 128:(k + 1) * 128], H[:, k, lc0:lc0 + w], k == 0, k == 7,
                       [uname, "H.%d.%d" % (k, gt)], [psn])
                evac(gt, c0, lc0, w, ps, psn)

        def body():
          for l in range(2):
            for b in range(2):
                tiles = blocks[b]
                TB = 1152 if b == 1 else 1024
                ptiles = [t for t in tiles if t[0] != 4]
                has_s = (b == 1)
                phase()
                make_h(tiles, P_G1 + l * 8)

                def win(u):
                    return wget(w_in_u[l, u])

                phase()
                S.barrier()
                off = 0
                UAp, off = carve(off, [2, 1026], F32)
                UAs, off = carve(off, [2, 16, 10], F32)
                ACC, off = carve(off, [2, 1152], F32)
                ACt, off = carve(off, [1152], F32)
                PBp, off = carve(off, [2, 1040], F32)
                PBs, off = carve(off, [2, 16, 23], F32)
                T1, off = carve(off, [1040], F32)
                T2, off = carve(off, [1040], F32)
                T1s, off = carve(off, [16, 23], F32)
                T2s, off = carve(off, [16, 23], F32)
                T16, off = carve(off, [16], F32)
                PL, off = carve(off, [2, 1152], BF16)

                def A1(c):
                    if b == 0:
                        vms(UAp[:, c, 0:2], 0.0, ["UAp.%d" % c])
                    else:
                        vcp(UAp[:, c, 0:2], CARA[:, c, :], ["CARA"], ["UAp.%d" % c])
                        vcp(UAs[:, c, :, 0:2], STGA[:, c], ["STGA.%d" % c], ["UAs.%d" % c])
                    u_c, n_c = win(2 + c)
                    u_u, n_u = win(4 + c)

                    def ev_c(gt, c0, lc0, w, ps, psn):
                        act(ACt[:, lc0:lc0 + w], ps[:, 0:w], AF.Copy, [psn], ["ACt.%d" % gt])

                    def ev_u(gt, c0, lc0, w, ps, psn):
                        if gt != 4:
                            tt(UAp[:, c, 2 + lc0:2 + lc0 + w], ps[:, 0:w], ACt[:, lc0:lc0 + w], ALU.mult,
                               [psn, "ACt.%d" % gt], ["UAp.%d" % c])
                        else:
                            tt(UAs[:, c, :, 2:10], ps[:, 0:128].rearrange("p (j t) -> p j t", t=8),
                               ACt[:, lc0:lc0 + 128].rearrange("p (j t) -> p j t", t=8), ALU.mult,
                               [psn, "ACt.%d" % gt], ["UAs.%d" % c])

                    proj(u_c, n_c, tiles, ev_c)
                    proj(u_u, n_u, tiles, ev_u)

                def A2(c):
                    cw = P_CW + (l * 2 + c) * 3
                    ts(ACC[:, c, 0:1024], UAp[:, c, 0:1024], pc(cw), None, ALU.mult, None, ["UAp.%d" % c, "PAR"], ["ACC.%d" % c])
                    for kk in (1, 2):
                        stt(ACC[:, c, 0:1024], UAp[:, c, kk:kk + 1024], pc(cw + kk), ACC[:, c, 0:1024], ALU.mult, ALU.add,
                            ["UAp.%d" % c, "ACC.%d" % c, "PAR"], ["ACC.%d" % c])
                    if has_s:
                        accs = ACC[:, c, 1024:1152].rearrange("p (j t) -> p j t", t=8)
                        ts(accs, UAs[:, c, :, 0:8], pc(cw), None, ALU.mult, None, ["UAs.%d" % c, "PAR"], ["ACCs.%d" % c])
                        for kk in (1, 2):
                            stt(accs, UAs[:, c, :, kk:kk + 8], pc(cw + kk), accs, ALU.mult, ALU.add,
                                ["UAs.%d" % c, "ACCs.%d" % c, "PAR"], ["ACCs.%d" % c])

                def A3(c):
                    u_b, n_b = win(0 + c)

                    def ev_b(gt, c0, lc0, w, ps, psn):
                        tt(MIX[:, c, lc0:lc0 + w], ps[:, 0:w], ACC[:, c, lc0:lc0 + w], ALU.mult,
                           [psn, "ACC.%d" % c, "ACCs.%d" % c], ["MIX.%d" % c])

                    proj(u_b, n_b, tiles, ev_b)
                    if b == 0:
                        vcp(CARA[:, c, :], UAp[:, c, 1024:1026], ["UAp.%d" % c], ["CARA"])
                    else:
                        vcp(OPA[:, c, :], UAp[:, c, 1024:1026], ["UAp.%d" % c], ["OPA.%d" % c])
                        dma_out(conv_pT[:, l, c, :], OPA[:, c, :], ["OPA.%d" % c], "cpA%d" % c, defer=-1)
                        vcp(STGA[:, c], UAs[:, c, :, 8:10], ["UAs.%d" % c], ["STGA.%d" % c])
                        dma_out(conv_sT[:, l, c], STGA[:, c], ["STGA.%d" % c], "csA%d" % c, defer=-1)
                        if l == 0:
                            dma_in(STGA[:, c], sconv[:, 1, c], ["STGA.%d" % c], "sa%d" % c, defer=-1)

                def B1(c):
                    if b == 0:
                        vms(PBp[:, c, 0:15], 0.0, ["PBp.%d" % c])
                    else:
                        vcp(PBp[:, c, 0:15], CARB[:, c, :], ["CARB"], ["PBp.%d" % c])
                        vcp(PBs[:, c, :, 0:15], STGB[:, c], ["STGB.%d" % c], ["PBs.%d" % c])
                    u_p, n_p = win(6 + c)

                    def ev_p(gt, c0, lc0, w, ps, psn):
                        if gt != 4:
                            act(PBp[:, c, 15 + lc0:15 + lc0 + w], ps[:, 0:w], AF.Copy, [psn], ["PBp.%d" % c])
                        else:
                            act(PBs[:, c, :, 15:23], ps[:, 0:128].rearrange("p (j t) -> p j t", t=8), AF.Copy,
                                [psn], ["PBs.%d" % c])

                    proj(u_p, n_p, tiles, ev_p)

                def B2(c):
                    NP_ = 1039
                    pb = "PBp.%d" % c
                    tt(T1[:, 1:NP_], PBp[:, c, 1:NP_], PBp[:, c, 0:NP_ - 1], ALU.add, [pb], ["T1"])
                    tt(T2[:, 3:NP_], T1[:, 3:NP_], T1[:, 1:NP_ - 2], ALU.add, ["T1"], ["T2"])
                    if c == 1:
                        tt(T1[:, 7:NP_], T2[:, 7:NP_], T2[:, 3:NP_ - 4], ALU.add, ["T2", "T1"], ["T1"])
                        tt(T2[:, 15:NP_], T1[:, 15:NP_], T1[:, 7:NP_ - 8], ALU.add, ["T1", "T2"], ["T2"])
                    iw = CST[:, C_IW + c:C_IW + c + 1]
                    for (lo, hi, Tw) in ((0, 64, T1), (64, 128, T2)):
                        stt(PL[lo:hi, c, 0:1024], Tw[lo:hi, 15:NP_], iw[lo:hi], PBp[lo:hi, c, 15:NP_], ALU.mult, ALU.subtract,
                            ["T1", "T2", pb, "CST"], ["PL.%d" % c])
                        if b == 0:
                            tt(T16[lo:hi, :], Tw[lo:hi, 15:31], CST[lo:hi, C_RC + 16 * c:C_RC + 16 * c + 16], ALU.mult,
                               ["T1", "T2", "CST"], ["T16"])
                            tt(PL[lo:hi, c, 0:16], T16[lo:hi, :], PBp[lo:hi, c, 15:31], ALU.subtract,
                               ["T16", pb, "PL.%d" % c], ["PL.%d" % c])
                    if has_s:
                        sbn = "PBs.%d" % c
                        tt(T1s[:, :, 1:23], PBs[:, c, :, 1:23], PBs[:, c, :, 0:22], ALU.add, [sbn], ["T1s"])
                        tt(T2s[:, :, 3:23], T1s[:, :, 3:23], T1s[:, :, 1:21], ALU.add, ["T1s"], ["T2s"])
                        if c == 1:
                            tt(T1s[:, :, 7:23], T2s[:, :, 7:23], T2s[:, :, 3:19], ALU.add, ["T2s", "T1s"], ["T1s"])
                            tt(T2s[:, :, 15:23], T1s[:, :, 15:23], T1s[:, :, 7:15], ALU.add, ["T1s", "T2s"], ["T2s"])
                        pls = PL[:, c, 1024:1152].rearrange("p (j t) -> p j t", t=8)
                        for (lo, hi, Tw) in ((0, 64, T1s), (64, 128, T2s)):
                            stt(pls[lo:hi], Tw[lo:hi, :, 15:23], iw[lo:hi], PBs[lo:hi, c, :, 15:23], ALU.mult, ALU.subtract,
                                ["T1s", "T2s", sbn, "CST"], ["PLs.%d" % c])

                def B3(c):
                    pb = "PBp.%d" % c
                    for (gt, c0, lc0, w) in tiles:
                        ps, psn = newps()
                        mm(ps[:, 0:w], PWB[:, l, c, :], PL[:, c, lc0:lc0 + w], True, True,
                           ["PWB", "PL.%d" % c, "PLs.%d" % c], [psn])
                        act(MIX[:, 2 + c, lc0:lc0 + w], ps[:, 0:w], AF.Identity, [psn, "PAR"], ["MIX.%d" % (2 + c)],
                            scale=pc(P_PS + l * 2 + c))
                    if b == 0:
                        vcp(CARB[:, c, :], PBp[:, c, 1024:1039], [pb], ["CARB"])
                    else:
                        vcp(OPB[:, c, :], PBp[:, c, 1024:1039], [pb], ["OPB.%d" % c])
                        dma_out(pool_pT[:, l, c, :], OPB[:, c, :], ["OPB.%d" % c], "cpB%d" % c, defer=-1)
                        vcp(STGB[:, c], PBs[:, c, :, 8:23], ["PBs.%d" % c], ["STGB.%d" % c])
                        dma_out(pool_sT[:, l, c], STGB[:, c], ["STGB.%d" % c], "csB%d" % c, defer=-1)
                        if l == 0:
                            dma_in(STGB[:, c], spool[:, 1, c], ["STGB.%d" % c], "sb%d" % c, defer=-1)

                A1(0); B1(0); A2(0); B2(0); A1(1); A3(0); B1(1); B3(0); A2(1); B2(1); A3(1); B3(1)
                phase()

                phase()
                S.barrier()
                off = 0
                UDp, off = carve(off, [2, 1054], BF16)
                UDT, off = carve(off, [2, 30], F32)
                UDs32, off = carve(off, [2, 16, 38], F32)
                UDs, off = carve(off, [2, 16, 38], BF16)
                DIAG, off = carve(off, [2, 31, 128], BF16)
                Z2, off = carve(off, [2, 2, 512], F32)
                ZB, off = carve(off, [4, 512], BF16)
                SD3, off = carve(off, [2, 512], F32)
                SIG = SD3[:, 0]
                D1, off = carve(off, [512], F32)
                D2, off = carve(off, [512], F32)
                for kk in range(31):
                    ts(DIAG[:, 0, kk, :], CST[:, C_ID:C_ID + 128], pc(P_DW + (l * 2 + 0) * 31 + kk), None, ALU.mult, None,
                       ["CST", "PAR"], ["DIAG.%d.%d" % (0, kk)])
                    act(DIAG[:, 1, kk, :], CST[:, C_ID:C_ID + 128], AF.Copy, ["CST", "PAR"], ["DIAG.%d.%d" % (1, kk)],
                        scale=pc(P_DW + (l * 2 + 1) * 31 + kk))
                for c in range(2):
                    if b == 0:
                        vms(UDp[:, c, 0:30], 0.0, ["UDp.%d" % c])
                    else:
                        vcp(UDp[:, c, 0:30], CARD[:, c, :], ["CARD"], ["UDp.%d" % c])
                        vcp(UDs32[:, c, :, 0:30], STGD[:, c], ["STGD.%d" % c], ["UDs32.%d" % c])
                    u_g, n_g = win(18 + c)
                    u_a, n_a = win(16 + c)
                    for tl in tiles:
                        (gt, c0, lc0, w) = tl
                        proj(u_g, n_g, [tl], lambda gt, c0, lc0, w, ps, psn: act(SIG[:, 0:w], ps[:, 0:w], AF.Sigmoid, [psn], ["SIG"]))

                        def ev_a(gt, c0, lc0, w, ps, psn, c=c):
                            if gt != 4:
                                tt(UDp[:, c, 30 + lc0:30 + lc0 + w], ps[:, 0:w], SIG[:, 0:w], ALU.mult, [psn, "SIG"], ["UDp.%d" % c])
                                if gt == 3:
                                    tt(UDT[:, c, :], ps[:, 482:512], SIG[:, 482:512], ALU.mult, [psn, "SIG"], ["UDT.%d" % c])
                            else:
                                tt(UDs32[:, c, :, 30:38], ps[:, 0:128].rearrange("p (j t) -> p j t", t=8),
                                   SIG[:, 0:128].rearrange("p (j t) -> p j t", t=8), ALU.mult, [psn, "SIG"], ["UDs32.%d" % c])

                        proj(u_a, n_a, [tl], ev_a)
                    if has_s:
                        vcp(UDs[:, c], UDs32[:, c], ["UDs32.%d" % c], ["UDs.%d" % c])
                    if b == 0:
                        vcp(CARD[:, c, :], UDp[:, c, 1024:1054], ["UDp.%d" % c], ["CARD"])
                    else:
                        vcp(OPD[:, c, :], UDT[:, c, :], ["UDT.%d" % c], ["OPD.%d" % c])
                        dma_out(conf_pT[:, l, c, :], OPD[:, c, :], ["OPD.%d" % c], "cpD%d" % c, defer=-1)
                        vcp(STGD[:, c], UDs32[:, c, :, 8:38], ["UDs32.%d" % c], ["STGD.%d" % c])
                        dma_out(conf_sT[:, l, c], STGD[:, c], ["STGD.%d" % c], "csD%d" % c, defer=-1)
                        if l == 0:
                            dma_in(STGD[:, c], sconf[:, 1, c], ["STGD.%d" % c], "sd%d" % c, defer=-1)

                def d_conv(ti):
                    (gt, c0, lc0, w) = tiles[ti]
                    zi = ti % 2
                    for c in range(2):
                        ps, psn = newps()
                        for kk in range(31):
                            if gt != 4:
                                rhs = UDp[:, c, lc0 + kk:lc0 + kk + w]
                                rb = "UDp.%d" % c
                            else:
                                rhs = UDs[:, c, :, kk:kk + 8]
                                rb = "UDs.%d" % c
                            mm(ps[:, 0:w], DIAG[:, c, kk, :], rhs, kk == 0, kk == 30, ["DIAG.%d.%d" % (c, kk), rb], [psn])
                        act(Z2[:, zi, c, 0:w], ps[:, 0:w], AF.Identity, [psn, "PAR"], ["Z.%d.%d" % (c, zi)],
                            bias=pc(P_CB + l * 2 + c))

                def d_ln(ti):
                    (gt, c0, lc0, w) = tiles[ti]
                    zi = ti % 2
                    Zt = Z2[:, zi, :, 0:w]
                    zn = ["Z.0.%d" % zi, "Z.1.%d" % zi]
                    act(ZB[:, 0:2, 0:w], Zt, AF.Copy, zn, ["ZB.0", "ZB.1"])
                    act(ZB[:, 2:4, 0:w], Zt, AF.Square, zn, ["ZB.2", "ZB.3"])
                    ps1, pn1 = newps()
                    ps2, pn2 = newps()
                    for c in range(2):
                        mm(ps1[:, 0:w], ONEB[:], ZB[:, c, 0:w], c == 0, c == 1, ["ONEB", "ZB.%d" % c], [pn1])
                    for c in range(2):
                        mm(ps2[:, 0:w], ONEB[:], ZB[:, 2 + c, 0:w], c == 0, c == 1, ["ONEB", "ZB.%d" % (2 + c)], [pn2])
                    act(D1[:, 0:w], ps1[:, 0:w], AF.Identity, [pn1], ["D1"], scale=1.0 / 256.0)
                    tt(D2[:, 0:w], D1[:, 0:w], D1[:, 0:w], ALU.mult, ["D1"], ["D2"])
                    stt(D2[:, 0:w], ps2[:, 0:w], 1.0 / 256.0, D2[:, 0:w], ALU.mult, ALU.subtract, [pn2, "D2"], ["D2"])
                    act(D2[:, 0:w], D2[:, 0:w], AF.Ln, ["D2", "EPSC"], ["D2"], bias=EPSC[:, 0:1])
                    act(D2[:, 0:w], D2[:, 0:w], AF.Exp, ["D2"], ["D2"], scale=-0.5)
                    tt(SD3[:, :, 0:w], Zt, D1[:, 0:w].unsqueeze(1).broadcast_to([128, 2, w]), ALU.subtract,
                       zn + ["D1"], ["SIG", "D3"])
                    tt(SD3[:, :, 0:w], SD3[:, :, 0:w], D2[:, 0:w].unsqueeze(1).broadcast_to([128, 2, w]), ALU.mult,
                       ["SIG", "D3", "D2"], ["SIG", "D3"])
                    for c in range(2):
                        act(MIX[:, 6 + c, lc0:lc0 + w], SD3[:, c, 0:w], AF.Silu, ["SIG", "D3", "PAR"], ["MIX.%d" % (6 + c)],
                            scale=pc(P_LG + l * 2 + c), bias=pc(P_LBI + l * 2 + c))

                nt_ = len(tiles)
                d_conv(0)
                for ti in range(nt_):
                    if ti + 1 < nt_:
                        d_conv(ti + 1)
                    d_ln(ti)

                phase()
                S.barrier()
                off = 0
                QT, off = carve(off, [2, 1152], BF16)
                KTt, off = carve(off, [2, 1152], BF16)
                O, off = carve(off, [2, 1152], F32)
                EBEp, off = carve(off, [2, 16], F32)
                EBEs, off = carve(off, [2, 16], F32)
                STt2, off = carve(off, [2, 8, 2, 64], F32)
                off_dead = off
                VP, off = carve(off, [2, 2, 2, 128], BF16)
                KP, off = carve(off, [2, 2, 2, 128], BF16)
                AT, off = carve(off, [2, 4, 64], BF16)
                S0BD, off = carve(off, [8, 2, 128], BF16)
                VBLK, off = carve(off, [2, 8, 64], BF16)
                off3 = off
                TF2, off = carve(off, [2, 1152], F32)
                TE_a, off = carve(off, [512], F32)
                uq = [win(8 + p) for p in range(2)]
                uf = [win(10 + p) for p in range(2)]
                for p in range(2):
                    proj(uq[p][0], uq[p][1], tiles, lambda gt, c0, lc0, w, ps, psn: act(O[:, p, lc0:lc0 + w], ps[:, 0:w], AF.Silu, [psn],
                                                                                      ["OQ.%d.%d" % (p, gt)]))
                for p in range(2):
                    proj(uf[p][0], uf[p][1], tiles, lambda gt, c0, lc0, w, ps, psn: act(TF2[:, p, lc0:lc0 + w], ps[:, 0:w], AF.Sigmoid, [psn],
                                                                                      ["TF.%d.%d" % (p, hx) for hx in range(lc0 // 256, (lc0 + w + 255) // 256)]))
                items = []
                for (gt_, c0_, lc0_, w_) in tiles:
                    halves = [(lc0_, 256), (lc0_ + 256, 256)] if w_ == 512 else [(lc0_, w_)]
                    for (hl, hw) in halves:
                        for p_ in range(2):
                            items.append(((gt_, c0_ + (hl - lc0_), hl, hw), p_))

                def p1_stage1(it):
                    (gt, c0, lc0, w), p = items[it]
                    TLB = TMPB if it % 2 == 0 else TMPC
                    tln, tfn = ("TMPB" if it % 2 == 0 else "TMPC"), "TF.%d.%d" % (p, lc0 // 256)
                    TFt = TF2[:, p, lc0:lc0 + w]
                    ts(TFt, TFt, OML[:, l, p:p + 1], LOW[:, l, p:p + 1], ALU.mult, ALU.add, [tfn, "OML", "LOW"], [tfn])
                    ts(TLB[:, 0:w], TFt, F_MIN, None, ALU.max, None, [tfn], [tln])
                    act(TLB[:, 0:w], TLB[:, 0:w], AF.Ln, [tln], [tln])

                def p1_stage2(it):
                    (gt, c0, lc0, w), p = items[it]
                    TLB = TMPB if it % 2 == 0 else TMPC
                    TE = TE_a if it % 2 == 0 else TMPA
                    tln, ten, tfn = ("TMPB" if it % 2 == 0 else "TMPC"), ("TE_a" if it % 2 == 0 else "TMPA"), "TF.%d.%d" % (p, lc0 // 256)
                    TFt = TF2[:, p, lc0:lc0 + w]
                    m0 = CST[:, C_M0P:C_M0P + w] if gt != 4 else CST[:, C_M0S:C_M0S + 128]
                    S.op("dve", lambda: nc.vector.tensor_tensor_scan(
                        out=TLB[:, 0:w], data0=m0, data1=TLB[:, 0:w], initial=0.0, op0=ALU.mult, op1=ALU.add),
                         [tln, "CST"], [tln])
                    act(TE[:, 0:w], TLB[:, 0:w], AF.Exp, [tln], [ten])
                    tt(QT[:, p, lc0:lc0 + w], O[:, p, lc0:lc0 + w], TE[:, 0:w], ALU.mult, ["OQ.%d.%d" % (p, gt), ten], ["QT.%d.%d" % (p, lc0 // 256)])
                    if gt != 4:
                        ch0 = lc0 // 64
                        vcp(EBEp[:, p, ch0:ch0 + w // 64], TE[:, 0:w].rearrange("p (a b) -> p a b", b=64)[:, :, 63], [ten], ["EBEp"])
                    else:
                        vcp(EBEs[:, p, :], TE[:, 0:128].rearrange("p (a b) -> p a b", b=8)[:, :, 7], [ten], ["EBEs"])
                    act(TE[:, 0:w], TLB[:, 0:w], AF.Exp, [tln, ten], [ten], scale=-1.0)
                    ts(TFt, TFt, -1.0, 1.0, ALU.mult, ALU.add, [tfn], [tfn])
                    tt(KTt[:, p, lc0:lc0 + w], TFt, TE[:, 0:w], ALU.mult, [tfn, ten], ["KTt.%d.%d" % (p, lc0 // 256)])

                p1_calls = [(p1_stage1, 0, None)]
                for it in range(1, len(items)):
                    p1_calls.append((p1_stage1, it, None))
                    p1_calls.append((p1_stage2, it - 1, it - 1))
                p1_calls.append((p1_stage2, len(items) - 1, len(items) - 1))
                p1_state = {"pos": 0, "done": -1}

                def p1_emit_one():
                    if p1_state["pos"] >= len(p1_calls):
                        return False
                    fn, arg, done = p1_calls[p1_state["pos"]]
                    p1_state["pos"] += 1
                    fn(arg)
                    if done is not None:
                        p1_state["done"] = done
                    return True

                def p1_flush_tile(hidx):
                    while p1_state["done"] < 2 * hidx + 1:
                        assert p1_emit_one()
                if st["w"] % NSLOT == NSLOT - 1:
                    st["w"] += 1
                vslot = st["w"] % NSLOT
                u_i0, n_i0 = win(12)
                u_i1, n_i1 = win(13)
                vms(VP[:].rearrange("p a b c d -> p (a b c d)"), 0.0, ["VP0", "VP1"])
                vms(KP[:].rearrange("p a b c d -> p (a b c d)"), 0.0, ["KP0", "KP1"])
                vms(S0BD[:].rearrange("p a b c -> p (a b c)"), 0.0, ["S0BD"])
                if has_s:
                    for sc_ in range(2):
                        dma_in(STt2[:, sc_], shgrn[:, l, 8 * sc_:8 * sc_ + 8, :, :], ["STt%d" % sc_], "sh%d" % sc_)
                if b == 0:
                    vms(SS[:].rearrange("p a b -> p (a b)"), 0.0, ["SS"])
                    vms(SBD[:].rearrange("p a b -> p (a b)"), 0.0, ["SBD"])
                nch = TB // 64
                st["held"] = {5, 6}

                def stageA(ch):
                        lcol = ch * 64
                        is_s = ch >= 16
                        gt = tiles[min(lcol // 512, len(tiles) - 1)][0]
                        e = (ch // 2) % 2
                        hf = ch % 2
                        r0, r1 = hf * 64, hf * 64 + 64
                        vpn, kpn, atn = "VP%d" % e, "KP%d" % e, "AT%d.%d" % (e, hf)
                        if hf == 0:
                            psv, pvn = newps()
                            for k in range(8):
                                mm(psv[:, 0:256], H[:, k, lcol:lcol + 128], WR[:, vslot:vslot + 2, k * 128:(k + 1) * 128],
                                   k == 0, k == 7, [n_i0, n_i1, "H.%d.%d" % (k, gt)], [pvn])
                            for h in range(2):
                                act(VP[:, e, :, h, h * 64:(h + 1) * 64],
                                    psv[:, 0:256].rearrange("s (p h v) -> s p h v", p=2, h=2)[:, :, h, :], AF.Copy, [pvn], [vpn])
                            for p in range(2):
                                tr(PSB[:, p * 128:(p + 1) * 128], KTt[:, p, lcol:lcol + 128], IDB[:], ["KTt.%d.%d" % (p, lcol // 256), "IDB"], ["psb"])
                            for h in range(2):
                                act(KP[:, e, :, h, h * 64:(h + 1) * 64],
                                    PSB[:, 0:256].rearrange("s (p h v) -> s p h v", p=2, h=2)[:, :, h, :], AF.Copy, ["psb"], [kpn])
                        for h in range(2):
                            for p in range(2):
                                mm(PS_all[r0:r1, 5 + h, p * 64:(p + 1) * 64], KTt[h * 64:(h + 1) * 64, p, lcol:lcol + 64],
                                   QT[h * 64:(h + 1) * 64, p, lcol:lcol + 64], True, True, ["KTt.%d.%d" % (p, lcol // 256), "QT.%d.%d" % (p, lcol // 256)], ["ps%d" % (5 + h)])
                        mk = CST[r0:r1, C_MS:C_MS + 64] if is_s else CST[r0:r1, C_MC:C_MC + 64]
                        tt(AT[r0:r1, e].rearrange("s (p h) t -> s h p t", h=2),
                           PS_all[r0:r1, 5:7, 0:128].rearrange("s h (p t) -> s h p t", p=2),
                           mk.unsqueeze(1).unsqueeze(1).broadcast_to([64, 2, 2, 64]), ALU.mult, ["ps5", "ps6", "CST"], [atn])

                def stageB(ch):
                        lcol = ch * 64
                        is_s = ch >= 16
                        gt = tiles[min(lcol // 512, len(tiles) - 1)][0]
                        e = (ch // 2) % 2
                        hf = ch % 2
                        r0, r1 = hf * 64, hf * 64 + 64
                        vpn, kpn, atn = "VP%d" % e, "KP%d" % e, "AT%d.%d" % (e, hf)
                        if not is_s:
                            pso, pon = newps()
                            for p in range(2):
                                for h in range(2):
                                    mm(pso[:, p * 64:(p + 1) * 64], VP[r0:r1, e, p, h, :], AT[r0:r1, e, p * 2 + h, :], h == 0, False,
                                       [vpn, atn], [pon])
                                mm(pso[:, p * 64:(p + 1) * 64], SBD[:, p, :], QT[:, p, lcol:lcol + 64], False, True,
                                   ["SBD", "QT.%d.%d" % (p, lcol // 256)], [pon])
                            act(O[:, :, lcol:lcol + 64], pso[:, 0:128].rearrange("p (a b) -> p a b", a=2), AF.Copy, [pon],
                                ["O.%d" % gt, "OQ.0.%d" % gt, "OQ.1.%d" % gt])
                            psu, pun = newps()
                            for p in range(2):
                                for h in range(2):
                                    mm(psu[:, p * 64:(p + 1) * 64], KP[r0:r1, e, p, h, :], VP[r0:r1, e, p, h, h * 64:(h + 1) * 64],
                                       h == 0, h == 1, [kpn, vpn], [pun])
                            tt(SS[:], psu[:, 0:128].rearrange("p (a b) -> p a b", a=2), SS[:], ALU.add, [pun, "SS"], ["SS"])
                            tt(SS[:], SS[:], EBEp[:, :, ch:ch + 1].broadcast_to([128, 2, 64]), ALU.mult, ["SS", "EBEp"], ["SS"])
                            tt(SBD[:].rearrange("q p (h v) -> q p h v", h=2), SS[:].unsqueeze(2).broadcast_to([128, 2, 2, 64]),
                               CST[:, C_OB:C_OB + 128].rearrange("q (h v) -> q h v", h=2).unsqueeze(1).broadcast_to([128, 2, 2, 64]),
                               ALU.mult, ["SS", "CST"], ["SBD"])
                            if b == 1 and ch == 15:
                                dma_out(hgrn_pT[:, l, :, :], SS[:], ["SS"], "hp", defer=-1)
                        else:
                            sc = ch - 16
                            STt = STt2[:, sc]
                            stn = "STt%d" % sc
                            for h in range(2):
                                vcp(S0BD[h * 64:(h + 1) * 64, :, :, h * 64:(h + 1) * 64], STt[h * 64:(h + 1) * 64, :, :, :],
                                    [stn], ["S0BD"])
                            for p in range(2):
                                pso, pon = newps()
                                for h in range(2):
                                    mm(pso[:, 0:64], VP[r0:r1, e, p, h, :], AT[r0:r1, e, p * 2 + h, :], h == 0, False, [vpn, atn], [pon])
                                for j in range(8):
                                    mm(pso[:, 8 * j:8 * j + 8], S0BD[:, j, p, :], QT[:, p, lcol + 8 * j:lcol + 8 * j + 8], False, j == 7,
                                       ["S0BD", "QT.%d.%d" % (p, lcol // 256)], [pon])
                                act(O[:, p, lcol:lcol + 64], pso[:, 0:64], AF.Copy, [pon], ["O.%d" % gt, "OQ.0.%d" % gt, "OQ.1.%d" % gt])
                                for h in range(2):
                                    tt(VBLK[r0:r1, h, :, :], VP[r0:r1, e, p, h, h * 64:(h + 1) * 64].unsqueeze(1).broadcast_to([64, 8, 64]),
                                       CST[r0:r1, C_MJ:C_MJ + 8].unsqueeze(2).broadcast_to([64, 8, 64]), ALU.mult, [vpn, "CST"], ["VBLK"])
                                psu, pun = newps()
                                for h in range(2):
                                    mm(psu[:, 0:512], KP[r0:r1, e, p, h, :], VBLK[r0:r1, h].rearrange("p a b -> p (a b)"), h == 0, h == 1,
                                       [kpn, "VBLK"], [pun])
                                tt(STt[:, :, p, :], psu[:, 0:512].rearrange("p (a b) -> p a b", a=8), STt[:, :, p, :], ALU.add,
                                   [pun, stn, "S0BD"], [stn])
                                tt(STt[:, :, p, :], STt[:, :, p, :], EBEs[:, p, 8 * sc:8 * sc + 8].unsqueeze(2).broadcast_to([128, 8, 64]),
                                   ALU.mult, [stn, "EBEs"], [stn])
                            dma_out(hgrn_sT[:, l, 8 * sc:8 * sc + 8, :, :], STt, [stn], "hs%d" % sc, defer=1)

                def tix(ch):
                    return ch // 4

                p1_flush_tile(0)
                stageA(0)
                for ch in range(nch):
                    if ch + 1 < nch:
                        p1_flush_tile(tix(ch + 1))
                        stageA(ch + 1)
                    stageB(ch)
                    p1_emit_one()
                while p1_emit_one():
                    pass
                st["held"] = set()
                phase()
                S.barrier()
                SG, _ = carve(0, [2, 512], F32)
                off = off_dead
                OSQ2, off = carve(off, [2, 512], BF16)
                N12, off = carve(off, [2, 512], F32)
                N2a, _ = carve(4608, [512], F32)
                assert off <= 34304
                MO, _ = carve(34304, [8, 512], F32)
                ug = [win(14), win(15)]
                phase()
                wo_units = [wget(w_out_u[l, m]) for m in range(8)]

                def c3_tile(tl):
                    (gt, c0, lc0, w) = tl
                    for p in range(2):
                        proj(ug[p][0], ug[p][1], [tl], lambda gt, c0, lc0, w, ps, psn: act(SG[:, p, 0:w], ps[:, 0:w], AF.Silu, [psn],
                                                                                         ["SG.%d" % p]))
                    pss = []
                    for p in range(2):
                        OSQ = OSQ2[:, p]
                        on = "OSQ%d" % p
                        act(OSQ[:, 0:w], O[:, p, lc0:lc0 + w], AF.Square, ["O.%d" % gt], [on])
                        ps, psn = newps()
                        mm(ps[:, 0:w], OBB[:], OSQ[:, 0:w], True, True, ["OBB", on], [psn])
                        pss.append((ps, psn))
                    for p in range(2):
                        N1 = N12[:, p]
                        n1n = "N1%d" % p
                        ps, psn = pss[p]
                        act(N1[:, 0:w], ps[:, 0:w], AF.Ln, [psn, "EPSC"], [n1n], scale=1.0 / 64.0, bias=EPSC[:, 0:1])
                        act(N1[:, 0:w], N1[:, 0:w], AF.Exp, [n1n], [n1n], scale=-0.5)
                        stt(N2a[:, 0:w], O[:, p, lc0:lc0 + w], pc(P_HN + l * 2 + p), N1[:, 0:w], ALU.mult, ALU.mult,
                            ["O.%d" % gt, n1n, "PAR"], ["N2a"])
                        tt(MIX[:, 4 + p, lc0:lc0 + w], N2a[:, 0:w], SG[:, p, 0:w], ALU.mult, ["N2a", "SG.%d" % p],
                           ["MIX.%d.%d" % (4 + p, gt)])

                def wout_tile(tl):
                    (gt, c0, lc0, w) = tl
                    korder = (0, 1, 2, 3, 6, 7, 4, 5)
                    for m in range(8):
                        u_o, n_o = wo_units[m]
                        ps, psn = newps()
                        for i_, k in enumerate(korder):
                            kn = "MIX.%d.%d" % (k, gt) if k in (4, 5) else "MIX.%d" % k
                            mm(ps[:, 0:w], u_o[:, k * 128:(k + 1) * 128], MIX[:, k, lc0:lc0 + w], i_ == 0, i_ == 7, [n_o, kn], [psn])
                        act(MO[:, m, 0:w], ps[:, 0:w], AF.Copy, [psn], ["MO.%d" % m])
                        act(SQ[:, m, 0:w], ps[:, 0:w], AF.Square, [psn], ["SQ.%d" % m])

                def resid_stats(tl):
                    (gt, c0, lc0, w) = tl
                    rms_stats([MO[:, k, 0:w] for k in range(8)], ["MO.%d" % k for k in range(8)], lc0, w, str(gt), presq=True)

                def resid_apply(tl):
                    (gt, c0, lc0, w) = tl
                    for k in range(8):
                        stt(TMPB[:, 0:w], MO[:, k, 0:w], pc(P_G2 + l * 8 + k), RSTD[:, lc0:lc0 + w], ALU.mult, ALU.mult,
                            ["MO.%d" % k, "RSTD.%d" % (lc0 // 512), "PAR"], ["TMPB"])
                        tt(X[:, k, c0:c0 + w], X[:, k, c0:c0 + w], TMPB[:, 0:w], ALU.add, [xb(k, gt), "TMPB"], [xb(k, gt)])

                nt_ = len(tiles)
                c3_tile(tiles[0])
                for i in range(nt_):
                    wout_tile(tiles[i])
                    resid_stats(tiles[i])
                    if i + 1 < nt_:
                        c3_tile(tiles[i + 1])
                    resid_apply(tiles[i])
                    if i >= 1:
                        make_h([tiles[i - 1]], P_G3 + l * 8)
                make_h([tiles[nt_ - 1]], P_G3 + l * 8)

                S.barrier()
                ACTB, _ = carve(0, [22, 1152], BF16)
                ffc = [0]

                def gate_up(j, tl, u_g, n_g, u_u, n_u):
                    (gt, c0, lc0, w) = tl
                    tmp = TMPB if (ffc[0] % 2 == 0) else TMPC
                    tn = "TMPB" if (ffc[0] % 2 == 0) else "TMPC"
                    ffc[0] += 1
                    proj(u_g, n_g, [tl], lambda gt, c0, lc0, w, ps, psn: act(tmp[:, 0:w], ps[:, 0:w], AF.Silu, [psn], [tn]))
                    proj(u_u, n_u, [tl], lambda gt, c0, lc0, w, ps, psn: tt(ACTB[:, j, lc0:lc0 + w], ps[:, 0:w], tmp[:, 0:w], ALU.mult,
                                                                             [psn, tn], ["ACTB.%d.%d" % (j, gt)]))

                J0 = 4
                first = [(wget(w_gate_u[l, j]), wget(w_up_u[l, j])) for j in range(J0)]
                for tl in tiles:
                    for j in range(J0):
                        (u_g, n_g), (u_u, n_u) = first[j]
                        gate_up(j, tl, u_g, n_g, u_u, n_u)
                for j in range(J0, 22):
                    u_g, n_g = wget(w_gate_u[l, j])
                    u_u, n_u = wget(w_up_u[l, j])
                    tmps = {}

                    def ev_g(gt, c0, lc0, w, ps, psn):
                        tmp, tn = (TMPB, "TMPB") if (ffc[0] % 2 == 0) else (TMPC, "TMPC")
                        ffc[0] += 1
                        tmps[gt] = (tmp, tn)
                        act(tmp[:, 0:w], ps[:, 0:w], AF.Silu, [psn], [tn])

                    def ev_u(gt, c0, lc0, w, ps, psn, j=j):
                        tmp, tn = tmps[gt]
                        tt(ACTB[:, j, lc0:lc0 + w], ps[:, 0:w], tmp[:, 0:w], ALU.mult, [psn, tn], ["ACTB.%d.%d" % (j, gt)])

                    for g0 in range(0, len(tiles), 2):
                        grp = tiles[g0:g0 + 2]
                        proj_k(u_g, n_g, grp, ev_g)
                        proj_k(u_u, n_u, grp, ev_u)
                phase()
                def down_pieces(m):
                    return [wget(w_down_u[l, m, :, k0 * 128:(k0 + nk) * 128], ncols=nk * 128) for (k0, nk) in ((0, 8), (8, 8), (16, 6))]

                def down_group(m, tl, pieces):
                    (gt, c0, lc0, w) = tl
                    ps, psn = newps()
                    for j in range(22):
                        pa, pn_ = pieces[j // 8]
                        jj = j % 8
                        mm(ps[:, 0:w], pa[:, jj * 128:(jj + 1) * 128], ACTB[:, j, lc0:lc0 + w], j == 0, j == 21,
                           [pn_, "ACTB.%d.%d" % (j, gt)], [psn])
                    act(FF[:, m, lc0:lc0 + w], ps[:, 0:w], AF.Copy, [psn], ["FF.%d.%d" % (m, gt)])

                for m in range(5):
                    pcs = down_pieces(m)
                    for tl in tiles:
                        down_group(m, tl, pcs)
                last = {m: down_pieces(m) for m in (5, 6, 7)}
                for m in (5, 6, 7):
                    down_group(m, tiles[0], last[m])
                for i in range(1, len(tiles)):
                    down_group(5, tiles[i], last[5])
                    add_residual(FF, "FF", [tiles[i - 1]], P_G4 + l * 8)
                    down_group(6, tiles[i], last[6])
                    down_group(7, tiles[i], last[7])
                add_residual(FF, "FF", [tiles[-1]], P_G4 + l * 8)
                if l == 1:
                    for (gt, c0, lc0, w) in tiles:
                        for k in range(8):
                            dma_out(yT[:, k, c0:c0 + w], X[:, k, c0:c0 + w], [xb(k, gt)], "y")
        try:
            body()
        except _StopBuild:
            S.barrier()
            for k in range(8):
                dma_out(yT[:, k, :], X[:, k, :], [xb(k, t) for t in range(5)], "y")
        S.emit(final_wait_chans=[c for c in S.chan_cnt if c.startswith("o_")])
    return nc


def _consts():
    c = np.zeros((128, NCST), np.float32)
    c[:, C_ID:C_ID + 128] = np.eye(128, dtype=np.float32)
    m = np.ones(512, np.float32)
    m[0::64] = 0.0
    c[:, C_M0P:C_M0P + 512] = m[None]
    m = np.ones(128, np.float32)
    m[0::8] = 0.0
    c[:, C_M0S:C_M0S + 128] = m[None]
    s = np.arange(64)[:, None]
    t = np.arange(64)[None, :]
    for r_ in (0, 64):
        c[r_:r_ + 64, C_MC:C_MC + 64] = (s <= t).astype(np.float32)
        c[r_:r_ + 64, C_MS:C_MS + 64] = ((s <= t) & (s // 8 == t // 8)).astype(np.float32)
        c[r_:r_ + 64, C_MJ:C_MJ + 8] = (np.arange(64)[:, None] // 8 == np.arange(8)[None, :]).astype(np.float32)
    pp = np.arange(128)
    c[:, C_OB:C_OB + 128] = (pp[:, None] // 64 == pp[None, :] // 64).astype(np.float32)
    wins = (2, 4, 8, 16)
    for ch in range(2):
        for half in range(2):
            w = wins[ch * 2 + half]
            cnt = np.minimum(np.arange(16) + 1, w).astype(np.float32)
            c[half * 64:(half + 1) * 64, C_RC + 16 * ch:C_RC + 16 * ch + 16] = (1.0 / cnt)[None]
            c[half * 64:(half + 1) * 64, C_IW + ch] = 1.0 / w
    return c


def _fm(v):
    v = np.asarray(v, np.float32)
    lead = v.shape[:-1]
    n = v.shape[-1] // 128
    return np.moveaxis(v.reshape(lead + (n, 128)), -1, 0)


_PROG = {}


def kernel(x_prompt, x_sample, state_conv, state_pool, state_hgrn, state_conf,
           norm_mix_pre, norm_mix_post, w_in, conv_w, pool_w, pool_scale, hgrn_lb, hgrn_norm,
           conf_dw, conf_b, conf_ln_g, conf_ln_b, w_out, norm_ffn_pre, norm_ffn_post,
           w_gate, w_up, w_down):
    f = lambda a: np.ascontiguousarray(np.asarray(a, dtype=np.float32))
    x_prompt, x_sample = f(x_prompt), f(x_sample)
    par = np.zeros((128, NPAR), np.float32)
    for col, arr in ((P_G1, norm_mix_pre), (P_G2, norm_mix_post), (P_G3, norm_ffn_pre), (P_G4, norm_ffn_post)):
        par[:, col:col + 16] = _fm(arr).reshape(128, 16)
    par[:, P_CW:P_CW + 12] = np.transpose(_fm(conv_w), (0, 1, 3, 2)).reshape(128, 12)
    for col, arr in ((P_PS, pool_scale), (P_LB, hgrn_lb), (P_HN, hgrn_norm), (P_CB, conf_b), (P_LG, conf_ln_g), (P_LBI, conf_ln_b)):
        par[:, col:col + 4] = _fm(arr).reshape(128, 4)
    par[:, P_DW:P_DW + 124] = np.transpose(_fm(conf_dw), (0, 1, 3, 2)).reshape(128, 124)
    cst = _consts()
    pw = f(pool_w)
    pwbd = np.zeros((128, 2, 2, 128), np.float32)
    for l in range(2):
        for g in range(4):
            c, hh = g // 2, g % 2
            pwbd[hh * 64:(hh + 1) * 64, l, c, hh * 64:(hh + 1) * 64] = pw[l, g]

    def units(w, nu):
        w = f(w)
        L, KK, NN = w.shape
        K = KK // 128
        return np.ascontiguousarray(w.reshape(L, K, 128, nu, 128).transpose(0, 3, 2, 1, 4).reshape(L, nu, 128, K * 128))

    w_in_u = units(w_in, 20)
    w_out_u = units(w_out, 8)
    w_gate_u = units(w_gate, 22)
    w_up_u = units(w_up, 22)
    w_down_u = units(w_down, 8)
    sc_, sp_, sh_, sf_ = f(state_conv), f(state_pool), f(state_hgrn), f(state_conf)

    in_maps = []
    for b in range(8):
        tok = np.concatenate([x_prompt[b], x_sample[16 * b:16 * b + 16].reshape(128, 1024)], axis=0)
        xT = np.ascontiguousarray(tok.reshape(NT, 8, 128).transpose(2, 1, 0))

        def st(a, R):
            return np.ascontiguousarray(a[16 * b:16 * b + 16].reshape(16, 2, R, 2, 128).transpose(4, 1, 3, 0, 2))

        hg = sh_[16 * b:16 * b + 16].reshape(16, 2, 2, 2, 64, 64).transpose(3, 4, 1, 0, 2, 5).reshape(128, 2, 16, 2, 64)
        in_maps.append({
            "xT": xT, "sconv": st(sc_, 2), "spool": st(sp_, 15), "sconf": st(sf_, 30), "shgrn": np.ascontiguousarray(hg),
            "w_in_u": w_in_u, "w_out_u": w_out_u, "w_gate_u": w_gate_u, "w_up_u": w_up_u, "w_down_u": w_down_u,
            "par": par, "cst": cst, "pwbd": pwbd,
        })
    if "nc" not in _PROG:
        import os
        stop = os.environ.get("MK_STOP")
        _PROG["nc"] = build_program(None if stop is None else int(stop))
    res = run_bass_kernel_spmd(_PROG["nc"], in_maps, core_ids=list(range(8)))
    R = res.results

    y_prompt = np.empty((8, 2048, 1024), np.float32)
    y_sample = np.empty((128, 8, 1024), np.float32)
    conv_p = np.empty((8, 2, 2, 256), np.float32)
    pool_p = np.empty((8, 2, 15, 256), np.float32)
    hgrn_p = np.empty((8, 2, 4, 64, 64), np.float32)
    conf_p = np.empty((8, 2, 30, 256), np.float32)
    conv_s = np.empty((128, 2, 2, 256), np.float32)
    pool_s = np.empty((128, 2, 15, 256), np.float32)
    hgrn_s = np.empty((128, 2, 4, 64, 64), np.float32)
    conf_s = np.empty((128, 2, 30, 256), np.float32)
    for b in range(8):
        r = R[b]
        tok = np.asarray(r["yT"]).transpose(2, 1, 0).reshape(NT, 1024)
        y_prompt[b] = tok[:2048]
        y_sample[16 * b:16 * b + 16] = tok[2048:].reshape(16, 8, 1024)
        for dst, key in ((conv_p, "conv_pT"), (pool_p, "pool_pT"), (conf_p, "conf_pT")):
            a = np.asarray(r[key])
            dst[b] = a.transpose(1, 3, 2, 0).reshape(2, a.shape[3], 256)
        hp = np.asarray(r["hgrn_pT"]).reshape(2, 64, 2, 2, 64)
        hgrn_p[b] = hp.transpose(2, 3, 0, 1, 4).reshape(2, 4, 64, 64)
        for dst, key in ((conv_s, "conv_sT"), (pool_s, "pool_sT"), (conf_s, "conf_sT")):
            a = np.asarray(r[key])
            dst[16 * b:16 * b + 16] = a.transpose(3, 1, 4, 2, 0).reshape(16, 2, a.shape[4], 256)
        hs = np.asarray(r["hgrn_sT"]).reshape(2, 64, 2, 16, 2, 64)
        hgrn_s[16 * b:16 * b + 16] = hs.transpose(3, 2, 4, 0, 1, 5).reshape(16, 2, 4, 64, 64)
    return (y_prompt, y_sample, conv_p, pool_p, hgrn_p, conf_p, conv_s, pool_s, hgrn_s, conf_s)
```

```python
import numpy as np
from contextlib import ExitStack
import concourse.bass as bass
import concourse.mybir as mybir
from concourse.bass_utils import run_bass_kernel_spmd

F32 = mybir.dt.float32
BF16 = mybir.dt.bfloat16
AF = mybir.ActivationFunctionType
ALU = mybir.AluOpType

NT = 2176
EPS = 1e-6
F_MIN = 1e-20
ENGS = ("pe", "act", "dve", "pool", "sp")


class Buf:
    __slots__ = ("name", "w", "r")

    def __init__(self, name):
        self.name = name
        self.w = None
        self.r = []


class Op:
    __slots__ = ("eng", "fn", "waits", "signal", "idx", "is_dma", "chan", "val", "known", "is_nop")


class Sched:
    def __init__(self, nc, self_sync=("act", "dve", "pool")):
        self.nc = nc
        self.prog = {e: [] for e in ENGS}
        self.known = {e: {} for e in ENGS}
        self.self_sync = set(self_sync)
        self.chan_cnt = {}
        self.bufs = {}
        self.last = {}
        self.dma_pending = []

    def buf(self, name):
        b = self.bufs.get(name)
        if b is None:
            b = Buf(name)
            self.bufs[name] = b
        return b

    def _norm(self, lst):
        out = []
        for x in lst:
            if x is None:
                continue
            if isinstance(x, str):
                out.append(self.buf(x))
            else:
                out.extend(self._norm(x))
        return out

    def _add(self, eng, fn, reads, writes, is_dma=False, chan=None, extra=(), defer=0):
        reads = self._norm(reads)
        writes = self._norm(writes)
        op = Op()
        op.eng = eng
        op.fn = fn
        op.is_dma = is_dma
        op.chan = chan
        op.signal = False
        op.is_nop = False
        op.idx = len(self.prog[eng])
        op.waits = []
        kn = self.known[eng]
        toks = list(extra)
        for b in reads:
            if b.w is not None:
                toks.append(b.w)
        for b in writes:
            if b.w is not None:
                toks.append(b.w)
            toks.extend(b.r)
        best = {}
        for t in toks:
            if t[0] not in best or best[t[0]][1] < t[1]:
                best[t[0]] = t
        for t in best.values():
            src, v, top = t
            if src == eng and not top.is_dma and eng not in self.self_sync:
                continue
            if kn.get(src, -1) >= v:
                continue
            kn[src] = v
            op.waits.append(t)
            top.signal = True
            if top.known is not None:
                for s2, v2 in top.known.items():
                    if kn.get(s2, -1) < v2:
                        kn[s2] = v2
        if is_dma:
            n = self.chan_cnt.get(chan, 0) + 1
            self.chan_cnt[chan] = n
            op.val = 16 * n
            tok = ("dma:" + chan, op.val, op)
            op.known = None
            if defer >= 0:
                self.dma_pending.append([tok, defer])
        else:
            op.val = op.idx
            tok = (eng, op.idx, op)
            op.known = dict(kn)
            self.last[eng] = tok
        for b in reads:
            b.r.append(tok)
        for b in writes:
            b.w = tok
            b.r = []
        self.prog[eng].append(op)
        return op

    def op(self, eng, fn, reads=(), writes=()):
        return self._add(eng, fn, reads, writes)

    def dma(self, eng, fn, reads=(), writes=(), chan="d", defer=0):
        return self._add(eng, fn, reads, writes, is_dma=True, chan=chan, defer=defer)

    def barrier(self, engs=("act", "dve", "sp")):
        nc = self.nc
        toks = [self.last[e] for e in ENGS if e in self.last] + [t for t, d in self.dma_pending if d == 0]
        self.dma_pending = [[t, d - 1] for t, d in self.dma_pending if d > 0]
        hand = {"pe": nc.tensor, "act": nc.scalar, "dve": nc.vector, "sp": nc.sync}
        saved = dict(self.last)
        for e in engs:
            o = self._add(e, (lambda h=hand[e]: h.nop()), (), (), extra=toks)
            o.is_nop = True
        self.last = saved

    def emit(self, final_wait_chans=()):
        nc = self.nc
        with ExitStack() as es:
            esem = {e: es.enter_context(nc.semaphore("s_" + e)) for e in ENGS}
            csem = {c: es.enter_context(nc.semaphore("c_" + c)) for c in self.chan_cnt}
            sigcnt = {}
            for e in ENGS:
                c = 0
                for op in self.prog[e]:
                    if not op.is_dma and op.signal:
                        c += 1
                        sigcnt[(e, op.idx)] = c
            block = es.enter_context(nc.Block())
            hand = {"pe": nc.tensor, "act": nc.scalar, "dve": nc.vector, "pool": nc.gpsimd, "sp": nc.sync}

            def run(e):
                h = hand[e]
                for op in self.prog[e]:
                    for (src, v, top) in op.waits:
                        if top.is_dma:
                            h.wait_ge(csem[top.chan], v)
                        else:
                            h.wait_ge(esem[src], sigcnt[(src, v)])
                    ins = op.fn()
                    if op.is_dma:
                        ins.then_inc(csem[op.chan], 16)
                    elif op.signal:
                        ins.then_inc(esem[e], 1)
                if e == "sp":
                    for c in final_wait_chans:
                        h.wait_ge(csem[c], 16 * self.chan_cnt[c])

            @block.tensor
            def _(eng):
                run("pe")

            @block.scalar
            def _(eng):
                run("act")

            @block.vector
            def _(eng):
                run("dve")

            @block.gpsimd
            def _(eng):
                run("pool")

            @block.sync
            def _(eng):
                run("sp")


C_ID = 0
C_M0P = 128
C_M0S = 640
C_MC = 768
C_MS = 832
C_MJ = 896
C_OB = 904
C_RC = 1032
C_IW = 1064
NCST = 1066

P_G1, P_G2, P_G3, P_G4 = 0, 16, 32, 48
P_CW = 64
P_PS = 76
P_LB = 80
P_HN = 84
P_CB = 88
P_LG = 92
P_LBI = 96
P_DW = 100
NPAR = 224

NSLOT = 10
BIG32 = 12672


class _StopBuild(Exception):
    pass


MARKS = []
MARKS_DVE = []
S_holder = [None]


def build_program(stop=None):
    nc = bass.Bass("TRN2", target_bir_lowering=False)
    phase_ctr = [0]

    def phase():
        MARKS.append(sum(1 for o in S_holder[0].prog["pe"] if not getattr(o, "is_nop", False)))
        MARKS_DVE.append(sum(1 for o in S_holder[0].prog["dve"] if not getattr(o, "is_nop", False)))
        phase_ctr[0] += 1
        if stop is not None and phase_ctr[0] > stop:
            raise _StopBuild()

    def din(name, shape):
        return nc.dram_tensor(name, shape, F32, kind="ExternalInput").ap()

    def dout(name, shape):
        return nc.dram_tensor(name, shape, F32, kind="ExternalOutput").ap()

    xT = din("xT", [128, 8, NT])
    sconv = din("sconv", [128, 2, 2, 16, 2])
    spool = din("spool", [128, 2, 2, 16, 15])
    sconf = din("sconf", [128, 2, 2, 16, 30])
    shgrn = din("shgrn", [128, 2, 16, 2, 64])
    w_in_u = din("w_in_u", [2, 20, 128, 1024])
    w_out_u = din("w_out_u", [2, 8, 128, 1024])
    w_gate_u = din("w_gate_u", [2, 22, 128, 1024])
    w_up_u = din("w_up_u", [2, 22, 128, 1024])
    w_down_u = din("w_down_u", [2, 8, 128, 2816])
    par_d = din("par", [128, NPAR])
    cst_d = din("cst", [128, NCST])
    pwbd_d = din("pwbd", [128, 2, 2, 128])

    yT = dout("yT", [128, 8, NT])
    conv_pT = dout("conv_pT", [128, 2, 2, 2])
    pool_pT = dout("pool_pT", [128, 2, 2, 15])
    conf_pT = dout("conf_pT", [128, 2, 2, 30])
    hgrn_pT = dout("hgrn_pT", [128, 2, 2, 64])
    conv_sT = dout("conv_sT", [128, 2, 2, 16, 2])
    pool_sT = dout("pool_sT", [128, 2, 2, 16, 15])
    conf_sT = dout("conf_sT", [128, 2, 2, 16, 30])
    hgrn_sT = dout("hgrn_sT", [128, 2, 16, 2, 64])

    with ExitStack() as es:
        def sb(name, shape, dt):
            return es.enter_context(nc.sbuf_tensor(name, shape, dt))

        X = sb("X", [128, 8, NT], F32)
        HM = sb("HM", [128, 2, 8, 1152], BF16)
        BIG = sb("BIG", [128, BIG32], F32)
        SQ = sb("SQ", [128, 8, 512], BF16)
        RSTD = sb("RSTD", [128, 1152], F32)
        TMPA = sb("TMPA", [128, 512], F32)
        TMPB = sb("TMPB", [128, 512], F32)
        TMPC = sb("TMPC", [128, 512], F32)
        CST = sb("CST", [128, NCST], F32)
        PAR = sb("PAR", [128, NPAR], F32)
        PWB = sb("PWB", [128, 2, 2, 128], BF16)
        IDB = sb("IDB", [128, 128], BF16)
        ONEB = sb("ONEB", [128, 128], BF16)
        OBB = sb("OBB", [128, 128], BF16)
        EPSC = sb("EPSC", [128, 1], F32)
        LOW = sb("LOW", [128, 2, 2], F32)
        OML = sb("OML", [128, 2, 2], F32)
        LBT = sb("LBT", [128, 8], F32)
        CARA = sb("CARA", [128, 2, 2], F32)
        CARB = sb("CARB", [128, 2, 15], F32)
        CARD = sb("CARD", [128, 2, 30], BF16)
        SS = sb("SS", [128, 2, 64], F32)
        SBD = sb("SBD", [128, 2, 128], BF16)
        WR = sb("WR", [128, NSLOT, 1024], BF16)
        STGA = sb("STGA", [128, 2, 16, 2], F32)
        STGB = sb("STGB", [128, 2, 16, 15], F32)
        STGD = sb("STGD", [128, 2, 16, 30], F32)
        OPA = sb("OPA", [128, 2, 2], F32)
        OPB = sb("OPB", [128, 2, 15], F32)
        OPD = sb("OPD", [128, 2, 30], F32)
        PS_all = es.enter_context(nc.psum_tensor("psall", [128, 7, 512], F32))
        PS = [PS_all[:, i, :] for i in range(7)]
        PSB = es.enter_context(nc.psum_tensor("psb", [128, 1024], BF16))

        H = HM[:, 0]
        MIX = HM[:, 1]
        FF = HM[:].rearrange("p a k n -> p (a k n)").bitcast(F32).rearrange("p (k n) -> p k n", k=8)

        S = Sched(nc)
        S_holder[0] = S
        st = {"ps": 0, "w": 0, "held": set()}

        def mm(out, lhsT, rhs, start, stop, r, w):
            S.op("pe", lambda: nc.tensor.matmul(out, lhsT=lhsT, rhs=rhs, start=start, stop=stop), r, w)

        def tr(out, in_, ident, r, w):
            S.op("pe", lambda: nc.tensor.transpose(out, in_, ident), r, w)

        def act(out, in_, func, r, w, scale=None, bias=None):
            kw = {}
            if scale is not None:
                kw["scale"] = scale
            if bias is not None:
                kw["bias"] = bias
            S.op("act", lambda: nc.scalar.activation(out=out, in_=in_, func=func, **kw), r, w)

        def tt(out, in0, in1, op, r, w):
            S.op("dve", lambda: nc.vector.tensor_tensor(out=out, in0=in0, in1=in1, op=op), r, w)

        def ts(out, in0, s1, s2, op0, op1, r, w):
            if op1 is None:
                S.op("dve", lambda: nc.vector.tensor_scalar(out=out, in0=in0, scalar1=s1, scalar2=None, op0=op0), r, w)
            else:
                S.op("dve", lambda: nc.vector.tensor_scalar(out=out, in0=in0, scalar1=s1, scalar2=s2, op0=op0, op1=op1), r, w)

        def stt(out, in0, scalar, in1, op0, op1, r, w):
            S.op("dve", lambda: nc.vector.scalar_tensor_tensor(out=out, in0=in0, scalar=scalar, in1=in1, op0=op0, op1=op1), r, w)

        def vcp(out, in_, r, w):
            S.op("dve", lambda: nc.vector.tensor_copy(out=out, in_=in_), r, w)

        def vms(ap, val, w):
            S.op("dve", lambda: nc.vector.memset(ap, val), (), w)

        def vrec(out, in_, r, w):
            S.op("dve", lambda: nc.vector.reciprocal(out=out, in_=in_), r, w)

        def dma_in(out, in_, w, chan, defer=0):
            S.dma("sp", lambda: nc.sync.dma_start(out=out, in_=in_), (), w, chan=chan, defer=defer)

        def dma_out(out, in_, r, chan, defer=0):
            S.dma("sp", lambda: nc.sync.dma_start(out=out, in_=in_), r, (), chan="o_" + chan, defer=defer)

        def newps():
            while True:
                i = st["ps"] % 7
                st["ps"] += 1
                if i not in st["held"]:
                    return PS[i], "ps%d" % i

        def wget(src, ncols=1024):
            s = st["w"] % NSLOT
            st["w"] += 1
            name = "wr%d" % s
            dst = WR[:, s, 0:ncols]
            S.dma("pool", lambda: nc.gpsimd.dma_start(out=dst, in_=src), (), [name], chan=name)
            return WR[:, s, :], name

        def carve(off, shape, dt):
            n = 1
            for s_ in shape:
                n *= s_
            nb = n * (4 if dt == F32 else 2)
            assert off % 4 == 0
            n32 = (nb + 3) // 4
            assert off // 4 + n32 <= BIG32, (off, shape)
            ap = BIG[:, off // 4: off // 4 + n32]
            if dt == BF16:
                ap = ap.bitcast(BF16)[:, 0:n]
            if len(shape) == 2:
                ap = ap.rearrange("p (a b) -> p a b", a=shape[0])
            elif len(shape) == 3:
                ap = ap.rearrange("p (a b c) -> p a b c", a=shape[0], b=shape[1])
            elif len(shape) == 4:
                ap = ap.rearrange("p (a b c d) -> p a b c d", a=shape[0], b=shape[1], c=shape[2])
            return ap, off + 4 * n32

        def pc(col):
            return PAR[:, col:col + 1]

        dma_in(CST[:], cst_d, ["CST"], "c0")
        dma_in(PAR[:], par_d, ["PAR"], "c1")
        dma_in(TMPB[:], pwbd_d.rearrange("p a b c -> p (a b c)"), ["TMPB"], "c2")
        for k in range(8):
            dma_in(X[:, k, 0:512], xT[:, k, 0:512], ["X.%d.0" % k], "x%d" % k)
        for k in range(8):
            dma_in(X[:, k, 512:1024], xT[:, k, 512:1024], ["X.%d.1" % k], "xc%d" % k)
        for k in range(8):
            dma_in(X[:, k, 1024:NT], xT[:, k, 1024:NT], ["X.%d.%d" % (k, t) for t in (2, 3, 4)], "xb%d" % k)
        for c in range(2):
            dma_in(STGA[:, c], sconv[:, 0, c], ["STGA.%d" % c], "sa%d" % c, defer=-1)
            dma_in(STGB[:, c], spool[:, 0, c], ["STGB.%d" % c], "sb%d" % c, defer=-1)
            dma_in(STGD[:, c], sconf[:, 0, c], ["STGD.%d" % c], "sd%d" % c, defer=-1)
        vcp(PWB[:].rearrange("p a b c -> p (a b c)"), TMPB[:], ["TMPB"], ["PWB"])
        vcp(IDB[:], CST[:, C_ID:C_ID + 128], ["CST"], ["IDB"])
        vcp(OBB[:], CST[:, C_OB:C_OB + 128], ["CST"], ["OBB"])
        vms(ONEB[:], 1.0, ["ONEB"])
        vms(EPSC[:], EPS, ["EPSC"])
        act(LBT[:, 0:4], PAR[:, P_LB:P_LB + 4], AF.Exp, ["PAR"], ["LBT"])
        tt(LBT[:, 4:6], LBT[:, 0:2], LBT[:, 2:4], ALU.add, ["LBT"], ["LBT"])
        vrec(LBT[:, 6:8], LBT[:, 4:6], ["LBT"], ["LBT"])
        vms(LOW[:, 0, :], 0.0, ["LOW"])
        tt(LOW[:, 1, :], LBT[:, 2:4], LBT[:, 6:8], ALU.mult, ["LBT", "LOW"], ["LOW"])
        ts(OML[:].rearrange("p a b -> p (a b)"), LOW[:].rearrange("p a b -> p (a b)"), -1.0, 1.0, ALU.mult, ALU.add,
           ["LOW"], ["OML"])

        blocks = [
            [(0, 0, 0, 512), (1, 512, 512, 512)],
            [(2, 1024, 0, 512), (3, 1536, 512, 512), (4, 2048, 1024, 128)],
        ]

        def xb(k, gt):
            return "X.%d.%d" % (k, gt)

        def rms_stats(srcs, srcbufs, lc0, w, tag, presq=False):
            for k in range(8):
                if not presq:
                    act(SQ[:, k, 0:w], srcs[k], AF.Square, [srcbufs[k]], ["SQ.%d" % k])
            ps, psn = newps()
            for k in range(8):
                mm(ps[:, 0:w], ONEB[:], SQ[:, k, 0:w], k == 0, k == 7, ["ONEB", "SQ.%d" % k], [psn])
            act(TMPA[:, 0:w], ps[:, 0:w], AF.Ln, [psn, "EPSC"], ["TMPA"], scale=1.0 / 1024.0, bias=EPSC[:, 0:1])
            act(RSTD[:, lc0:lc0 + w], TMPA[:, 0:w], AF.Exp, ["TMPA"], ["RSTD.%d" % (lc0 // 512)], scale=-0.5)

        def make_h(tiles, gcol):
            for (gt, c0, lc0, w) in tiles:
                rms_stats([X[:, k, c0:c0 + w] for k in range(8)], [xb(k, gt) for k in range(8)], lc0, w, str(gt))
                for k in range(8):
                    stt(H[:, k, lc0:lc0 + w], X[:, k, c0:c0 + w], pc(gcol + k), RSTD[:, lc0:lc0 + w], ALU.mult, ALU.mult,
                        [xb(k, gt), "RSTD.%d" % (lc0 // 512), "PAR"], ["H.%d.%d" % (k, gt), "HMA.%d" % (k // 2)])

        def add_residual(src, srcname, tiles, gcol):
            for (gt, c0, lc0, w) in tiles:
                rms_stats([src[:, k, lc0:lc0 + w] for k in range(8)], ["%s.%d.%d" % (srcname, k, gt) for k in range(8)],
                          lc0, w, str(gt))
                for k in range(8):
                    stt(TMPB[:, 0:w], src[:, k, lc0:lc0 + w], pc(gcol + k), RSTD[:, lc0:lc0 + w], ALU.mult, ALU.mult,
                        ["%s.%d.%d" % (srcname, k, gt), "RSTD.%d" % (lc0 // 512), "PAR"] + (["HMA.%d" % k] if srcname == "FF" else []),
                        ["TMPB"])
                    tt(X[:, k, c0:c0 + w], X[:, k, c0:c0 + w], TMPB[:, 0:w], ALU.add, [xb(k, gt), "TMPB"], [xb(k, gt)])

        def proj_k(unit, uname, tiles_, evac):
            banks = [newps() for _ in tiles_]
            for k in range(8):
                for (gt, c0, lc0, w), (ps, psn) in zip(tiles_, banks):
                    mm(ps[:, 0:w], unit[:, k * 128:(k + 1) * 128], H[:, k, lc0:lc0 + w], k == 0, k == 7,
                       [uname, "H.%d.%d" % (k, gt)], [psn])
            for (gt, c0, lc0, w), (ps, psn) in zip(tiles_, banks):
                evac(gt, c0, lc0, w, ps, psn)

        def proj(unit, uname, tiles, evac):
            for (gt, c0, lc0, w) in tiles:
                ps, psn = newps()
                for k in range(8):
                    mm(ps[:, 0:w], unit[:, k * 128:(k + 1) * 128], H[:, k, lc0:lc0 + w], k == 0, k == 7,
                       [uname, "H.%d.%d" % (k, gt)], [psn])
                evac(gt, c0, lc0, w, ps, psn)

        def body():
          for l in range(2):
            for b in range(2):
                tiles = blocks[b]
                TB = 1152 if b == 1 else 1024
                ptiles = [t for t in tiles if t[0] != 4]
                has_s = (b == 1)
                phase()
                make_h(tiles, P_G1 + l * 8)

                def win(u):
                    return wget(w_in_u[l, u])

                phase()
                S.barrier()
                off = 0
                UAp, off = carve(off, [2, 1026], F32)
                UAs, off = carve(off, [2, 16, 10], F32)
                ACC, off = carve(off, [2, 1152], F32)
                ACt, off = carve(off, [1152], F32)
                PBp, off = carve(off, [2, 1040], F32)
                PBs, off = carve(off, [2, 16, 23], F32)
                T1, off = carve(off, [1040], F32)
                T2, off = carve(off, [1040], F32)
                T1s, off = carve(off, [16, 23], F32)
                T2s, off = carve(off, [16, 23], F32)
                T16, off = carve(off, [16], F32)
                PL, off = carve(off, [2, 1152], BF16)

                def A1(c):
                    if b == 0:
                        vms(UAp[:, c, 0:2], 0.0, ["UAp.%d" % c])
                    else:
                        vcp(UAp[:, c, 0:2], CARA[:, c, :], ["CARA"], ["UAp.%d" % c])
                        vcp(UAs[:, c, :, 0:2], STGA[:, c], ["STGA.%d" % c], ["UAs.%d" % c])
                    u_c, n_c = win(2 + c)
                    u_u, n_u = win(4 + c)

                    def ev_c(gt, c0, lc0, w, ps, psn):
                        act(ACt[:, lc0:lc0 + w], ps[:, 0:w], AF.Copy, [psn], ["ACt.%d" % gt])

                    def ev_u(gt, c0, lc0, w, ps, psn):
                        if gt != 4:
                            tt(UAp[:, c, 2 + lc0:2 + lc0 + w], ps[:, 0:w], ACt[:, lc0:lc0 + w], ALU.mult,
                               [psn, "ACt.%d" % gt], ["UAp.%d" % c])
                        else:
                            tt(UAs[:, c, :, 2:10], ps[:, 0:128].rearrange("p (j t) -> p j t", t=8),
                               ACt[:, lc0:lc0 + 128].rearrange("p (j t) -> p j t", t=8), ALU.mult,
                               [psn, "ACt.%d" % gt], ["UAs.%d" % c])

                    proj(u_c, n_c, tiles, ev_c)
                    proj(u_u, n_u, tiles, ev_u)

                def A2(c):
                    cw = P_CW + (l * 2 + c) * 3
                    ts(ACC[:, c, 0:1024], UAp[:, c, 0:1024], pc(cw), None, ALU.mult, None, ["UAp.%d" % c, "PAR"], ["ACC.%d" % c])
                    for kk in (1, 2):
                        stt(ACC[:, c, 0:1024], UAp[:, c, kk:kk + 1024], pc(cw + kk), ACC[:, c, 0:1024], ALU.mult, ALU.add,
                            ["UAp.%d" % c, "ACC.%d" % c, "PAR"], ["ACC.%d" % c])
                    if has_s:
                        accs = ACC[:, c, 1024:1152].rearrange("p (j t) -> p j t", t=8)
                        ts(accs, UAs[:, c, :, 0:8], pc(cw), None, ALU.mult, None, ["UAs.%d" % c, "PAR"], ["ACCs.%d" % c])
                        for kk in (1, 2):
                            stt(accs, UAs[:, c, :, kk:kk + 8], pc(cw + kk), accs, ALU.mult, ALU.add,
                                ["UAs.%d" % c, "ACCs.%d" % c, "PAR"], ["ACCs.%d" % c])

                def A3(c):
                    u_b, n_b = win(0 + c)

                    def ev_b(gt, c0, lc0, w, ps, psn):
                        tt(MIX[:, c, lc0:lc0 + w], ps[:, 0:w], ACC[:, c, lc0:lc0 + w], ALU.mult,
                           [psn, "ACC.%d" % c, "ACCs.%d" % c], ["MIX.%d" % c])

                    proj(u_b, n_b, tiles, ev_b)
                    if b == 0:
                        vcp(CARA[:, c, :], UAp[:, c, 1024:1026], ["UAp.%d" % c], ["CARA"])
                    else:
                        vcp(OPA[:, c, :], UAp[:, c, 1024:1026], ["UAp.%d" % c], ["OPA.%d" % c])
                        dma_out(conv_pT[:, l, c, :], OPA[:, c, :], ["OPA.%d" % c], "cpA%d" % c, defer=-1)
                        vcp(STGA[:, c], UAs[:, c, :, 8:10], ["UAs.%d" % c], ["STGA.%d" % c])
                        dma_out(conv_sT[:, l, c], STGA[:, c], ["STGA.%d" % c], "csA%d" % c, defer=-1)
                        if l == 0:
                            dma_in(STGA[:, c], sconv[:, 1, c], ["STGA.%d" % c], "sa%d" % c, defer=-1)

                def B1(c):
                    if b == 0:
                        vms(PBp[:, c, 0:15], 0.0, ["PBp.%d" % c])
                    else:
                        vcp(PBp[:, c, 0:15], CARB[:, c, :], ["CARB"], ["PBp.%d" % c])
                        vcp(PBs[:, c, :, 0:15], STGB[:, c], ["STGB.%d" % c], ["PBs.%d" % c])
                    u_p, n_p = win(6 + c)

                    def ev_p(gt, c0, lc0, w, ps, psn):
                        if gt != 4:
                            act(PBp[:, c, 15 + lc0:15 + lc0 + w], ps[:, 0:w], AF.Copy, [psn], ["PBp.%d" % c])
                        else:
                            act(PBs[:, c, :, 15:23], ps[:, 0:128].rearrange("p (j t) -> p j t", t=8), AF.Copy,
                                [psn], ["PBs.%d" % c])

                    proj(u_p, n_p, tiles, ev_p)

                def B2(c):
                    NP_ = 1039
                    pb = "PBp.%d" % c
                    tt(T1[:, 1:NP_], PBp[:, c, 1:NP_], PBp[:, c, 0:NP_ - 1], ALU.add, [pb], ["T1"])
                    tt(T2[:, 3:NP_], T1[:, 3:NP_], T1[:, 1:NP_ - 2], ALU.add, ["T1"], ["T2"])
                    if c == 1:
                        tt(T1[:, 7:NP_], T2[:, 7:NP_], T2[:, 3:NP_ - 4], ALU.add, ["T2", "T1"], ["T1"])
                        tt(T2[:, 15:NP_], T1[:, 15:NP_], T1[:, 7:NP_ - 8], ALU.add, ["T1", "T2"], ["T2"])
                    iw = CST[:, C_IW + c:C_IW + c + 1]
                    for (lo, hi, Tw) in ((0, 64, T1), (64, 128, T2)):
                        stt(PL[lo:hi, c, 0:1024], Tw[lo:hi, 15:NP_], iw[lo:hi], PBp[lo:hi, c, 15:NP_], ALU.mult, ALU.subtract,
                            ["T1", "T2", pb, "CST"], ["PL.%d" % c])
                        if b == 0:
                            tt(T16[lo:hi, :], Tw[lo:hi, 15:31], CST[lo:hi, C_RC + 16 * c:C_RC + 16 * c + 16], ALU.mult,
                               ["T1", "T2", "CST"], ["T16"])
                            tt(PL[lo:hi, c, 0:16], T16[lo:hi, :], PBp[lo:hi, c, 15:31], ALU.subtract,
                               ["T16", pb, "PL.%d" % c], ["PL.%d" % c])
                    if has_s:
                        sbn = "PBs.%d" % c
                        tt(T1s[:, :, 1:23], PBs[:, c, :, 1:23], PBs[:, c, :, 0:22], ALU.add, [sbn], ["T1s"])
                        tt(T2s[:, :, 3:23], T1s[:, :, 3:23], T1s[:, :, 1:21], ALU.add, ["T1s"], ["T2s"])
                        if c == 1:
                            tt(T1s[:, :, 7:23], T2s[:, :, 7:23], T2s[:, :, 3:19], ALU.add, ["T2s", "T1s"], ["T1s"])
                            tt(T2s[:, :, 15:23], T1s[:, :, 15:23], T1s[:, :, 7:15], ALU.add, ["T1s", "T2s"], ["T2s"])
                        pls = PL[:, c, 1024:1152].rearrange("p (j t) -> p j t", t=8)
                        for (lo, hi, Tw) in ((0, 64, T1s), (64, 128, T2s)):
                            stt(pls[lo:hi], Tw[lo:hi, :, 15:23], iw[lo:hi], PBs[lo:hi, c, :, 15:23], ALU.mult, ALU.subtract,
                                ["T1s", "T2s", sbn, "CST"], ["PLs.%d" % c])

                def B3(c):
                    pb = "PBp.%d" % c
                    for (gt, c0, lc0, w) in tiles:
                        ps, psn = newps()
                        mm(ps[:, 0:w], PWB[:, l, c, :], PL[:, c, lc0:lc0 + w], True, True,
                           ["PWB", "PL.%d" % c, "PLs.%d" % c], [psn])
                        act(MIX[:, 2 + c, lc0:lc0 + w], ps[:, 0:w], AF.Identity, [psn, "PAR"], ["MIX.%d" % (2 + c)],
                            scale=pc(P_PS + l * 2 + c))
                    if b == 0:
                        vcp(CARB[:, c, :], PBp[:, c, 1024:1039], [pb], ["CARB"])
                    else:
                        vcp(OPB[:, c, :], PBp[:, c, 1024:1039], [pb], ["OPB.%d" % c])
                        dma_out(pool_pT[:, l, c, :], OPB[:, c, :], ["OPB.%d" % c], "cpB%d" % c, defer=-1)
                        vcp(STGB[:, c], PBs[:, c, :, 8:23], ["PBs.%d" % c], ["STGB.%d" % c])
                        dma_out(pool_sT[:, l, c], STGB[:, c], ["STGB.%d" % c], "csB%d" % c, defer=-1)
                        if l == 0:
                            dma_in(STGB[:, c], spool[:, 1, c], ["STGB.%d" % c], "sb%d" % c, defer=-1)

                A1(0); B1(0); A2(0); B2(0); A1(1); A3(0); B1(1); B3(0); A2(1); B2(1); A3(1); B3(1)
                phase()

                phase()
                S.barrier()
                off = 0
                UDp, off = carve(off, [2, 1054], BF16)
                UDT, off = carve(off, [2, 30], F32)
                UDs32, off = carve(off, [2, 16, 38], F32)
                UDs, off = carve(off, [2, 16, 38], BF16)
                DIAG, off = carve(off, [2, 31, 128], BF16)
                Z2, off = carve(off, [2, 2, 512], F32)
                ZB, off = carve(off, [4, 512], BF16)
                SD3, off = carve(off, [2, 512], F32)
                SIG = SD3[:, 0]
                D1, off = carve(off, [512], F32)
                D2, off = carve(off, [512], F32)
                for kk in range(31):
                    ts(DIAG[:, 0, kk, :], CST[:, C_ID:C_ID + 128], pc(P_DW + (l * 2 + 0) * 31 + kk), None, ALU.mult, None,
                       ["CST", "PAR"], ["DIAG.%d.%d" % (0, kk)])
                    act(DIAG[:, 1, kk, :], CST[:, C_ID:C_ID + 128], AF.Copy, ["CST", "PAR"], ["DIAG.%d.%d" % (1, kk)],
                        scale=pc(P_DW + (l * 2 + 1) * 31 + kk))
                for c in range(2):
                    if b == 0:
                        vms(UDp[:, c, 0:30], 0.0, ["UDp.%d" % c])
                    else:
                        vcp(UDp[:, c, 0:30], CARD[:, c, :], ["CARD"], ["UDp.%d" % c])
                        vcp(UDs32[:, c, :, 0:30], STGD[:, c], ["STGD.%d" % c], ["UDs32.%d" % c])
                    u_g, n_g = win(18 + c)
                    u_a, n_a = win(16 + c)
                    for tl in tiles:
                        (gt, c0, lc0, w) = tl
                        proj(u_g, n_g, [tl], lambda gt, c0, lc0, w, ps, psn: act(SIG[:, 0:w], ps[:, 0:w], AF.Sigmoid, [psn], ["SIG"]))

                        def ev_a(gt, c0, lc0, w, ps, psn, c=c):
                            if gt != 4:
                                tt(UDp[:, c, 30 + lc0:30 + lc0 + w], ps[:, 0:w], SIG[:, 0:w], ALU.mult, [psn, "SIG"], ["UDp.%d" % c])
                                if gt == 3:
                                    tt(UDT[:, c, :], ps[:, 482:512], SIG[:, 482:512], ALU.mult, [psn, "SIG"], ["UDT.%d" % c])
                            else:
                                tt(UDs32[:, c, :, 30:38], ps[:, 0:128].rearrange("p (j t) -> p j t", t=8),
                                   SIG[:, 0:128].rearrange("p (j t) -> p j t", t=8), ALU.mult, [psn, "SIG"], ["UDs32.%d" % c])

                        proj(u_a, n_a, [tl], ev_a)
                    if has_s:
                        vcp(UDs[:, c], UDs32[:, c], ["UDs32.%d" % c], ["UDs.%d" % c])
                    if b == 0:
                        vcp(CARD[:, c, :], UDp[:, c, 1024:1054], ["UDp.%d" % c], ["CARD"])
                    else:
                        vcp(OPD[:, c, :], UDT[:, c, :], ["UDT.%d" % c], ["OPD.%d" % c])
                        dma_out(conf_pT[:, l, c, :], OPD[:, c, :], ["OPD.%d" % c], "cpD%d" % c, defer=-1)
                        vcp(STGD[:, c], UDs32[:, c, :, 8:38], ["UDs32.%d" % c], ["STGD.%d" % c])
                        dma_out(conf_sT[:, l, c], STGD[:, c], ["STGD.%d" % c], "csD%d" % c, defer=-1)
                        if l == 0:
                            dma_in(STGD[:, c], sconf[:, 1, c], ["STGD.%d" % c], "sd%d" % c, defer=-1)

                def d_conv_mm(ti, c):
                    (gt, c0, lc0, w) = tiles[ti]
                    ps, psn = newps()
                    for kk in range(31):
                        if gt != 4:
                            rhs = UDp[:, c, lc0 + kk:lc0 + kk + w]
                            rb = "UDp.%d" % c
                        else:
                            rhs = UDs[:, c, :, kk:kk + 8]
                            rb = "UDs.%d" % c
                        mm(ps[:, 0:w], DIAG[:, c, kk, :], rhs, kk == 0, kk == 30, ["DIAG.%d.%d" % (c, kk), rb], [psn])
                    return ps, psn

                def d_conv_ev(ti, c, ps, psn):
                    (gt, c0, lc0, w) = tiles[ti]
                    zi = ti % 2
                    act(Z2[:, zi, c, 0:w], ps[:, 0:w], AF.Identity, [psn, "PAR"], ["Z.%d.%d" % (c, zi)],
                        bias=pc(P_CB + l * 2 + c))

                def d_ln_act(ti):
                    (gt, c0, lc0, w) = tiles[ti]
                    zi = ti % 2
                    Zt = Z2[:, zi, :, 0:w]
                    zn = ["Z.0.%d" % zi, "Z.1.%d" % zi]
                    act(ZB[:, 0:2, 0:w], Zt, AF.Copy, zn, ["ZB.0", "ZB.1"])
                    act(ZB[:, 2:4, 0:w], Zt, AF.Square, zn, ["ZB.2", "ZB.3"])

                def d_ln_mm(ti):
                    (gt, c0, lc0, w) = tiles[ti]
                    ps1, pn1 = newps()
                    ps2, pn2 = newps()
                    for c in range(2):
                        mm(ps1[:, 0:w], ONEB[:], ZB[:, c, 0:w], c == 0, c == 1, ["ONEB", "ZB.%d" % c], [pn1])
                    for c in range(2):
                        mm(ps2[:, 0:w], ONEB[:], ZB[:, 2 + c, 0:w], c == 0, c == 1, ["ONEB", "ZB.%d" % (2 + c)], [pn2])
                    return ps1, pn1, ps2, pn2

                def d_ln_b(ti, pss):
                    (gt, c0, lc0, w) = tiles[ti]
                    ps1, pn1, ps2, pn2 = pss
                    zi = ti % 2
                    Zt = Z2[:, zi, :, 0:w]
                    zn = ["Z.0.%d" % zi, "Z.1.%d" % zi]
                    act(D1[:, 0:w], ps1[:, 0:w], AF.Identity, [pn1], ["D1"], scale=1.0 / 256.0)
                    tt(D2[:, 0:w], D1[:, 0:w], D1[:, 0:w], ALU.mult, ["D1"], ["D2"])
                    stt(D2[:, 0:w], ps2[:, 0:w], 1.0 / 256.0, D2[:, 0:w], ALU.mult, ALU.subtract, [pn2, "D2"], ["D2"])
                    act(D2[:, 0:w], D2[:, 0:w], AF.Ln, ["D2", "EPSC"], ["D2"], bias=EPSC[:, 0:1])
                    act(D2[:, 0:w], D2[:, 0:w], AF.Exp, ["D2"], ["D2"], scale=-0.5)
                    tt(SD3[:, :, 0:w], Zt, D1[:, 0:w].unsqueeze(1).broadcast_to([128, 2, w]), ALU.subtract,
                       zn + ["D1"], ["SIG", "D3"])
                    tt(SD3[:, :, 0:w], SD3[:, :, 0:w], D2[:, 0:w].unsqueeze(1).broadcast_to([128, 2, w]), ALU.mult,
                       ["SIG", "D3", "D2"], ["SIG", "D3"])
                    for c in range(2):
                        act(MIX[:, 6 + c, lc0:lc0 + w], SD3[:, c, 0:w], AF.Silu, ["SIG", "D3", "PAR"], ["MIX.%d" % (6 + c)],
                            scale=pc(P_LG + l * 2 + c), bias=pc(P_LBI + l * 2 + c))

                nt_ = len(tiles)
                for c in range(2):
                    d_conv_ev(0, c, *d_conv_mm(0, c))
                for ti in range(nt_):
                    d_ln_act(ti)
                    if ti + 1 < nt_:
                        n0 = d_conv_mm(ti + 1, 0)
                    pss = d_ln_mm(ti)
                    if ti + 1 < nt_:
                        n1 = d_conv_mm(ti + 1, 1)
                    d_ln_b(ti, pss)
                    if ti + 1 < nt_:
                        d_conv_ev(ti + 1, 0, *n0)
                        d_conv_ev(ti + 1, 1, *n1)

                phase()
                S.barrier()
                off = 0
                QT, off = carve(off, [2, 1152], BF16)
                KTt, off = carve(off, [2, 1152], BF16)
                O, off = carve(off, [2, 1152], F32)
                EBEp, off = carve(off, [2, 16], F32)
                EBEs, off = carve(off, [2, 16], F32)
                STt2, off = carve(off, [2, 8, 2, 64], F32)
                off_dead = off
                VP, off = carve(off, [2, 2, 2, 128], BF16)
                KP, off = carve(off, [2, 2, 2, 128], BF16)
                AT, off = carve(off, [2, 4, 64], BF16)
                S0BD, off = carve(off, [8, 2, 128], BF16)
                VBLK, off = carve(off, [2, 8, 64], BF16)
                off3 = off
                TF2, off = carve(off, [2, 1152], F32)
                TE_a, off = carve(off, [512], F32)
                uq = [win(8 + p) for p in range(2)]
                uf = [win(10 + p) for p in range(2)]
                for p in range(2):
                    proj(uq[p][0], uq[p][1], tiles, lambda gt, c0, lc0, w, ps, psn: act(O[:, p, lc0:lc0 + w], ps[:, 0:w], AF.Silu, [psn],
                                                                                      ["OQ.%d.%d" % (p, gt)]))
                for p in range(2):
                    proj(uf[p][0], uf[p][1], tiles, lambda gt, c0, lc0, w, ps, psn: act(TF2[:, p, lc0:lc0 + w], ps[:, 0:w], AF.Sigmoid, [psn],
                                                                                      ["TF.%d.%d" % (p, hx) for hx in range(lc0 // 256, (lc0 + w + 255) // 256)]))
                items = []
                for (gt_, c0_, lc0_, w_) in tiles:
                    halves = [(lc0_, 256), (lc0_ + 256, 256)] if w_ == 512 else [(lc0_, w_)]
                    for (hl, hw) in halves:
                        for p_ in range(2):
                            items.append(((gt_, c0_ + (hl - lc0_), hl, hw), p_))

                def p1_stage1(it):
                    (gt, c0, lc0, w), p = items[it]
                    TLB = TMPB if it % 2 == 0 else TMPC
                    tln, tfn = ("TMPB" if it % 2 == 0 else "TMPC"), "TF.%d.%d" % (p, lc0 // 256)
                    TFt = TF2[:, p, lc0:lc0 + w]
                    ts(TFt, TFt, OML[:, l, p:p + 1], LOW[:, l, p:p + 1], ALU.mult, ALU.add, [tfn, "OML", "LOW"], [tfn])
                    ts(TLB[:, 0:w], TFt, F_MIN, None, ALU.max, None, [tfn], [tln])
                    act(TLB[:, 0:w], TLB[:, 0:w], AF.Ln, [tln], [tln])

                def p1_stage2(it):
                    (gt, c0, lc0, w), p = items[it]
                    TLB = TMPB if it % 2 == 0 else TMPC
                    TE = TE_a if it % 2 == 0 else TMPA
                    tln, ten, tfn = ("TMPB" if it % 2 == 0 else "TMPC"), ("TE_a" if it % 2 == 0 else "TMPA"), "TF.%d.%d" % (p, lc0 // 256)
                    TFt = TF2[:, p, lc0:lc0 + w]
                    m0 = CST[:, C_M0P:C_M0P + w] if gt != 4 else CST[:, C_M0S:C_M0S + 128]
                    S.op("dve", lambda: nc.vector.tensor_tensor_scan(
                        out=TLB[:, 0:w], data0=m0, data1=TLB[:, 0:w], initial=0.0, op0=ALU.mult, op1=ALU.add),
                         [tln, "CST"], [tln])
                    act(TE[:, 0:w], TLB[:, 0:w], AF.Exp, [tln], [ten])
                    tt(QT[:, p, lc0:lc0 + w], O[:, p, lc0:lc0 + w], TE[:, 0:w], ALU.mult, ["OQ.%d.%d" % (p, gt), ten], ["QT.%d.%d" % (p, lc0 // 256)])
                    if gt != 4:
                        ch0 = lc0 // 64
                        vcp(EBEp[:, p, ch0:ch0 + w // 64], TE[:, 0:w].rearrange("p (a b) -> p a b", b=64)[:, :, 63], [ten], ["EBEp"])
                    else:
                        vcp(EBEs[:, p, :], TE[:, 0:128].rearrange("p (a b) -> p a b", b=8)[:, :, 7], [ten], ["EBEs"])
                    act(TE[:, 0:w], TLB[:, 0:w], AF.Exp, [tln, ten], [ten], scale=-1.0)
                    ts(TFt, TFt, -1.0, 1.0, ALU.mult, ALU.add, [tfn], [tfn])
                    tt(KTt[:, p, lc0:lc0 + w], TFt, TE[:, 0:w], ALU.mult, [tfn, ten], ["KTt.%d.%d" % (p, lc0 // 256)])

                p1_calls = [(p1_stage1, 0, None)]
                for it in range(1, len(items)):
                    p1_calls.append((p1_stage1, it, None))
                    p1_calls.append((p1_stage2, it - 1, it - 1))
                p1_calls.append((p1_stage2, len(items) - 1, len(items) - 1))
                p1_state = {"pos": 0, "done": -1}

                def p1_emit_one():
                    if p1_state["pos"] >= len(p1_calls):
                        return False
                    fn, arg, done = p1_calls[p1_state["pos"]]
                    p1_state["pos"] += 1
                    fn(arg)
                    if done is not None:
                        p1_state["done"] = done
                    return True

                def p1_flush_tile(hidx):
                    while p1_state["done"] < 2 * hidx + 1:
                        assert p1_emit_one()
                if st["w"] % NSLOT == NSLOT - 1:
                    st["w"] += 1
                vslot = st["w"] % NSLOT
                u_i0, n_i0 = win(12)
                u_i1, n_i1 = win(13)
                vms(VP[:].rearrange("p a b c d -> p (a b c d)"), 0.0, ["VP0", "VP1"])
                vms(KP[:].rearrange("p a b c d -> p (a b c d)"), 0.0, ["KP0", "KP1"])
                vms(S0BD[:].rearrange("p a b c -> p (a b c)"), 0.0, ["S0BD"])
                if has_s:
                    for sc_ in range(2):
                        dma_in(STt2[:, sc_], shgrn[:, l, 8 * sc_:8 * sc_ + 8, :, :], ["STt%d" % sc_], "sh%d" % sc_)
                if b == 0:
                    vms(SS[:].rearrange("p a b -> p (a b)"), 0.0, ["SS"])
                    vms(SBD[:].rearrange("p a b -> p (a b)"), 0.0, ["SBD"])
                nch = TB // 64
                st["held"] = {5, 6}

                def stageA(ch):
                        lcol = ch * 64
                        is_s = ch >= 16
                        gt = tiles[min(lcol // 512, len(tiles) - 1)][0]
                        e = (ch // 2) % 2
                        hf = ch % 2
                        r0, r1 = hf * 64, hf * 64 + 64
                        vpn, kpn, atn = "VP%d" % e, "KP%d" % e, "AT%d.%d" % (e, hf)
                        if hf == 0:
                            psv, pvn = newps()
                            for k in range(8):
                                mm(psv[:, 0:256], H[:, k, lcol:lcol + 128], WR[:, vslot:vslot + 2, k * 128:(k + 1) * 128],
                                   k == 0, k == 7, [n_i0, n_i1, "H.%d.%d" % (k, gt)], [pvn])
                            for h in range(2):
                                act(VP[:, e, :, h, h * 64:(h + 1) * 64],
                                    psv[:, 0:256].rearrange("s (p h v) -> s p h v", p=2, h=2)[:, :, h, :], AF.Copy, [pvn], [vpn])
                            for p in range(2):
                                tr(PSB[:, p * 128:(p + 1) * 128], KTt[:, p, lcol:lcol + 128], IDB[:], ["KTt.%d.%d" % (p, lcol // 256), "IDB"], ["psb"])
                            for h in range(2):
                                act(KP[:, e, :, h, h * 64:(h + 1) * 64],
                                    PSB[:, 0:256].rearrange("s (p h v) -> s p h v", p=2, h=2)[:, :, h, :], AF.Copy, ["psb"], [kpn])
                        for h in range(2):
                            for p in range(2):
                                mm(PS_all[r0:r1, 5 + h, p * 64:(p + 1) * 64], KTt[h * 64:(h + 1) * 64, p, lcol:lcol + 64],
                                   QT[h * 64:(h + 1) * 64, p, lcol:lcol + 64], True, True, ["KTt.%d.%d" % (p, lcol // 256), "QT.%d.%d" % (p, lcol // 256)], ["ps%d" % (5 + h)])
                        mk = CST[r0:r1, C_MS:C_MS + 64] if is_s else CST[r0:r1, C_MC:C_MC + 64]
                        tt(AT[r0:r1, e].rearrange("s (p h) t -> s h p t", h=2),
                           PS_all[r0:r1, 5:7, 0:128].rearrange("s h (p t) -> s h p t", p=2),
                           mk.unsqueeze(1).unsqueeze(1).broadcast_to([64, 2, 2, 64]), ALU.mult, ["ps5", "ps6", "CST"], [atn])

                def stageB(ch):
                        lcol = ch * 64
                        is_s = ch >= 16
                        gt = tiles[min(lcol // 512, len(tiles) - 1)][0]
                        e = (ch // 2) % 2
                        hf = ch % 2
                        r0, r1 = hf * 64, hf * 64 + 64
                        vpn, kpn, atn = "VP%d" % e, "KP%d" % e, "AT%d.%d" % (e, hf)
                        if not is_s:
                            pso, pon = newps()
                            for p in range(2):
                                for h in range(2):
                                    mm(pso[:, p * 64:(p + 1) * 64], VP[r0:r1, e, p, h, :], AT[r0:r1, e, p * 2 + h, :], h == 0, False,
                                       [vpn, atn], [pon])
                                mm(pso[:, p * 64:(p + 1) * 64], SBD[:, p, :], QT[:, p, lcol:lcol + 64], False, True,
                                   ["SBD", "QT.%d.%d" % (p, lcol // 256)], [pon])
                            act(O[:, :, lcol:lcol + 64], pso[:, 0:128].rearrange("p (a b) -> p a b", a=2), AF.Copy, [pon],
                                ["O.%d" % gt, "OQ.0.%d" % gt, "OQ.1.%d" % gt])
                            psu, pun = newps()
                            for p in range(2):
                                for h in range(2):
                                    mm(psu[:, p * 64:(p + 1) * 64], KP[r0:r1, e, p, h, :], VP[r0:r1, e, p, h, h * 64:(h + 1) * 64],
                                       h == 0, h == 1, [kpn, vpn], [pun])
                            tt(SS[:], psu[:, 0:128].rearrange("p (a b) -> p a b", a=2), SS[:], ALU.add, [pun, "SS"], ["SS"])
                            tt(SS[:], SS[:], EBEp[:, :, ch:ch + 1].broadcast_to([128, 2, 64]), ALU.mult, ["SS", "EBEp"], ["SS"])
                            tt(SBD[:].rearrange("q p (h v) -> q p h v", h=2), SS[:].unsqueeze(2).broadcast_to([128, 2, 2, 64]),
                               CST[:, C_OB:C_OB + 128].rearrange("q (h v) -> q h v", h=2).unsqueeze(1).broadcast_to([128, 2, 2, 64]),
                               ALU.mult, ["SS", "CST"], ["SBD"])
                            if b == 1 and ch == 15:
                                dma_out(hgrn_pT[:, l, :, :], SS[:], ["SS"], "hp", defer=-1)
                        else:
                            sc = ch - 16
                            STt = STt2[:, sc]
                            stn = "STt%d" % sc
                            for h in range(2):
                                vcp(S0BD[h * 64:(h + 1) * 64, :, :, h * 64:(h + 1) * 64], STt[h * 64:(h + 1) * 64, :, :, :],
                                    [stn], ["S0BD"])
                            for p in range(2):
                                pso, pon = newps()
                                for h in range(2):
                                    mm(pso[:, 0:64], VP[r0:r1, e, p, h, :], AT[r0:r1, e, p * 2 + h, :], h == 0, False, [vpn, atn], [pon])
                                for j in range(8):
                                    mm(pso[:, 8 * j:8 * j + 8], S0BD[:, j, p, :], QT[:, p, lcol + 8 * j:lcol + 8 * j + 8], False, j == 7,
                                       ["S0BD", "QT.%d.%d" % (p, lcol // 256)], [pon])
                                act(O[:, p, lcol:lcol + 64], pso[:, 0:64], AF.Copy, [pon], ["O.%d" % gt, "OQ.0.%d" % gt, "OQ.1.%d" % gt])
                                for h in range(2):
                                    tt(VBLK[r0:r1, h, :, :], VP[r0:r1, e, p, h, h * 64:(h + 1) * 64].unsqueeze(1).broadcast_to([64, 8, 64]),
                                       CST[r0:r1, C_MJ:C_MJ + 8].unsqueeze(2).broadcast_to([64, 8, 64]), ALU.mult, [vpn, "CST"], ["VBLK"])
                                psu, pun = newps()
                                for h in range(2):
                                    mm(psu[:, 0:512], KP[r0:r1, e, p, h, :], VBLK[r0:r1, h].rearrange("p a b -> p (a b)"), h == 0, h == 1,
                                       [kpn, "VBLK"], [pun])
                                tt(STt[:, :, p, :], psu[:, 0:512].rearrange("p (a b) -> p a b", a=8), STt[:, :, p, :], ALU.add,
                                   [pun, stn, "S0BD"], [stn])
                                tt(STt[:, :, p, :], STt[:, :, p, :], EBEs[:, p, 8 * sc:8 * sc + 8].unsqueeze(2).broadcast_to([128, 8, 64]),
                                   ALU.mult, [stn, "EBEs"], [stn])
                            dma_out(hgrn_sT[:, l, 8 * sc:8 * sc + 8, :, :], STt, [stn], "hs%d" % sc, defer=1)

                def tix(ch):
                    return ch // 4

                p1_flush_tile(0)
                stageA(0)
                for ch in range(nch):
                    if ch + 1 < nch:
                        p1_flush_tile(tix(ch + 1))
                        stageA(ch + 1)
                    stageB(ch)
                    p1_emit_one()
                while p1_emit_one():
                    pass
                st["held"] = set()
                phase()
                S.barrier()
                SG, _ = carve(0, [2, 512], F32)
                off = off_dead
                OSQ2, off = carve(off, [2, 512], BF16)
                N12, off = carve(off, [2, 512], F32)
                N2a, _ = carve(4608, [512], F32)
                assert off <= 34304
                MO, _ = carve(34304, [8, 512], F32)
                ug = [win(14), win(15)]
                phase()
                wo_units = [wget(w_out_u[l, m]) for m in range(8)]

                def c3_tile(tl):
                    (gt, c0, lc0, w) = tl
                    for p in range(2):
                        proj(ug[p][0], ug[p][1], [tl], lambda gt, c0, lc0, w, ps, psn: act(SG[:, p, 0:w], ps[:, 0:w], AF.Silu, [psn],
                                                                                         ["SG.%d" % p]))
                    pss = []
                    for p in range(2):
                        OSQ = OSQ2[:, p]
                        on = "OSQ%d" % p
                        act(OSQ[:, 0:w], O[:, p, lc0:lc0 + w], AF.Square, ["O.%d" % gt], [on])
                        ps, psn = newps()
                        mm(ps[:, 0:w], OBB[:], OSQ[:, 0:w], True, True, ["OBB", on], [psn])
                        pss.append((ps, psn))
                    for p in range(2):
                        N1 = N12[:, p]
                        n1n = "N1%d" % p
                        ps, psn = pss[p]
                        act(N1[:, 0:w], ps[:, 0:w], AF.Ln, [psn, "EPSC"], [n1n], scale=1.0 / 64.0, bias=EPSC[:, 0:1])
                        act(N1[:, 0:w], N1[:, 0:w], AF.Exp, [n1n], [n1n], scale=-0.5)
                        stt(N2a[:, 0:w], O[:, p, lc0:lc0 + w], pc(P_HN + l * 2 + p), N1[:, 0:w], ALU.mult, ALU.mult,
                            ["O.%d" % gt, n1n, "PAR"], ["N2a"])
                        tt(MIX[:, 4 + p, lc0:lc0 + w], N2a[:, 0:w], SG[:, p, 0:w], ALU.mult, ["N2a", "SG.%d" % p],
                           ["MIX.%d.%d" % (4 + p, gt)])

                def wout_tile(tl):
                    (gt, c0, lc0, w) = tl
                    korder = (0, 1, 2, 3, 6, 7, 4, 5)
                    for m in range(8):
                        u_o, n_o = wo_units[m]
                        ps, psn = newps()
                        for i_, k in enumerate(korder):
                            kn = "MIX.%d.%d" % (k, gt) if k in (4, 5) else "MIX.%d" % k
                            mm(ps[:, 0:w], u_o[:, k * 128:(k + 1) * 128], MIX[:, k, lc0:lc0 + w], i_ == 0, i_ == 7, [n_o, kn], [psn])
                        act(MO[:, m, 0:w], ps[:, 0:w], AF.Copy, [psn], ["MO.%d" % m])
                        act(SQ[:, m, 0:w], ps[:, 0:w], AF.Square, [psn], ["SQ.%d" % m])

                def resid_stats(tl):
                    (gt, c0, lc0, w) = tl
                    rms_stats([MO[:, k, 0:w] for k in range(8)], ["MO.%d" % k for k in range(8)], lc0, w, str(gt), presq=True)

                def resid_apply(tl):
                    (gt, c0, lc0, w) = tl
                    for k in range(8):
                        stt(TMPB[:, 0:w], MO[:, k, 0:w], pc(P_G2 + l * 8 + k), RSTD[:, lc0:lc0 + w], ALU.mult, ALU.mult,
                            ["MO.%d" % k, "RSTD.%d" % (lc0 // 512), "PAR"], ["TMPB"])
                        tt(X[:, k, c0:c0 + w], X[:, k, c0:c0 + w], TMPB[:, 0:w], ALU.add, [xb(k, gt), "TMPB"], [xb(k, gt)])

                nt_ = len(tiles)
                c3_tile(tiles[0])
                for i in range(nt_):
                    wout_tile(tiles[i])
                    resid_stats(tiles[i])
                    if i + 1 < nt_:
                        c3_tile(tiles[i + 1])
                    resid_apply(tiles[i])
                    if i >= 1:
                        make_h([tiles[i - 1]], P_G3 + l * 8)
                make_h([tiles[nt_ - 1]], P_G3 + l * 8)

                S.barrier()
                ACTB, _ = carve(0, [22, 1152], BF16)
                ffc = [0]

                def gate_up(j, tl, u_g, n_g, u_u, n_u):
                    (gt, c0, lc0, w) = tl
                    tmp = TMPB if (ffc[0] % 2 == 0) else TMPC
                    tn = "TMPB" if (ffc[0] % 2 == 0) else "TMPC"
                    ffc[0] += 1
                    proj(u_g, n_g, [tl], lambda gt, c0, lc0, w, ps, psn: act(tmp[:, 0:w], ps[:, 0:w], AF.Silu, [psn], [tn]))
                    proj(u_u, n_u, [tl], lambda gt, c0, lc0, w, ps, psn: tt(ACTB[:, j, lc0:lc0 + w], ps[:, 0:w], tmp[:, 0:w], ALU.mult,
                                                                             [psn, tn], ["ACTB.%d.%d" % (j, gt)]))

                J0 = 4
                first = [(wget(w_gate_u[l, j]), wget(w_up_u[l, j])) for j in range(J0)]
                for tl in tiles:
                    for j in range(J0):
                        (u_g, n_g), (u_u, n_u) = first[j]
                        gate_up(j, tl, u_g, n_g, u_u, n_u)
                for j in range(J0, 22):
                    u_g, n_g = wget(w_gate_u[l, j])
                    u_u, n_u = wget(w_up_u[l, j])
                    tmps = {}

                    def ev_g(gt, c0, lc0, w, ps, psn):
                        tmp, tn = (TMPB, "TMPB") if (ffc[0] % 2 == 0) else (TMPC, "TMPC")
                        ffc[0] += 1
                        tmps[gt] = (tmp, tn)
                        act(tmp[:, 0:w], ps[:, 0:w], AF.Silu, [psn], [tn])

                    def ev_u(gt, c0, lc0, w, ps, psn, j=j):
                        tmp, tn = tmps[gt]
                        tt(ACTB[:, j, lc0:lc0 + w], ps[:, 0:w], tmp[:, 0:w], ALU.mult, [psn, tn], ["ACTB.%d.%d" % (j, gt)])

                    for g0 in range(0, len(tiles), 2):
                        grp = tiles[g0:g0 + 2]
                        proj_k(u_g, n_g, grp, ev_g)
                        proj_k(u_u, n_u, grp, ev_u)
                phase()
                def down_pieces(m):
                    return [wget(w_down_u[l, m, :, k0 * 128:(k0 + nk) * 128], ncols=nk * 128) for (k0, nk) in ((0, 8), (8, 8), (16, 6))]

                def down_group(m, tl, pieces):
                    (gt, c0, lc0, w) = tl
                    ps, psn = newps()
                    for j in range(22):
                        pa, pn_ = pieces[j // 8]
                        jj = j % 8
                        mm(ps[:, 0:w], pa[:, jj * 128:(jj + 1) * 128], ACTB[:, j, lc0:lc0 + w], j == 0, j == 21,
                           [pn_, "ACTB.%d.%d" % (j, gt)], [psn])
                    act(FF[:, m, lc0:lc0 + w], ps[:, 0:w], AF.Copy, [psn], ["FF.%d.%d" % (m, gt)])

                for m in range(5):
                    pcs = down_pieces(m)
                    for tl in tiles:
                        down_group(m, tl, pcs)
                last = {m: down_pieces(m) for m in (5, 6, 7)}
                for m in (5, 6, 7):
                    down_group(m, tiles[0], last[m])
                for i in range(1, len(tiles)):
                    down_group(5, tiles[i], last[5])
                    add_residual(FF, "FF", [tiles[i - 1]], P_G4 + l * 8)
                    down_group(6, tiles[i], last[6])
                    down_group(7, tiles[i], last[7])
                add_residual(FF, "FF", [tiles[-1]], P_G4 + l * 8)
                if l == 1:
                    for (gt, c0, lc0, w) in tiles:
                        for k in range(8):
                            dma_out(yT[:, k, c0:c0 + w], X[:, k, c0:c0 + w], [xb(k, gt)], "y")
        try:
            body()
        except _StopBuild:
            S.barrier()
            for k in range(8):
                dma_out(yT[:, k, :], X[:, k, :], [xb(k, t) for t in range(5)], "y")
        S.emit(final_wait_chans=[c for c in S.chan_cnt if c.startswith("o_")])
    return nc


def _consts():
    c = np.zeros((128, NCST), np.float32)
    c[:, C_ID:C_ID + 128] = np.eye(128, dtype=np.float32)
    m = np.ones(512, np.float32)
    m[0::64] = 0.0
    c[:, C_M0P:C_M0P + 512] = m[None]
    m = np.ones(128, np.float32)
    m[0::8] = 0.0
    c[:, C_M0S:C_M0S + 128] = m[None]
    s = np.arange(64)[:, None]
    t = np.arange(64)[None, :]
    for r_ in (0, 64):
        c[r_:r_ + 64, C_MC:C_MC + 64] = (s <= t).astype(np.float32)
        c[r_:r_ + 64, C_MS:C_MS + 64] = ((s <= t) & (s // 8 == t // 8)).astype(np.float32)
        c[r_:r_ + 64, C_MJ:C_MJ + 8] = (np.arange(64)[:, None] // 8 == np.arange(8)[None, :]).astype(np.float32)
    pp = np.arange(128)
    c[:, C_OB:C_OB + 128] = (pp[:, None] // 64 == pp[None, :] // 64).astype(np.float32)
    wins = (2, 4, 8, 16)
    for ch in range(2):
        for half in range(2):
            w = wins[ch * 2 + half]
            cnt = np.minimum(np.arange(16) + 1, w).astype(np.float32)
            c[half * 64:(half + 1) * 64, C_RC + 16 * ch:C_RC + 16 * ch + 16] = (1.0 / cnt)[None]
            c[half * 64:(half + 1) * 64, C_IW + ch] = 1.0 / w
    return c


def _fm(v):
    v = np.asarray(v, np.float32)
    lead = v.shape[:-1]
    n = v.shape[-1] // 128
    return np.moveaxis(v.reshape(lead + (n, 128)), -1, 0)


_PROG = {}


def kernel(x_prompt, x_sample, state_conv, state_pool, state_hgrn, state_conf,
           norm_mix_pre, norm_mix_post, w_in, conv_w, pool_w, pool_scale, hgrn_lb, hgrn_norm,
           conf_dw, conf_b, conf_ln_g, conf_ln_b, w_out, norm_ffn_pre, norm_ffn_post,
           w_gate, w_up, w_down):
    f = lambda a: np.ascontiguousarray(np.asarray(a, dtype=np.float32))
    x_prompt, x_sample = f(x_prompt), f(x_sample)
    par = np.zeros((128, NPAR), np.float32)
    for col, arr in ((P_G1, norm_mix_pre), (P_G2, norm_mix_post), (P_G3, norm_ffn_pre), (P_G4, norm_ffn_post)):
        par[:, col:col + 16] = _fm(arr).reshape(128, 16)
    par[:, P_CW:P_CW + 12] = np.transpose(_fm(conv_w), (0, 1, 3, 2)).reshape(128, 12)
    for col, arr in ((P_PS, pool_scale), (P_LB, hgrn_lb), (P_HN, hgrn_norm), (P_CB, conf_b), (P_LG, conf_ln_g), (P_LBI, conf_ln_b)):
        par[:, col:col + 4] = _fm(arr).reshape(128, 4)
    par[:, P_DW:P_DW + 124] = np.transpose(_fm(conf_dw), (0, 1, 3, 2)).reshape(128, 124)
    cst = _consts()
    pw = f(pool_w)
    pwbd = np.zeros((128, 2, 2, 128), np.float32)
    for l in range(2):
        for g in range(4):
            c, hh = g // 2, g % 2
            pwbd[hh * 64:(hh + 1) * 64, l, c, hh * 64:(hh + 1) * 64] = pw[l, g]

    def units(w, nu):
        w = f(w)
        L, KK, NN = w.shape
        K = KK // 128
        return np.ascontiguousarray(w.reshape(L, K, 128, nu, 128).transpose(0, 3, 2, 1, 4).reshape(L, nu, 128, K * 128))

    w_in_u = units(w_in, 20)
    w_out_u = units(w_out, 8)
    w_gate_u = units(w_gate, 22)
    w_up_u = units(w_up, 22)
    w_down_u = units(w_down, 8)
    sc_, sp_, sh_, sf_ = f(state_conv), f(state_pool), f(state_hgrn), f(state_conf)

    in_maps = []
    for b in range(8):
        tok = np.concatenate([x_prompt[b], x_sample[16 * b:16 * b + 16].reshape(128, 1024)], axis=0)
        xT = np.ascontiguousarray(tok.reshape(NT, 8, 128).transpose(2, 1, 0))

        def st(a, R):
            return np.ascontiguousarray(a[16 * b:16 * b + 16].reshape(16, 2, R, 2, 128).transpose(4, 1, 3, 0, 2))

        hg = sh_[16 * b:16 * b + 16].reshape(16, 2, 2, 2, 64, 64).transpose(3, 4, 1, 0, 2, 5).reshape(128, 2, 16, 2, 64)
        in_maps.append({
            "xT": xT, "sconv": st(sc_, 2), "spool": st(sp_, 15), "sconf": st(sf_, 30), "shgrn": np.ascontiguousarray(hg),
            "w_in_u": w_in_u, "w_out_u": w_out_u, "w_gate_u": w_gate_u, "w_up_u": w_up_u, "w_down_u": w_down_u,
            "par": par, "cst": cst, "pwbd": pwbd,
        })
    if "nc" not in _PROG:
        import os
        stop = os.environ.get("MK_STOP")
        _PROG["nc"] = build_program(None if stop is None else int(stop))
    res = run_bass_kernel_spmd(_PROG["nc"], in_maps, core_ids=list(range(8)))
    R = res.results

    y_prompt = np.empty((8, 2048, 1024), np.float32)
    y_sample = np.empty((128, 8, 1024), np.float32)
    conv_p = np.empty((8, 2, 2, 256), np.float32)
    pool_p = np.empty((8, 2, 15, 256), np.float32)
    hgrn_p = np.empty((8, 2, 4, 64, 64), np.float32)
    conf_p = np.empty((8, 2, 30, 256), np.float32)
    conv_s = np.empty((128, 2, 2, 256), np.float32)
    pool_s = np.empty((128, 2, 15, 256), np.float32)
    hgrn_s = np.empty((128, 2, 4, 64, 64), np.float32)
    conf_s = np.empty((128, 2, 30, 256), np.float32)
    for b in range(8):
        r = R[b]
        tok = np.asarray(r["yT"]).transpose(2, 1, 0).reshape(NT, 1024)
        y_prompt[b] = tok[:2048]
        y_sample[16 * b:16 * b + 16] = tok[2048:].reshape(16, 8, 1024)
        for dst, key in ((conv_p, "conv_pT"), (pool_p, "pool_pT"), (conf_p, "conf_pT")):
            a = np.asarray(r[key])
            dst[b] = a.transpose(1, 3, 2, 0).reshape(2, a.shape[3], 256)
        hp = np.asarray(r["hgrn_pT"]).reshape(2, 64, 2, 2, 64)
        hgrn_p[b] = hp.transpose(2, 3, 0, 1, 4).reshape(2, 4, 64, 64)
        for dst, key in ((conv_s, "conv_sT"), (pool_s, "pool_sT"), (conf_s, "conf_sT")):
            a = np.asarray(r[key])
            dst[16 * b:16 * b + 16] = a.transpose(3, 1, 4, 2, 0).reshape(16, 2, a.shape[4], 256)
        hs = np.asarray(r["hgrn_sT"]).reshape(2, 64, 2, 16, 2, 64)
        hgrn_s[16 * b:16 * b + 16] = hs.transpose(3, 2, 4, 0, 1, 5).reshape(16, 2, 4, 64, 64)
    return (y_prompt, y_sample, conv_p, pool_p, hgrn_p, conf_p, conv_s, pool_s, hgrn_s, conf_s)
```
